# Optimizing a Trainium2 kernel written in Bass

```python
import math
import jax, jax.numpy as jnp
from jax import lax
import numpy as np


D_MODEL = 4096
BATCH = 4
SEQ = 4096
DEPTH = 2
DEC_BATCH = 16
DEC_SEQ = 16
PAST_LEN = 1024

CHUNK = 64
QBLOCK = 128
D_MIX = D_MODEL
D_SSM = D_MIX // 2
SSM_HEAD_DIM = 64
N_SSM_HEADS = D_SSM // SSM_HEAD_DIM
N_SSM_GROUPS = 4
D_STATE = 128
CONV_W = 4
D_CONV = D_SSM + 2 * N_SSM_GROUPS * D_STATE
D_DIFF = D_MIX // 4
N_DIFF_HEADS = 4
DIFF_HEAD_DIM = D_DIFF // (2 * N_DIFF_HEADS)
D_MEM = D_MIX - D_SSM - D_DIFF
N_MEM_HEADS = 4
MEM_HEAD_DIM = D_MEM // N_MEM_HEADS
N_MEM = 256
NORM_EPS = 1e-6
SUBLN_EPS = 1e-5
IN_SIZES = (D_SSM, D_CONV, N_SSM_HEADS, D_DIFF, D_DIFF, D_DIFF, D_DIFF, D_MEM, D_MEM)
IN_SPLITS = tuple(int(s) for s in np.cumsum(IN_SIZES)[:-1])
D_IN = sum(IN_SIZES)

kernel_name = 'hybrid_ssd_diffattn_memory_streaming_step'


def rmsnorm(x, w, eps=NORM_EPS):
    xf = x.astype(jnp.float32)
    y = xf * lax.rsqrt(jnp.mean(xf * xf, axis=-1, keepdims=True) + eps)
    return (y * w.astype(jnp.float32)).astype(x.dtype)


def causal_conv(xp, w, b, length):
    out = b
    for j in range(CONV_W):
        out = out + xp[:, j:j + length] * w[j]
    return out


def ssd_scan(x, dt, a, bm, cm, h0):
    bsz, length, n_heads, p_dim = x.shape
    g_n, n_st = bm.shape[2], bm.shape[3]
    r_n = n_heads // g_n
    q_len = min(CHUNK, length)
    nc = length // q_len
    xd = (x.astype(jnp.float32) * dt[..., None]).reshape(bsz, nc, q_len, g_n, r_n, p_dim)
    da = (dt * a).reshape(bsz, nc, q_len, g_n, r_n).transpose(0, 3, 4, 1, 2)
    bc = bm.reshape(bsz, nc, q_len, g_n, n_st)
    cc = cm.reshape(bsz, nc, q_len, g_n, n_st)
    cum = jnp.cumsum(da, axis=-1)
    causal = jnp.tril(jnp.ones((q_len, q_len), dtype=bool))
    seg = cum[..., :, None] - cum[..., None, :]
    lmat = jnp.exp(jnp.where(causal, seg, -jnp.inf))
    cb = jnp.einsum('bclgn,bcsgn->bgcls', cc, bc)
    y_diag = jnp.einsum('bgrcls,bcsgrp->bclgrp', cb[:, :, None] * lmat, xd)
    decay_st = jnp.exp(cum[..., -1:] - cum)
    states = jnp.einsum('bclgn,bgrcl,bclgrp->cbgrpn', bc, decay_st, xd)
    chunk_decay = jnp.exp(cum[..., -1]).transpose(3, 0, 1, 2)

    def step(h, inp):
        st, dc = inp
        return h * dc[..., None, None] + st, h

    h_init = h0.astype(jnp.float32).reshape(bsz, g_n, r_n, p_dim, n_st)
    h_last, h_prev = lax.scan(step, h_init, (states, chunk_decay))
    y_off = jnp.einsum('bclgn,cbgrpn,bgrcl->bclgrp', cc, h_prev, jnp.exp(cum))
    y = (y_diag + y_off).reshape(bsz, length, n_heads, p_dim)
    return y, h_last.reshape(bsz, n_heads, p_dim, n_st)


def diff_attention(q, k, v, q_start, lam, subln_w, lambda_init):
    bsz, lq = q.shape[0], q.shape[1]
    lk = k.shape[1]
    qb = min(QBLOCK, lq)
    nb = lq // qb
    k_chunk = jnp.arange(lk) // CHUNK
    scale = DIFF_HEAD_DIM ** -0.5

    def block(args):
        qi, start = args
        q_chunk = (q_start + start + jnp.arange(qb)) // CHUNK
        s = jnp.einsum('bqhmd,bkhmd->bhmqk', qi, k).astype(jnp.float32) * scale
        mask = k_chunk[None, :] <= q_chunk[:, None]
        s = jnp.where(mask, s, -jnp.inf)
        pr = jax.nn.softmax(s, axis=-1)
        wgt = pr[:, :, 0] - lam * pr[:, :, 1]
        return jnp.einsum('bhqk,bkhe->bqhe', wgt, v)

    qs = q.reshape(bsz, nb, qb, N_DIFF_HEADS, 2, DIFF_HEAD_DIM).transpose(1, 0, 2, 3, 4, 5)
    o = lax.map(block, (qs, jnp.arange(nb) * qb))
    o = o.transpose(1, 0, 2, 3, 4).reshape(bsz, lq, N_DIFF_HEADS, 2 * DIFF_HEAD_DIM)
    o = rmsnorm(o, subln_w, SUBLN_EPS) * (1.0 - lambda_init)
    return o.astype(q.dtype)


def mem_attention(q, mk, mv):
    s = jnp.einsum('bqhd,bkhd->bhqk', q, mk).astype(jnp.float32) * (MEM_HEAD_DIM ** -0.5)
    pr = jax.nn.softmax(s, axis=-1)
    return jnp.einsum('bhqk,bkhd->bqhd', pr, mv).astype(q.dtype)


def mixer_layer(x, conv_hist, h0, k_past, v_past, mem_k, mem_v, p, lambda_init):
    bsz, length, _ = x.shape
    h = rmsnorm(x, p['norm_pre_w'])
    proj = h @ p['w_in']
    z, xbc, dt_raw, q_d, k_d, v_d, g_d, q_m, g_m = jnp.split(proj, IN_SPLITS, axis=-1)
    xp = jnp.concatenate([conv_hist.astype(xbc.dtype), xbc], axis=1)
    new_conv = xp[:, length:]
    xbc_c = jax.nn.silu(causal_conv(xp, p['conv_w'], p['conv_b'], length))
    xs, bs, cs = jnp.split(xbc_c, [D_SSM, D_SSM + N_SSM_GROUPS * D_STATE], axis=-1)
    xs = xs.reshape(bsz, length, N_SSM_HEADS, SSM_HEAD_DIM)
    bs = bs.reshape(bsz, length, N_SSM_GROUPS, D_STATE)
    cs = cs.reshape(bsz, length, N_SSM_GROUPS, D_STATE)
    dt = jax.nn.softplus(dt_raw.astype(jnp.float32) + p['dt_bias'].astype(jnp.float32))
    a = -jnp.exp(p['a_log'].astype(jnp.float32))
    y_s, h_new = ssd_scan(xs, dt, a, bs, cs, h0)
    y_s = y_s + xs.astype(jnp.float32) * p['d_skip'].astype(jnp.float32)[:, None]
    y_s = y_s.reshape(bsz, length, D_SSM) * jax.nn.silu(z.astype(jnp.float32))
    y_s = rmsnorm(y_s.reshape(bsz, length, N_SSM_GROUPS, D_SSM // N_SSM_GROUPS),
                  p['ssm_norm_w'].reshape(N_SSM_GROUPS, D_SSM // N_SSM_GROUPS))
    y_s = y_s.reshape(bsz, length, D_SSM).astype(x.dtype)
    q = q_d.reshape(bsz, length, N_DIFF_HEADS, 2, DIFF_HEAD_DIM)
    k = k_d.reshape(bsz, length, N_DIFF_HEADS, 2, DIFF_HEAD_DIM)
    v = v_d.reshape(bsz, length, N_DIFF_HEADS, 2 * DIFF_HEAD_DIM)
    if k_past is None:
        keys, vals, q_start = k, v, 0
    else:
        keys = jnp.concatenate([k_past.astype(k.dtype), k], axis=1)
        vals = jnp.concatenate([v_past.astype(v.dtype), v], axis=1)
        q_start = k_past.shape[1]
    lam = (jnp.exp(jnp.sum(p['lambda_q1'].astype(jnp.float32) * p['lambda_k1'].astype(jnp.float32)))
           - jnp.exp(jnp.sum(p['lambda_q2'].astype(jnp.float32) * p['lambda_k2'].astype(jnp.float32)))
           + lambda_init)
    y_d = diff_attention(q, keys, vals, q_start, lam, p['subln_w'], lambda_init)
    y_d = y_d.reshape(bsz, length, D_DIFF) * jax.nn.silu(g_d)
    y_m = mem_attention(q_m.reshape(bsz, length, N_MEM_HEADS, MEM_HEAD_DIM), mem_k.astype(q_m.dtype), mem_v.astype(q_m.dtype))
    y_m = y_m.reshape(bsz, length, D_MEM) * jax.nn.silu(g_m)
    out = jnp.concatenate([y_s, y_d, y_m], axis=-1) @ p['w_out']
    x = x + rmsnorm(out, p['norm_post_w'])
    return x, new_conv, h_new, k, v


def setup_inputs(seed: int = 0) -> dict:
    key = jax.random.key(seed)
    ks = jax.random.split(key, 32)
    f32 = jnp.float32

    def nrm(k, shape, s):
        return jax.random.normal(k, shape, f32) * s

    dt0 = jnp.exp(jax.random.uniform(ks[14], (DEPTH, N_SSM_HEADS), f32, math.log(1e-3), math.log(1e-1)))
    return {
        'x_prompt': nrm(ks[0], (BATCH, SEQ, D_MODEL), 1.0),
        'x_sample': nrm(ks[1], (DEC_BATCH, DEC_SEQ, D_MODEL), 1.0),
        'mem_prompt': nrm(ks[2], (BATCH, N_MEM, D_MODEL), 1.0),
        'cache_conv': nrm(ks[3], (DEPTH, DEC_BATCH, CONV_W - 1, D_CONV), 1.0),
        'state_ssm': nrm(ks[4], (DEPTH, DEC_BATCH, N_SSM_HEADS, SSM_HEAD_DIM, D_STATE), 0.1),
        'cache_k': nrm(ks[5], (DEPTH, DEC_BATCH, PAST_LEN, N_DIFF_HEADS, 2, DIFF_HEAD_DIM), 1.0),
        'cache_v': nrm(ks[6], (DEPTH, DEC_BATCH, PAST_LEN, N_DIFF_HEADS, 2 * DIFF_HEAD_DIM), 1.0),
        'cache_mem_k': nrm(ks[7], (DEPTH, DEC_BATCH, N_MEM, N_MEM_HEADS, MEM_HEAD_DIM), 1.0),
        'cache_mem_v': nrm(ks[8], (DEPTH, DEC_BATCH, N_MEM, N_MEM_HEADS, MEM_HEAD_DIM), 1.0),
        'norm_pre_w': 1.0 + nrm(ks[9], (DEPTH, D_MODEL), 0.01),
        'norm_post_w': 1.0 + nrm(ks[10], (DEPTH, D_MODEL), 0.01),
        'w_in': nrm(ks[11], (DEPTH, D_MODEL, D_IN), D_MODEL ** -0.5),
        'conv_w': nrm(ks[12], (DEPTH, CONV_W, D_CONV), CONV_W ** -0.5),
        'conv_b': nrm(ks[13], (DEPTH, D_CONV), 0.01),
        'dt_bias': dt0 + jnp.log(-jnp.expm1(-dt0)),
        'a_log': jnp.log(jax.random.uniform(ks[15], (DEPTH, N_SSM_HEADS), f32, 1.0, 16.0)),
        'd_skip': 1.0 + nrm(ks[16], (DEPTH, N_SSM_HEADS), 0.01),
        'ssm_norm_w': 1.0 + nrm(ks[17], (DEPTH, D_SSM), 0.01),
        'lambda_q1': nrm(ks[18], (DEPTH, DIFF_HEAD_DIM), 0.1),
        'lambda_k1': nrm(ks[19], (DEPTH, DIFF_HEAD_DIM), 0.1),
        'lambda_q2': nrm(ks[20], (DEPTH, DIFF_HEAD_DIM), 0.1),
        'lambda_k2': nrm(ks[21], (DEPTH, DIFF_HEAD_DIM), 0.1),
        'subln_w': 1.0 + nrm(ks[22], (DEPTH, 2 * DIFF_HEAD_DIM), 0.01),
        'mem_norm_w': 1.0 + nrm(ks[23], (DEPTH, D_MODEL), 0.01),
        'w_mem_kv': nrm(ks[24], (DEPTH, D_MODEL, 2 * D_MEM), D_MODEL ** -0.5),
        'w_out': nrm(ks[25], (DEPTH, D_MIX, D_MODEL), D_MIX ** -0.5),
    }


def reference(x_prompt, x_sample, mem_prompt, cache_conv, state_ssm, cache_k, cache_v,
              cache_mem_k, cache_mem_v, norm_pre_w, norm_post_w, w_in, conv_w, conv_b,
              dt_bias, a_log, d_skip, ssm_norm_w, lambda_q1, lambda_k1, lambda_q2, lambda_k2,
              subln_w, mem_norm_w, w_mem_kv, w_out):
    bsz_p = x_prompt.shape[0]
    xp_ = x_prompt
    xs_ = x_sample
    p_conv, p_ssm, p_k, p_v, p_mk, p_mv = [], [], [], [], [], []
    s_conv, s_ssm, s_k, s_v = [], [], [], []
    for l in range(DEPTH):
        prm = {
            'norm_pre_w': norm_pre_w[l], 'norm_post_w': norm_post_w[l], 'w_in': w_in[l],
            'conv_w': conv_w[l], 'conv_b': conv_b[l], 'dt_bias': dt_bias[l], 'a_log': a_log[l],
            'd_skip': d_skip[l], 'ssm_norm_w': ssm_norm_w[l], 'lambda_q1': lambda_q1[l],
            'lambda_k1': lambda_k1[l], 'lambda_q2': lambda_q2[l], 'lambda_k2': lambda_k2[l],
            'subln_w': subln_w[l], 'w_out': w_out[l],
        }
        lambda_init = 0.8 - 0.6 * math.exp(-0.3 * l)
        mkv = rmsnorm(mem_prompt, mem_norm_w[l]) @ w_mem_kv[l]
        mk_p, mv_p = jnp.split(mkv, 2, axis=-1)
        mk_p = mk_p.reshape(bsz_p, N_MEM, N_MEM_HEADS, MEM_HEAD_DIM)
        mv_p = mv_p.reshape(bsz_p, N_MEM, N_MEM_HEADS, MEM_HEAD_DIM)
        conv0 = jnp.zeros((bsz_p, CONV_W - 1, D_CONV), x_prompt.dtype)
        h_zero = jnp.zeros((bsz_p, N_SSM_HEADS, SSM_HEAD_DIM, D_STATE), jnp.float32)
        xp_, c_p, h_p, k_p, v_p = mixer_layer(xp_, conv0, h_zero, None, None, mk_p, mv_p, prm, lambda_init)
        xs_, c_s, h_s, k_s, v_s = mixer_layer(xs_, cache_conv[l], state_ssm[l], cache_k[l], cache_v[l],
                                             cache_mem_k[l], cache_mem_v[l], prm, lambda_init)
        p_conv.append(c_p); p_ssm.append(h_p); p_k.append(k_p); p_v.append(v_p)
        p_mk.append(mk_p); p_mv.append(mv_p)
        s_conv.append(c_s); s_ssm.append(h_s); s_k.append(k_s); s_v.append(v_s)
    return (xp_, xs_, jnp.stack(p_conv), jnp.stack(p_ssm), jnp.stack(p_k), jnp.stack(p_v),
            jnp.stack(p_mk), jnp.stack(p_mv), jnp.stack(s_conv), jnp.stack(s_ssm),
            jnp.stack(s_k), jnp.stack(s_v))
```

```python
import math
from contextlib import ExitStack
import numpy as np
import concourse.bass as bass
import concourse.mybir as mybir
from concourse.bass_utils import run_bass_kernel_spmd

F32 = mybir.dt.float32
BF16 = mybir.dt.bfloat16
AF = mybir.ActivationFunctionType
ALU = mybir.AluOpType
AX = mybir.AxisListType

D = 4096
SEQ = 4096
DIN = 11296
NCORE = 8
COMPUTE = ("pe", "act", "dve", "pool")
NSLOT = {"sp": 16, "pool": 8}
LAM_INIT = [0.8 - 0.6 * math.exp(-0.3 * l) for l in range(2)]
C_Z, C_X, C_B, C_C, C_DT, C_Q, C_K, C_V, C_G, C_QM, C_GM = 0, 2048, 4096, 4608, 5120, 5152, 6176, 7200, 8224, 9248, 10272
PO = {}
_o = 0
for _n, _w in (("wpre", 32), ("memw", 32), ("convw", 96), ("convb", 24), ("dtb", 32), ("alog", 32), ("dsk", 32),
               ("ssmw", 16), ("lq1", 128), ("lk1", 128), ("lq2", 128), ("lk2", 128), ("subln", 256)):
    PO[_n] = (_o, _o + _w)
    _o += _w
NPRM = _o


class TT:
    __slots__ = ("name", "last_w", "readers", "excl")

    def __init__(self, name):
        self.name = name
        self.last_w = None
        self.readers = {}
        self.excl = False


class Tile:
    __slots__ = ("a", "t")

    def __init__(self, a, name):
        self.a = a
        self.t = TT(name)


class Op:
    __slots__ = ("idx", "eng", "fn", "dma", "deps", "needs_inc", "cnt", "slot", "slot_val")

    def __init__(self, idx, eng, fn, dma):
        self.idx = idx
        self.eng = eng
        self.fn = fn
        self.dma = dma
        self.deps = []
        self.needs_inc = False
        self.cnt = 0
        self.slot = None
        self.slot_val = 0


def I(method, *args, **kw):
    return lambda e: getattr(e, method)(*args, **kw)


def seq(thunks):
    def fn(e):
        r = None
        for t in thunks:
            r = t(e)
        return r
    return fn


class Prog:
    def __init__(self):
        self.ops = []
        self.ndma = {"sp": 0, "pool": 0}
        self.dry = False
        self.maxops = 10 ** 9

    def add(self, eng, fn, reads=(), writes=(), dma=False):
        if self.dry or len(self.ops) >= self.maxops:
            return None
        op = Op(len(self.ops), eng, fn, dma)
        deps = {}
        for t in reads:
            t = t.t if hasattr(t, 't') else t
            if t.last_w is not None:
                deps[t.last_w.idx] = t.last_w
            if t.excl:
                for k, r in t.readers.items():
                    if k != eng:
                        deps[r.idx] = r
        for t in writes:
            t = t.t if hasattr(t, 't') else t
            if t.last_w is not None:
                deps[t.last_w.idx] = t.last_w
            for r in t.readers.values():
                deps[r.idx] = r
        op.deps = [d for d in deps.values() if d.dma or d.eng != eng or eng != "pe"]
        for d in op.deps:
            d.needs_inc = True
        for t in reads:
            t = t.t if hasattr(t, 't') else t
            t.readers[("dma", op.idx) if dma else eng] = op
        for t in writes:
            t = t.t if hasattr(t, 't') else t
            t.last_w = op
            t.readers = {}
        if dma:
            i = self.ndma[eng]
            self.ndma[eng] += 1
            op.slot = i % NSLOT[eng]
            op.slot_val = 16 * (i // NSLOT[eng] + 1)
        self.ops.append(op)
        return op

    def dma(self, q, out_ap, in_ap, reads=(), writes=(), **kw):
        return self.add(q, I("dma_start", out=out_ap, in_=in_ap, **kw), reads, writes, dma=True)

    def emit(self, nc):
        with ExitStack() as es:
            sems = {e: es.enter_context(nc.semaphore("s_" + e)) for e in COMPUTE}
            dsem = {q: [es.enter_context(nc.semaphore(f"d_{q}{i}")) for i in range(NSLOT[q])] for q in NSLOT}
            cnt = {e: 0 for e in COMPUTE}
            for op in self.ops:
                if not op.dma:
                    if op.needs_inc:
                        cnt[op.eng] += 1
                    op.cnt = cnt[op.eng]
            final_slot = {q: [0] * NSLOT[q] for q in NSLOT}
            for op in self.ops:
                if op.dma:
                    final_slot[op.eng][op.slot] = op.slot_val
            block = es.enter_context(nc.Block())
            by_eng = {e: [op for op in self.ops if op.eng == e] for e in ("pe", "act", "dve", "pool", "sp")}

            def make(ename):
                ops = by_eng[ename]

                def body(e):
                    waited = {}
                    for op in ops:
                        for d in op.deps:
                            if d.dma:
                                key = (d.eng, d.slot)
                                if waited.get(key, 0) < d.slot_val:
                                    e.wait_ge(dsem[d.eng][d.slot], d.slot_val)
                                    waited[key] = d.slot_val
                            else:
                                if waited.get(d.eng, 0) < d.cnt:
                                    e.wait_ge(sems[d.eng], d.cnt)
                                    waited[d.eng] = d.cnt
                        if op.dma:
                            if op.slot_val > 16:
                                key = (op.eng, op.slot)
                                if waited.get(key, 0) < op.slot_val - 16:
                                    e.wait_ge(dsem[op.eng][op.slot], op.slot_val - 16)
                                    waited[key] = op.slot_val - 16
                            op.fn(e).then_inc(dsem[op.eng][op.slot], 16)
                        else:
                            ins = op.fn(e)
                            if op.needs_inc:
                                ins.then_inc(sems[op.eng], 1)
                    if ename == "sp":
                        for q in NSLOT:
                            for s in range(NSLOT[q]):
                                if final_slot[q][s] > 0:
                                    e.wait_ge(dsem[q][s], final_slot[q][s])
                return body

            block.tensor(make("pe"))
            block.scalar(make("act"))
            block.vector(make("dve"))
            block.gpsimd(make("pool"))
            block.sync(make("sp"))


class Group:
    def __init__(self, name, NT, NTT, nsuper):
        self.name, self.NT, self.NTT, self.nsuper = name, NT, NTT, nsuper
        self.TS = NT * NTT
        self.sample = name == "s"


def build(cfg=None):
    cfg = cfg or {}
    STG = cfg.get('stages', ('ssd', 'attn', 'mem', 'out'))
    NSUP = cfg.get('nsuper', 16)
    LAYERS = cfg.get('layers', (0, 1))
    DO_SAMPLE = cfg.get('sample', True)
    DO_PROMPT = cfg.get('prompt', True)
    DO_MEMKV = cfg.get('memkv', True)
    CUT = cfg.get('cut', 99)
    nc = bass.Bass("TRN2", target_bir_lowering=False)
    P = Prog()
    es = ExitStack()

    def din(name, shape, dt=F32):
        return nc.dram_tensor(name, shape, dt, kind="ExternalInput").ap()

    def dout(name, shape, dt=F32):
        return nc.dram_tensor(name, shape, dt, kind="ExternalOutput").ap()

    def dscr(name, shape, dt=F32):
        return nc.dram_tensor(name, shape, dt, kind="Internal").ap()

    def sb(name, shape, dt=F32):
        return Tile(es.enter_context(nc.sbuf_tensor("sb_" + name, shape, dt)), name)

    def psb(name, shape, dt=F32):
        t = Tile(es.enter_context(nc.psum_tensor("ps_" + name, shape, dt)), name)
        t.t.excl = True
        return t

    xp_d = din("xp", [SEQ, D])
    xs_d = din("xs", [32, D])
    mem_d = din("mem", [256, D])
    cconv_d = din("cconv", [2, 128, 24, 2, 3])
    sssm_d = din("sssm", [2, 2, 128, 2048])
    ckT_d = din("ckT", [2, 2, 8, 128, 1024])
    cv_d = din("cv", [2, 2, 1024, 4, 256])
    cmkT_d = din("cmkT", [2, 2, 8, 128, 256])
    cmv_d = din("cmv", [2, 2, 256, 4, 256])
    win_d = din("w_in", [2, D, DIN])
    wout_d = din("w_out", [2, D, D])
    wkv_d = din("w_kv", [2, D, 2048])
    prm_d = din("prm", [2, 128, NPRM])
    wpost_d = din("wpost", [2, 128, D])
    cst_d = din("cst", [128, 4, 128])

    yp_d = dout("y_p", [SEQ, D])
    ys_d = dout("y_s", [32, D])
    oconvp_d = dout("o_conv_p", [2, 128, 24, 3])
    ossmp_d = dout("o_ssm_p", [2, 128, 2048])
    pk_d = dout("p_k", [2, SEQ, 1024])
    pv_d = dout("p_v", [2, SEQ, 1024])
    pmk_d = dout("p_mk", [2, 256, 1024])
    pmv_d = dout("p_mv", [2, 256, 1024])
    oconvs_d = dout("o_conv_s", [2, 128, 24, 2, 3])
    ossms_d = dout("o_ssm_s", [2, 2, 128, 2048])
    sk_d = dout("s_k", [2, 32, 1024])
    sv_d = dout("s_v", [2, 32, 1024])

    x1p_d = dscr("x1p", [SEQ, D])
    x1s_d = dscr("x1s", [32, D])
    kts_d = dscr("kts", [2, 8, 128, SEQ], BF16)
    vs_d = dscr("vs", [2, SEQ, 4, 256], BF16)
    x1p_t = [TT(f"x1p{i}") for i in range(32)]
    x1s_t = [TT(f"x1s{i}") for i in range(2)]
    kts_t = [[TT(f"kts{l}_{i}") for i in range(32)] for l in range(2)]
    vs_t = [[TT(f"vs{l}_{i}") for i in range(32)] for l in range(2)]

    cst = sb("cst", [128, 4, 128])
    identb = sb("identb", [128, 128], BF16)
    onesb = sb("onesb", [128, 128], BF16)
    prm = sb("prm", [128, NPRM])
    abc = sb("abc", [128, 32])
    neglam = sb("neglam", [128, 1])
    lamtmp = sb("lamtmp", [128, 128])
    lams = sb("lams", [128, 4])
    wpostb = sb("wpostb", [128, 1024])
    xblk = [sb(f"xblk{i}", [128, 1024]) for i in range(2)]
    hsf = [sb(f"hsf{i}", [128, D], BF16) for i in range(2)]
    junkb = sb("junkb", [128, 1024], BF16)
    ssq = sb("ssq", [128, 8])
    rstd = sb("rstd", [128, 2])
    hT = sb("hT", [128, 32, 256], BF16)
    yT = sb("yT", [128, 32, 256], BF16)

    class View:
        def __init__(self, a, t):
            self.a, self.t = a, t
    outb = [View(hT.a[:, 16 * i:16 * (i + 1), :].rearrange("p a b -> p (a b)"), hT.t) for i in range(2)]
    wbuf = [sb(f"wbuf{i}", [128, 32, 256], BF16) for i in range(2)]
    carry = sb("carry", [128, 24, 2, 3])
    xpb = sb("xpb", [128, 4, 262])
    cacc = sb("cacc", [128, 4, 256])
    BT = sb("BT", [128, 4, 256], BF16)
    CT = sb("CT", [128, 4, 256], BF16)
    Btm = [sb(f"Btm{i}", [128, 512], BF16) for i in range(2)]
    xsT = sb("xsT", [128, 4, 256], BF16)
    dtt = [sb(f"dtt{i}", [128, 32]) for i in range(2)]
    dat = [sb(f"dat{i}", [128, 32]) for i in range(2)]
    ecum = [sb(f"ecum{i}", [128, 32]) for i in range(2)]
    wst = [sb(f"wst{i}", [128, 32]) for i in range(2)]
    cdec = [sb(f"cdec{i}", [128, 32]) for i in range(2)]
    sptmp = sb("sptmp", [128, 32])
    xtm = sb("xtm", [128, 512], BF16)
    xdt = sb("xdt", [128, 512], BF16)
    xw = sb("xw", [128, 512], BF16)
    xD = sb("xD", [128, 512])
    Rt = sb("Rt", [128, 8, 128])
    Et = sb("Et", [128, 8, 128], BF16)
    MT = sb("MT", [128, 8, 128], BF16)
    cbm = sb("cbm", [128, 128], BF16)
    t1 = sb("t1", [128, 512])
    t3 = sb("t3", [128, 512])
    ytm = sb("ytm", [128, 512], BF16)
    Hs = [sb(f"H{g}", [128, 512]) for g in range(4)]
    Hb = [sb(f"Hb{g}", [128, 512], BF16) for g in range(4)]
    nrm = sb("nrm", [128, 256])
    sqt = sb("sqt", [128, 256], BF16)
    QT = sb("QT", [128, 8, 256], BF16)
    QmT = sb("QmT", [128, 8, 256], BF16)
    kf = [sb(f"kf{i}", [128, 512]) for i in range(2)]
    kbf = sb("kbf", [128, 512], BF16)
    KTn = sb("KTn", [128, 8, 256], BF16)
    VAn = [sb(f"VAn{i}", [128, 4, 257], BF16) for i in range(2)]
    KTb = [[sb(f"KTb{i}{m}", [128, 1024], BF16) for m in range(2)] for i in range(2)]
    VA = [sb(f"VA{i}", [128, 8, 257], BF16) for i in range(2)]
    PT = [sb(f"PT{i}", [128, 128], BF16) for i in range(4)]
    Osb = [sb(f"Osb{i}", [128, 257]) for i in range(2)]
    rcp = sb("rcp", [128, 4])
    od = sb("od", [128, 256])
    od2 = sb("od2", [128, 256])
    ydtm = sb("ydtm", [128, 256], BF16)
    MKT = sb("MKT", [128, 8, 256], BF16)
    MVA = [sb(f"MVA{j}", [128, 4, 257], BF16) for j in range(2)]
    acc = [psb(f"acc{i}", [128, 512]) for i in range(4)]
    tmp = [psb(f"tmp{i}", [128, 512]) for i in range(2)]
    trb = [psb(f"trb{i}", [128, 1024], BF16) for i in range(2)]
    rr = {"tmp": 0, "trb": 0, "acc": 0, "pt": 0, "kf": 0}

    def nxt(kind, n):
        rr[kind] = (rr[kind] + 1) % n
        return rr[kind]

    ident = cst.a[:, 0, :]
    MLE = cst.a[:, 1, :]
    MGT = cst.a[:, 2, :]
    ones = cst.a[:, 3, :]

    def pc(name):
        a, b = PO[name]
        return prm.a[:, a:b]

    P.dma("sp", cst.a[:], cst_d, writes=[cst])
    P.add("dve", I("tensor_copy", identb.a[:], ident), [cst], [identb])
    P.add("dve", I("tensor_copy", onesb.a[:], ones), [cst], [onesb])
    for t in VAn + VA + MVA:
        P.add("pool", I("memset", t.a[:, :, 256:257], 1.0), [], [t])

    NWSCR = 160
    wscr_parts = [dscr(f"wscr{i}", [40, 128, 32, 256], BF16) for i in range(4)]
    WVN = {}

    def wscr_ap(idx):
        return wscr_parts[idx // 40][idx % 40]
    wscr_t = [TT(f"wscr{i}") for i in range(NWSCR)]

    class WStream:
        def __init__(self):
            self.descs = []
            self.issued = 0
            self.used = 0
            self.slot = {}

        def _issue(self, i):
            key, src, ncols = self.descs[i]
            b = wbuf[i % 2]
            if key not in self.slot:
                idx = len(self.slot)
                assert idx < NWSCR
                self.slot[key] = idx
                P.dma("pool", b.a[:, :, 0:ncols], src, writes=[b])
                P.dma("sp", wscr_ap(idx)[:, :, 0:ncols], b.a[:, :, 0:ncols], reads=[b], writes=[wscr_t[idx]])
            else:
                idx = self.slot[key]
                P.dma("pool", b.a[:, :, 0:ncols], wscr_ap(idx)[:, :, 0:ncols], reads=[wscr_t[idx]], writes=[b])

        def next(self, key, src, ncols):
            if P.dry:
                self.descs.append((key, src, ncols))
                return wbuf[0]
            i = self.used
            while self.issued < min(i + 2, len(self.descs)):
                self._issue(self.issued)
                self.issued += 1
            self.used += 1
            return wbuf[i % 2]

    def prenorm_A(G, rows_ap, rows_t, ntiles):
        NT = G.NT
        for tt in range(ntiles):
            src = rows_ap(tt)
            rt = rows_t(tt)
            for cb in range(4):
                xb = xblk[cb % 2]
                P.dma("sp", xb.a[:NT, :], src[:, cb * 1024:(cb + 1) * 1024], reads=rt, writes=[xb])
                P.add("act", I("activation", out=junkb.a[:NT, :], in_=xb.a[:NT, :], func=AF.Square, accum_out=ssq.a[:NT, cb:cb + 1]),
                      [xb], [junkb, ssq])
            P.add("dve", I("reduce_sum", out=rstd.a[:NT, 0:1], in_=ssq.a[:NT, 0:4], axis=AX.X), [ssq], [rstd])
            P.add("dve", I("tensor_scalar", out=rstd.a[:NT, 0:1], in0=rstd.a[:NT, 0:1], scalar1=1.0 / D, scalar2=1e-6, op0=ALU.mult, op1=ALU.add), [rstd], [rstd])
            P.add("act", I("activation", out=rstd.a[:NT, 0:1], in_=rstd.a[:NT, 0:1], func=AF.Sqrt), [rstd], [rstd])
            P.add("dve", I("reciprocal", out=rstd.a[:NT, 0:1], in_=rstd.a[:NT, 0:1]), [rstd], [rstd])
            for cb in range(4):
                xb = xblk[cb % 2]
                P.dma("sp", xb.a[:NT, :], src[:, cb * 1024:(cb + 1) * 1024], reads=rt, writes=[xb])
                P.add("dve", I("tensor_scalar_mul", out=hsf[tt].a[:NT, cb * 1024:(cb + 1) * 1024], in0=xb.a[:NT, :], scalar1=rstd.a[:NT, 0:1]), [xb, rstd], [hsf[tt]])

    def prenorm_B(G, wcol, ntiles):
        NT = G.NT
        for tt in range(ntiles):
            for cb in range(4):
                tb = trb[nxt("trb", 2)]
                P.add("pe", seq([I("transpose", out=tb.a[:, j * 128:j * 128 + NT], in_=hsf[tt].a[:NT, cb * 1024 + j * 128:cb * 1024 + (j + 1) * 128], identity=identb.a[:NT, :NT]) for j in range(8)]),
                      [hsf[tt], identb], [tb])
                o = hT.a[:, cb * 8:(cb + 1) * 8, tt * NT:(tt + 1) * NT]
                i0 = tb.a[:, :].rearrange("p (j t) -> p j t", t=128)[:, :, 0:NT]
                i1 = wcol[:, cb * 8:(cb + 1) * 8].unsqueeze(2).to_broadcast([128, 8, NT])
                P.add("dve", I("tensor_tensor", out=o, in0=i0, in1=i1, op=ALU.mult), [tb, prm], [hT])

    def fm_chunk(G, ws, wv, c0, ncols=512):
        TS = G.TS
        base = 2 * nxt("acc", 2)
        pss = [(acc[base + ct // 2], (ct % 2) * 256) for ct in range(4)]
        for half in range((ncols + 255) // 256):
            nc_ = min(256, ncols - half * 256)
            wb = ws.next((WVN[id(wv)], c0 + half * 256), wv[:, :, c0 + half * 256:c0 + half * 256 + nc_], nc_)
            th = []
            for c2 in range(nc_ // 128):
                pt, off = pss[half * 2 + c2]
                for kt in range(32):
                    th.append(I("matmul", pt.a[:, off:off + TS], wb.a[:, kt, c2 * 128:(c2 + 1) * 128], hT.a[:, kt, 0:TS], start=(kt == 0), stop=(kt == 31)))
            P.add("pe", seq(th), [wb, hT], [pss[half * 2][0]])
        return pss

    def tm_chunk(G, ws, wv, c0, ncols=512, src=None):
        NT = G.NT
        sT = hT if src is None else src
        base = 2 * nxt("acc", 2)
        pss = [acc[base], acc[base + 1]]
        for half in range((ncols + 255) // 256):
            nc_ = min(256, ncols - half * 256)
            wb = ws.next((WVN[id(wv)], c0 + half * 256), wv[:, :, c0 + half * 256:c0 + half * 256 + nc_], nc_)
            th = []
            for tt in range(G.NTT):
                for kt in range(32):
                    th.append(I("matmul", pss[tt].a[:NT, half * 256:half * 256 + nc_], sT.a[:, kt, tt * NT:(tt + 1) * NT], wb.a[:, kt, 0:nc_], start=(kt == 0), stop=(kt == 31)))
            P.add("pe", seq(th), [wb, sT], pss[:G.NTT])
        return pss

    def run_layer(l, G, xsrc_d, xsrc_t, xdst_d, xdst_t, ws):
        NT, NTT, TS = G.NT, G.NTT, G.TS
        wv_in = win_d[l].rearrange("(kt p) c -> p kt c", p=128)
        wv_out = wout_d[l].rearrange("(kt p) c -> p kt c", p=128)
        WVN[id(wv_in)] = f"in{l}"
        WVN[id(wv_out)] = f"out{l}"
        if not G.sample:
            P.add("pool", I("memset", carry.a[:], 0.0), [], [carry])
            for g in range(4):
                P.add("pool", I("memset", Hs[g].a[:], 0.0), [], [Hs[g]])
                P.add("pool", I("memset", Hb[g].a[:], 0.0), [], [Hb[g]])
        else:
            P.dma("sp", carry.a[:], cconv_d[l], writes=[carry])

        def conv_chunk(pss, ct0, dst):
            nseg = 2 if G.sample else 1
            sl = TS // nseg
            xv = xpb.a[:, :, 0:nseg * (3 + sl)].rearrange("p c (s t) -> p c s t", s=nseg)
            P.add("dve", I("tensor_copy", xv[:, :, :, 0:3], carry.a[:, ct0:ct0 + 4, 0:nseg, :]), [carry], [xpb])
            for ct in range(4):
                pt, off = pss[ct]
                P.add("act", I("copy", out=xv[:, ct, :, 3:3 + sl], in_=pt.a[:, off:off + TS].rearrange("p (s t) -> p s t", s=nseg)), [pt], [xpb])
            P.add("dve", I("tensor_copy", carry.a[:, ct0:ct0 + 4, 0:nseg, :], xv[:, :, :, sl:sl + 3]), [xpb], [carry])
            cv = cacc.a[:, :, 0:TS].rearrange("p c (s t) -> p c s t", s=nseg)
            cw = pc("convw").rearrange("p (c j) -> p c j", j=4)
            for ct in range(4):
                gct = ct0 + ct
                P.add("dve", I("tensor_scalar", out=cv[:, ct], in0=xv[:, ct, :, 0:sl], scalar1=cw[:, gct, 0:1], scalar2=pc("convb")[:, gct:gct + 1], op0=ALU.mult, op1=ALU.add), [xpb, prm], [cacc])
                for j in range(1, 4):
                    P.add("dve", I("scalar_tensor_tensor", out=cv[:, ct], in0=xv[:, ct, :, j:j + sl], scalar=cw[:, gct, j:j + 1], in1=cv[:, ct], op0=ALU.mult, op1=ALU.add), [xpb, prm, cacc], [cacc])
            P.add("act", I("activation", out=dst.a[:, :, 0:TS], in_=cacc.a[:, :, 0:TS], func=AF.Silu), [cacc], [dst])

        for s in range(G.nsuper):
            tok0 = s * TS
            gt = [s * NTT + tt for tt in range(NTT)]
            if s == 0:
                prenorm_A(G, lambda tt: xsrc_d[tok0 + tt * NT: tok0 + (tt + 1) * NT, :], lambda tt: xsrc_t(gt[tt]), NTT)
            prenorm_B(G, pc("wpre"), NTT)
            if CUT < 1:
                continue
            pss = fm_chunk(G, ws, wv_in, C_B)
            conv_chunk(pss, 16, BT)
            pss = fm_chunk(G, ws, wv_in, C_C)
            conv_chunk(pss, 20, CT)
            pss = tm_chunk(G, ws, wv_in, C_DT, 32)
            for tt in range(NTT):
                P.add("dve", I("tensor_tensor", out=sptmp.a[:NT, :], in0=pss[tt].a[:NT, 0:32], in1=pc("dtb")[:NT, :], op=ALU.add), [pss[tt], prm], [sptmp])
                P.add("act", I("activation", out=sptmp.a[:NT, :], in_=sptmp.a[:NT, :], func=AF.Exp), [sptmp], [sptmp])
                P.add("act", I("activation", out=dtt[tt].a[:NT, :], in_=sptmp.a[:NT, :], func=AF.Ln, bias=1.0), [sptmp], [dtt[tt]])
                P.add("dve", I("tensor_tensor", out=dat[tt].a[:NT, :], in0=dtt[tt].a[:NT, :], in1=abc.a[:NT, :], op=ALU.mult), [dtt[tt], abc], [dat[tt]])
                pt = tmp[nxt("tmp", 2)]
                P.add("pe", seq([
                    I("matmul", pt.a[:NT, 0:32], MLE[:NT, :NT], dat[tt].a[:NT, :], start=True, stop=True),
                    I("matmul", pt.a[:NT, 32:64], MGT[:NT, :NT], dat[tt].a[:NT, :], start=True, stop=True),
                    I("matmul", pt.a[:, 64:96], ones[:NT, :], dat[tt].a[:NT, :], start=True, stop=True)]), [cst, dat[tt]], [pt])
                P.add("act", I("activation", out=ecum[tt].a[:NT, :], in_=pt.a[:NT, 0:32], func=AF.Exp), [pt], [ecum[tt]])
                P.add("act", I("activation", out=wst[tt].a[:NT, :], in_=pt.a[:NT, 32:64], func=AF.Exp), [pt], [wst[tt]])
                P.add("act", I("activation", out=cdec[tt].a[:, :], in_=pt.a[:, 64:96], func=AF.Exp), [pt], [cdec[tt]])
                P.add("dve", I("tensor_tensor", out=wst[tt].a[:NT, :], in0=wst[tt].a[:NT, :], in1=dtt[tt].a[:NT, :], op=ALU.mult), [wst[tt], dtt[tt]], [wst[tt]])
                tb = trb[nxt("trb", 2)]
                P.add("pe", seq([I("transpose", out=tb.a[:NT, g * 128:(g + 1) * 128], in_=BT.a[:, g, tt * NT:(tt + 1) * NT], identity=identb.a[:]) for g in range(4)]), [BT, identb], [tb])
                P.add("act", I("copy", out=Btm[tt].a[:NT, :], in_=tb.a[:NT, 0:512]), [tb], [Btm[tt]])
            if CUT < 2:
                continue
            for g in range(4):
                pss = fm_chunk(G, ws, wv_in, C_Z + g * 512)
                for ct in range(4):
                    pt, off = pss[ct]
                    P.add("act", I("activation", out=yT.a[:, 4 * g + ct, 0:TS], in_=pt.a[:, off:off + TS], func=AF.Silu), [pt], [yT])
                pss = fm_chunk(G, ws, wv_in, C_X + g * 512)
                conv_chunk(pss, 4 * g, xsT)
                for tt in range(NTT):
                    tsl = slice(tt * NT, (tt + 1) * NT)
                    if G.sample:
                        P.dma("sp", Hs[g].a[:], sssm_d[l, tt, :, g * 512:(g + 1) * 512], writes=[Hs[g]])
                        P.add("act", I("copy", out=Hb[g].a[:], in_=Hs[g].a[:]), [Hs[g]], [Hb[g]])
                    tb = trb[nxt("trb", 2)]
                    P.add("pe", seq([I("transpose", out=tb.a[:NT, c * 128:(c + 1) * 128], in_=xsT.a[:, c, tsl], identity=identb.a[:]) for c in range(4)]), [xsT, identb], [tb])
                    P.add("act", I("copy", out=xtm.a[:NT, :], in_=tb.a[:NT, 0:512]), [tb], [xtm])
                    hs8 = slice(8 * g, 8 * g + 8)
                    x3 = xtm.a[:NT, :].rearrange("p (h d) -> p h d", d=64)
                    P.add("dve", I("tensor_tensor", out=xdt.a[:NT, :].rearrange("p (h d) -> p h d", d=64), in0=x3, in1=dtt[tt].a[:NT, hs8].unsqueeze(2).to_broadcast([NT, 8, 64]), op=ALU.mult), [xtm, dtt[tt]], [xdt])
                    P.add("dve", I("tensor_tensor", out=xw.a[:NT, :].rearrange("p (h d) -> p h d", d=64), in0=x3, in1=wst[tt].a[:NT, hs8].unsqueeze(2).to_broadcast([NT, 8, 64]), op=ALU.mult), [xtm, wst[tt]], [xw])
                    P.add("dve", I("tensor_tensor", out=xD.a[:NT, :].rearrange("p (h d) -> p h d", d=64), in0=x3, in1=pc("dsk")[:NT, hs8].unsqueeze(2).to_broadcast([NT, 8, 64]), op=ALU.mult), [xtm, prm], [xD])
                    P.add("dve", I("tensor_tensor", out=Rt.a[:NT, :, :NT], in0=dat[tt].a[:NT, hs8].unsqueeze(2).to_broadcast([NT, 8, NT]), in1=MLE[:NT, :NT].unsqueeze(1).to_broadcast([NT, 8, NT]), op=ALU.mult), [dat[tt], cst], [Rt])
                    sg = [tmp[0], tmp[1]]
                    P.add("pe", seq([I("matmul", sg[hh // 4].a[:NT, (hh % 4) * 128:(hh % 4) * 128 + NT], MGT[:NT, :NT], Rt.a[:NT, hh, :NT], start=True, stop=True) for hh in range(8)]), [cst, Rt], sg)
                    for q in range(2):
                        P.add("act", I("activation", out=Et.a[:NT, 4 * q:4 * q + 4, :NT], in_=sg[q].a[:NT, :].rearrange("p (h t) -> p h t", t=128)[:, :, :NT], func=AF.Exp), [sg[q]], [Et])
                    pcb = acc[2 * rr["acc"] + 0]
                    pa = [acc[(2 * rr["acc"] + 2) % 4], acc[(2 * rr["acc"] + 3) % 4]]
                    P.add("pe", I("matmul", pcb.a[:NT, 0:NT], BT.a[:, g, tsl], CT.a[:, g, tsl], start=True, stop=True), [BT, CT], [pcb])
                    P.add("dve", I("tensor_tensor", out=cbm.a[:NT, :NT], in0=pcb.a[:NT, 0:NT], in1=MLE[:NT, :NT], op=ALU.mult), [pcb, cst], [cbm])
                    P.add("dve", I("tensor_tensor", out=MT.a[:NT, :, :NT], in0=Et.a[:NT, :, :NT], in1=cbm.a[:NT, :NT].unsqueeze(1).to_broadcast([NT, 8, NT]), op=ALU.mult), [Et, cbm], [MT])
                    P.add("pe", seq([I("matmul", pa[0].a[:NT, hh * 64:(hh + 1) * 64], MT.a[:NT, hh, :NT], xdt.a[:NT, hh * 64:(hh + 1) * 64], start=True, stop=True) for hh in range(8)]), [MT, xdt], [pa[0]])
                    P.add("pe", I("matmul", pa[1].a[:NT, :], CT.a[:, g, tsl], Hb[g].a[:], start=True, stop=True), [CT, Hb[g]], [pa[1]])
                    P.add("dve", I("tensor_tensor", out=t1.a[:NT, :].rearrange("p (h d) -> p h d", d=64), in0=pa[1].a[:NT, :].rearrange("p (h d) -> p h d", d=64), in1=ecum[tt].a[:NT, hs8].unsqueeze(2).to_broadcast([NT, 8, 64]), op=ALU.mult), [pa[1], ecum[tt]], [t1])
                    P.add("dve", I("tensor_tensor", out=t3.a[:NT, :], in0=t1.a[:NT, :], in1=xD.a[:NT, :], op=ALU.add), [t1, xD], [t3])
                    P.add("dve", I("tensor_tensor", out=ytm.a[:NT, :], in0=t3.a[:NT, :], in1=pa[0].a[:NT, :], op=ALU.add), [t3, pa[0]], [ytm])
                    pS = tmp[0]
                    P.add("pe", I("matmul", pS.a[:, :], Btm[tt].a[:NT, g * 128:(g + 1) * 128], xw.a[:NT, :], start=True, stop=True), [Btm[tt], xw], [pS])
                    P.add("dve", I("tensor_tensor", out=Hs[g].a[:].rearrange("p (h d) -> p h d", d=64), in0=Hs[g].a[:].rearrange("p (h d) -> p h d", d=64), in1=cdec[tt].a[:, hs8].unsqueeze(2).to_broadcast([128, 8, 64]), op=ALU.mult), [Hs[g], cdec[tt]], [Hs[g]])
                    P.add("dve", I("tensor_tensor", out=Hs[g].a[:], in0=Hs[g].a[:], in1=pS.a[:, :], op=ALU.add), [Hs[g], pS], [Hs[g]])
                    P.add("act", I("copy", out=Hb[g].a[:], in_=Hs[g].a[:]), [Hs[g]], [Hb[g]])
                    if G.sample:
                        P.dma("sp", ossms_d[l, tt, :, g * 512:(g + 1) * 512], Hs[g].a[:], reads=[Hs[g]])
                    elif s == G.nsuper - 1 and tt == NTT - 1:
                        P.dma("sp", ossmp_d[l, :, g * 512:(g + 1) * 512], Hs[g].a[:], reads=[Hs[g]])
                    tb = trb[nxt("trb", 2)]
                    P.add("pe", seq([I("transpose", out=tb.a[:, c * 128:c * 128 + NT], in_=ytm.a[:NT, c * 128:(c + 1) * 128], identity=identb.a[:NT, :NT]) for c in range(4)]), [ytm, identb], [tb])
                    P.add("dve", I("tensor_tensor", out=yT.a[:, 4 * g:4 * g + 4, tsl], in0=yT.a[:, 4 * g:4 * g + 4, tsl], in1=tb.a[:, 0:512].rearrange("p (c t) -> p c t", t=128)[:, :, :NT], op=ALU.mult), [yT, tb], [yT])
                pn = tmp[1]
                for c in range(4):
                    P.add("dve", I("tensor_tensor", out=sqt.a[:, 0:TS], in0=yT.a[:, 4 * g + c, 0:TS], in1=yT.a[:, 4 * g + c, 0:TS], op=ALU.mult), [yT], [sqt])
                    P.add("pe", I("matmul", pn.a[:, 0:TS], onesb.a[:], sqt.a[:, 0:TS], start=(c == 0), stop=(c == 3)), [onesb, sqt], [pn])
                P.add("dve", I("tensor_scalar", out=nrm.a[:, 0:TS], in0=pn.a[:, 0:TS], scalar1=1.0 / 512, scalar2=1e-6, op0=ALU.mult, op1=ALU.add), [pn], [nrm])
                P.add("act", I("activation", out=nrm.a[:, 0:TS], in_=nrm.a[:, 0:TS], func=AF.Sqrt), [nrm], [nrm])
                P.add("dve", I("reciprocal", out=nrm.a[:, 0:TS], in_=nrm.a[:, 0:TS]), [nrm], [nrm])
                for c in range(4):
                    P.add("dve", I("scalar_tensor_tensor", out=yT.a[:, 4 * g + c, 0:TS], in0=yT.a[:, 4 * g + c, 0:TS], scalar=pc("ssmw")[:, 4 * g + c:4 * g + c + 1], in1=nrm.a[:, 0:TS], op0=ALU.mult, op1=ALU.mult), [yT, prm, nrm], [yT])
            if CUT < 3:
                continue
            if G.sample:
                P.dma("sp", oconvs_d[l], carry.a[:], reads=[carry])
            elif s == G.nsuper - 1:
                P.dma("sp", oconvp_d[l], carry.a[:, :, 0, :], reads=[carry])

            for c in range(2):
                pss = fm_chunk(G, ws, wv_in, C_Q + c * 512)
                for ct in range(4):
                    pt, off = pss[ct]
                    P.add("act", I("copy", out=QT.a[:, 4 * c + ct, 0:TS], in_=pt.a[:, off:off + TS]), [pt], [QT])
            kdst = sk_d if G.sample else pk_d
            vdst = sv_d if G.sample else pv_d
            for c in range(2):
                pss = tm_chunk(G, ws, wv_in, C_K + c * 512)
                for tt in range(NTT):
                    k32 = kf[nxt("kf", 2)]
                    P.add("act", I("copy", out=k32.a[:NT, :], in_=pss[tt].a[:NT, :]), [pss[tt]], [k32])
                    P.dma("sp", kdst[l, tok0 + tt * NT: tok0 + (tt + 1) * NT, c * 512:(c + 1) * 512], k32.a[:NT, :], reads=[k32])
                    P.add("dve", I("tensor_copy", kbf.a[:NT, :], pss[tt].a[:NT, :]), [pss[tt]], [kbf])
                    tb = trb[nxt("trb", 2)]
                    P.add("pe", seq([I("transpose", out=tb.a[:, j * 128:j * 128 + NT], in_=kbf.a[:NT, j * 128:(j + 1) * 128], identity=identb.a[:NT, :NT]) for j in range(4)]), [kbf, identb], [tb])
                    P.add("act", I("copy", out=KTn.a[:, 4 * c:4 * c + 4, tt * NT:(tt + 1) * NT], in_=tb.a[:, 0:512].rearrange("p (j t) -> p j t", t=128)[:, :, :NT]), [tb], [KTn])
            if not G.sample:
                for tt in range(NTT):
                    P.dma("sp", kts_d[l, :, :, tok0 + tt * NT: tok0 + (tt + 1) * NT].rearrange("j p t -> p j t"), KTn.a[:, :, tt * NT:(tt + 1) * NT], reads=[KTn], writes=[kts_t[l][gt[tt]]])
            for c in range(2):
                pss = tm_chunk(G, ws, wv_in, C_V + c * 512)
                for tt in range(NTT):
                    k32 = kf[nxt("kf", 2)]
                    P.add("act", I("copy", out=k32.a[:NT, :], in_=pss[tt].a[:NT, :]), [pss[tt]], [k32])
                    P.dma("sp", vdst[l, tok0 + tt * NT: tok0 + (tt + 1) * NT, c * 512:(c + 1) * 512], k32.a[:NT, :], reads=[k32])
                    P.add("dve", I("tensor_copy", VAn[tt].a[:NT, 2 * c:2 * c + 2, 0:256], pss[tt].a[:NT, :].rearrange("p (h e) -> p h e", e=256)), [pss[tt]], [VAn[tt]])
            if not G.sample:
                for tt in range(NTT):
                    P.dma("sp", vs_d[l, tok0 + tt * NT: tok0 + (tt + 1) * NT, :, :], VAn[tt].a[:NT, :, 0:256], reads=[VAn[tt]], writes=[vs_t[l][gt[tt]]])
            for c in range(2):
                pss = fm_chunk(G, ws, wv_in, C_G + c * 512)
                for ct in range(4):
                    pt, off = pss[ct]
                    P.add("act", I("activation", out=yT.a[:, 16 + 4 * c + ct, 0:TS], in_=pt.a[:, off:off + TS], func=AF.Silu), [pt], [yT])

            if CUT < 4:
                continue
            sc_d = 128 ** -0.5
            for tt in range(NTT):
                qsl = slice(tt * NT, (tt + 1) * NT)
                for h in range(4):
                    if G.sample:
                        pieces = [(0, 8)]
                    else:
                        nh = gt[tt]
                        pieces = [(a, min(8, nh - a)) for a in range(0, nh, 8)]
                    O = [acc[0], acc[1]] if (rr["acc"] == 0) else [acc[2], acc[3]]
                    nxt("acc", 2)
                    nk_total = sum(n for _, n in pieces) + 1
                    kcount = 0
                    for (k0, nkt) in pieces:
                        bi = nxt("pt", 2)
                        kb = KTb[bi]
                        va = VA[bi]
                        if G.sample:
                            for m in range(2):
                                P.dma("pool", kb[m].a[:, :], ckT_d[l, tt, 2 * h + m], writes=[kb[m]])
                            P.dma("pool", va.a[:, :, 0:256], cv_d[l, tt, :, h, :].rearrange("(kt p) e -> p kt e", p=128), writes=[va])
                        else:
                            rd_k = [kts_t[l][k0 + i] for i in range(nkt)]
                            rd_v = [vs_t[l][k0 + i] for i in range(nkt)]
                            for m in range(2):
                                P.dma("sp", kb[m].a[:, 0:nkt * 128], kts_d[l, 2 * h + m, :, k0 * 128:(k0 + nkt) * 128], reads=rd_k, writes=[kb[m]])
                            P.dma("sp", va.a[:, 0:nkt, 0:256], vs_d[l, k0 * 128:(k0 + nkt) * 128, h, :].rearrange("(kt p) e -> p kt e", p=128), reads=rd_v, writes=[va])
                        for kt in range(nkt):
                            for m in range(2):
                                st = tmp[nxt("tmp", 2)]
                                P.add("pe", I("matmul", st.a[:, 0:NT], kb[m].a[:, kt * 128:(kt + 1) * 128], QT.a[:, 2 * h + m, qsl], start=True, stop=True), [kb[m], QT], [st])
                                pt_ = PT[2 * m + (kcount % 2)]
                                P.add("act", I("activation", out=pt_.a[:, 0:NT], in_=st.a[:, 0:NT], func=AF.Exp, scale=sc_d), [st], [pt_])
                                P.add("pe", I("matmul", O[m].a[:NT, 0:257], pt_.a[:, 0:NT], va.a[:, kt, :], start=(kcount == 0), stop=False), [pt_, va], [O[m]])
                            kcount += 1
                    for m in range(2):
                        st = tmp[nxt("tmp", 2)]
                        P.add("pe", I("matmul", st.a[:NT, 0:NT], KTn.a[:, 2 * h + m, qsl], QT.a[:, 2 * h + m, qsl], start=True, stop=True), [KTn, QT], [st])
                        pt_ = PT[2 * m + (kcount % 2)]
                        P.add("act", I("activation", out=pt_.a[:NT, 0:NT], in_=st.a[:NT, 0:NT], func=AF.Exp, scale=sc_d), [st], [pt_])
                        if not G.sample:
                            P.add("dve", I("memset", pt_.a[64:128, 0:64], 0.0), [], [pt_])
                        P.add("pe", I("matmul", O[m].a[:NT, 0:257], pt_.a[:NT, 0:NT], VAn[tt].a[:NT, h, :], start=(kcount == 0), stop=True), [pt_, VAn[tt]], [O[m]])
                    for m in range(2):
                        P.add("act", I("copy", out=Osb[m].a[:NT, :], in_=O[m].a[:NT, 0:257]), [O[m]], [Osb[m]])
                        P.add("dve", I("reciprocal", out=rcp.a[:NT, m:m + 1], in_=Osb[m].a[:NT, 256:257]), [Osb[m]], [rcp])
                    P.add("dve", I("tensor_tensor", out=rcp.a[:NT, 1:2], in0=rcp.a[:NT, 1:2], in1=neglam.a[:NT, 0:1], op=ALU.mult), [rcp, neglam], [rcp])
                    P.add("dve", I("tensor_scalar_mul", out=od.a[:NT, :], in0=Osb[0].a[:NT, 0:256], scalar1=rcp.a[:NT, 0:1]), [Osb[0], rcp], [od])
                    P.add("dve", I("scalar_tensor_tensor", out=od.a[:NT, :], in0=Osb[1].a[:NT, 0:256], scalar=rcp.a[:NT, 1:2], in1=od.a[:NT, :], op0=ALU.mult, op1=ALU.add), [Osb[1], rcp, od], [od])
                    P.add("act", I("activation", out=od2.a[:NT, :], in_=od.a[:NT, :], func=AF.Square, accum_out=rcp.a[:NT, 2:3]), [od], [od2, rcp])
                    P.add("dve", I("tensor_scalar", out=rcp.a[:NT, 2:3], in0=rcp.a[:NT, 2:3], scalar1=1.0 / 256, scalar2=1e-5, op0=ALU.mult, op1=ALU.add), [rcp], [rcp])
                    P.add("act", I("activation", out=rcp.a[:NT, 2:3], in_=rcp.a[:NT, 2:3], func=AF.Sqrt), [rcp], [rcp])
                    P.add("dve", I("reciprocal", out=rcp.a[:NT, 3:4], in_=rcp.a[:NT, 2:3]), [rcp], [rcp])
                    P.add("dve", I("tensor_scalar", out=od2.a[:NT, :], in0=od.a[:NT, :], scalar1=rcp.a[:NT, 3:4], scalar2=1.0 - LAM_INIT[l], op0=ALU.mult, op1=ALU.mult), [od, rcp], [od2])
                    P.add("dve", I("tensor_tensor", out=ydtm.a[:NT, :], in0=od2.a[:NT, :], in1=pc("subln")[:NT, :], op=ALU.mult), [od2, prm], [ydtm])
                    tb = trb[nxt("trb", 2)]
                    P.add("pe", seq([I("transpose", out=tb.a[:, c * 128:c * 128 + NT], in_=ydtm.a[:NT, c * 128:(c + 1) * 128], identity=identb.a[:NT, :NT]) for c in range(2)]), [ydtm, identb], [tb])
                    P.add("dve", I("tensor_tensor", out=yT.a[:, 16 + 2 * h:16 + 2 * h + 2, qsl], in0=yT.a[:, 16 + 2 * h:16 + 2 * h + 2, qsl], in1=tb.a[:, 0:256].rearrange("p (c t) -> p c t", t=128)[:, :, :NT], op=ALU.mult), [yT, tb], [yT])

            if CUT < 5:
                continue
            for c in range(2):
                pss = fm_chunk(G, ws, wv_in, C_QM + c * 512)
                for ct in range(4):
                    pt, off = pss[ct]
                    P.add("act", I("copy", out=QmT.a[:, 4 * c + ct, 0:TS], in_=pt.a[:, off:off + TS]), [pt], [QmT])
            for c in range(2):
                pss = fm_chunk(G, ws, wv_in, C_GM + c * 512)
                for ct in range(4):
                    pt, off = pss[ct]
                    P.add("act", I("activation", out=yT.a[:, 24 + 4 * c + ct, 0:TS], in_=pt.a[:, off:off + TS], func=AF.Silu), [pt], [yT])
            sc_m = 256 ** -0.5
            for tt in range(NTT):
                qsl = slice(tt * NT, (tt + 1) * NT)
                if G.sample:
                    P.dma("pool", MKT.a[:], cmkT_d[l, tt].rearrange("j p t -> p j t"), writes=[MKT])
                    for j in range(2):
                        P.dma("pool", MVA[j].a[:, :, 0:256], cmv_d[l, tt, j * 128:(j + 1) * 128, :, :], writes=[MVA[j]])
                for h in range(4):
                    Om = acc[2 * nxt("acc", 2)]
                    for j in range(2):
                        st = tmp[nxt("tmp", 2)]
                        P.add("pe", seq([I("matmul", st.a[:, 0:NT], MKT.a[:, 2 * h + dt_, j * 128:(j + 1) * 128], QmT.a[:, 2 * h + dt_, qsl], start=(dt_ == 0), stop=(dt_ == 1)) for dt_ in range(2)]), [MKT, QmT], [st])
                        pt_ = PT[nxt("pt", 2)]
                        P.add("act", I("activation", out=pt_.a[:, 0:NT], in_=st.a[:, 0:NT], func=AF.Exp, scale=sc_m), [st], [pt_])
                        P.add("pe", I("matmul", Om.a[:NT, 0:257], pt_.a[:, 0:NT], MVA[j].a[:, h, :], start=(j == 0), stop=(j == 1)), [pt_, MVA[j]], [Om])
                    P.add("act", I("copy", out=Osb[0].a[:NT, :], in_=Om.a[:NT, 0:257]), [Om], [Osb[0]])
                    P.add("dve", I("reciprocal", out=rcp.a[:NT, 0:1], in_=Osb[0].a[:NT, 256:257]), [Osb[0]], [rcp])
                    P.add("dve", I("tensor_scalar_mul", out=ydtm.a[:NT, :], in0=Osb[0].a[:NT, 0:256], scalar1=rcp.a[:NT, 0:1]), [Osb[0], rcp], [ydtm])
                    tb = trb[nxt("trb", 2)]
                    P.add("pe", seq([I("transpose", out=tb.a[:, c * 128:c * 128 + NT], in_=ydtm.a[:NT, c * 128:(c + 1) * 128], identity=identb.a[:NT, :NT]) for c in range(2)]), [ydtm, identb], [tb])
                    P.add("dve", I("tensor_tensor", out=yT.a[:, 24 + 2 * h:24 + 2 * h + 2, qsl], in0=yT.a[:, 24 + 2 * h:24 + 2 * h + 2, qsl], in1=tb.a[:, 0:256].rearrange("p (c t) -> p c t", t=128)[:, :, :NT], op=ALU.mult), [yT, tb], [yT])

            if CUT < 6:
                continue
            if s + 1 < G.nsuper:
                prenorm_A(G, (lambda tt, t1_=tok0 + TS: xsrc_d[t1_ + tt * NT: t1_ + (tt + 1) * NT, :]), (lambda tt, g1_=(s + 1) * NTT: xsrc_t(g1_ + tt)), NTT)
            for c in range(8):
                pss = tm_chunk(G, ws, wv_out, c * 512, 512, src=yT)
                for tt in range(NTT):
                    P.add("act", I("activation", out=junkb.a[:NT, 0:512], in_=pss[tt].a[:NT, :], func=AF.Square, accum_out=ssq.a[:NT, c:c + 1]), [pss[tt]], [junkb, ssq]) if tt == 0 else \
                        P.add("act", I("activation", out=junkb.a[:NT, 512:1024], in_=pss[tt].a[:NT, :], func=AF.Square, accum_out=nrm.a[:NT, c:c + 1]), [pss[tt]], [junkb, nrm])
                    P.add("dve", I("tensor_copy", outb[tt].a[:NT, c * 512:(c + 1) * 512], pss[tt].a[:NT, :]), [pss[tt]], [outb[tt]])
            for tt in range(NTT):
                sq_src = ssq if tt == 0 else nrm
                P.add("dve", I("reduce_sum", out=rstd.a[:NT, 0:1], in_=sq_src.a[:NT, 0:8], axis=AX.X), [sq_src], [rstd])
                P.add("dve", I("tensor_scalar", out=rstd.a[:NT, 0:1], in0=rstd.a[:NT, 0:1], scalar1=1.0 / D, scalar2=1e-6, op0=ALU.mult, op1=ALU.add), [rstd], [rstd])
                P.add("act", I("activation", out=rstd.a[:NT, 0:1], in_=rstd.a[:NT, 0:1], func=AF.Sqrt), [rstd], [rstd])
                P.add("dve", I("reciprocal", out=rstd.a[:NT, 0:1], in_=rstd.a[:NT, 0:1]), [rstd], [rstd])
                r0 = tok0 + tt * NT
                for cb in range(4):
                    xb = xblk[cb % 2]
                    csl = slice(cb * 1024, (cb + 1) * 1024)
                    P.dma("sp", wpostb.a[:, :], wpost_d[l, :, csl], writes=[wpostb])
                    P.dma("sp", xb.a[:NT, :], xsrc_d[r0:r0 + NT, csl], reads=xsrc_t(gt[tt]), writes=[xb])
                    P.add("dve", I("scalar_tensor_tensor", out=t1.a[:NT, :], in0=outb[tt].a[:NT, cb * 1024:cb * 1024 + 512], scalar=rstd.a[:NT, 0:1], in1=wpostb.a[:NT, 0:512], op0=ALU.mult, op1=ALU.mult), [outb[tt], rstd, wpostb], [t1])
                    P.add("dve", I("scalar_tensor_tensor", out=t3.a[:NT, :], in0=outb[tt].a[:NT, cb * 1024 + 512:cb * 1024 + 1024], scalar=rstd.a[:NT, 0:1], in1=wpostb.a[:NT, 512:1024], op0=ALU.mult, op1=ALU.mult), [outb[tt], rstd, wpostb], [t3])
                    P.add("dve", I("tensor_tensor", out=xb.a[:NT, 0:512], in0=xb.a[:NT, 0:512], in1=t1.a[:NT, :], op=ALU.add), [xb, t1], [xb])
                    P.add("dve", I("tensor_tensor", out=xb.a[:NT, 512:1024], in0=xb.a[:NT, 512:1024], in1=t3.a[:NT, :], op=ALU.add), [xb, t3], [xb])
                    P.dma("sp", xdst_d[r0:r0 + NT, csl], xb.a[:NT, :], reads=[xb], writes=xdst_t(gt[tt]))

    Gp = Group("p", 128, 2, NSUP)
    Gs = Group("s", 16, 2, 1)
    Gm = Group("m", 128, 2, 1)
    def program():
        for l in LAYERS:
            P.dma("sp", prm.a[:], prm_d[l], writes=[prm])
            P.add("act", I("activation", out=abc.a[:], in_=pc("alog"), func=AF.Exp), [prm], [abc])
            P.add("dve", I("tensor_scalar_mul", out=abc.a[:], in0=abc.a[:], scalar1=-1.0), [abc], [abc])
            for i, (a, b) in enumerate((("lq1", "lk1"), ("lq2", "lk2"))):
                P.add("dve", I("tensor_tensor", out=lamtmp.a[:], in0=pc(a), in1=pc(b), op=ALU.mult), [prm], [lamtmp])
                P.add("dve", I("reduce_sum", out=lams.a[:, i:i + 1], in_=lamtmp.a[:], axis=AX.X), [lamtmp], [lams])
            P.add("act", I("activation", out=lams.a[:, 2:4], in_=lams.a[:, 0:2], func=AF.Exp), [lams], [lams])
            P.add("dve", I("tensor_tensor", out=neglam.a[:], in0=lams.a[:, 3:4], in1=lams.a[:, 2:3], op=ALU.subtract), [lams], [neglam])
            P.add("dve", I("tensor_scalar_add", out=neglam.a[:], in0=neglam.a[:], scalar1=-LAM_INIT[l]), [neglam], [neglam])
            prenorm_A(Gm, lambda tt: mem_d[tt * 128:(tt + 1) * 128, :], lambda tt: [], 2)
            prenorm_B(Gm, pc("memw"), 2)
            wv_kv = wkv_d[l].rearrange("(kt p) c -> p kt c", p=128)
            WVN[id(wv_kv)] = f"kv{l}"
            for c in range(4):
                pss = tm_chunk(Gm, WS, wv_kv, c * 512)
                for tt in range(2):
                    k32 = kf[nxt("kf", 2)]
                    P.add("act", I("copy", out=k32.a[:, :], in_=pss[tt].a[:, :]), [pss[tt]], [k32])
                    dst = pmk_d if c < 2 else pmv_d
                    P.dma("sp", dst[l, tt * 128:(tt + 1) * 128, (c % 2) * 512:(c % 2 + 1) * 512], k32.a[:, :], reads=[k32])
                    if c < 2:
                        P.add("dve", I("tensor_copy", kbf.a[:, :], pss[tt].a[:, :]), [pss[tt]], [kbf])
                        tb = trb[nxt("trb", 2)]
                        P.add("pe", seq([I("transpose", out=tb.a[:, j * 128:(j + 1) * 128], in_=kbf.a[:, j * 128:(j + 1) * 128], identity=identb.a[:]) for j in range(4)]), [kbf, identb], [tb])
                        P.add("act", I("copy", out=MKT.a[:, 4 * c:4 * c + 4, tt * 128:(tt + 1) * 128], in_=tb.a[:, 0:512].rearrange("p (j t) -> p j t", t=128)), [tb], [MKT])
                    else:
                        cc = c - 2
                        P.add("dve", I("tensor_copy", MVA[tt].a[:, 2 * cc:2 * cc + 2, 0:256], pss[tt].a[:, :].rearrange("p (h e) -> p h e", e=256)), [pss[tt]], [MVA[tt]])
            last = (l == LAYERS[-1])
            if DO_PROMPT:
                run_layer(l, Gp, xp_d if l == 0 else x1p_d, (lambda i: []) if l == 0 else (lambda i: [x1p_t[i]]),
                          yp_d if last else x1p_d, (lambda i: []) if last else (lambda i: [x1p_t[i]]), WS)
            if DO_SAMPLE:
                run_layer(l, Gs, xs_d if l == 0 else x1s_d, (lambda i: []) if l == 0 else (lambda i: [x1s_t[i]]),
                          ys_d if last else x1s_d, (lambda i: []) if last else (lambda i: [x1s_t[i]]), WS)

    P.maxops = cfg.get('maxops', 10 ** 9)
    WS = WStream()
    P.dry = True
    rr0 = dict(rr)
    program()
    P.dry = False
    rr.update(rr0)
    program()

    print('nops', len(P.ops), 'sbuf_free', nc.sbuf_bytes_remaining, flush=True)
    if cfg.get('dump'):
        for op in P.ops:
            print(op.idx, op.eng, 'dma' if op.dma else '', [d.idx for d in op.deps])
    P.emit(nc)
    return nc


def _fm(v, ntile):
    return np.ascontiguousarray(v.reshape(ntile, 128).T)


def kernel(x_prompt, x_sample, mem_prompt, cache_conv, state_ssm, cache_k, cache_v, cache_mem_k, cache_mem_v,
           norm_pre_w, norm_post_w, w_in, conv_w, conv_b, dt_bias, a_log, d_skip, ssm_norm_w,
           lambda_q1, lambda_k1, lambda_q2, lambda_k2, subln_w, mem_norm_w, w_mem_kv, w_out):
    f = np.float32
    A = lambda a: np.ascontiguousarray(np.asarray(a, dtype=f))
    x_prompt, x_sample, mem_prompt = A(x_prompt), A(x_sample), A(mem_prompt)
    cache_conv, state_ssm, cache_k, cache_v = A(cache_conv), A(state_ssm), A(cache_k), A(cache_v)
    cache_mem_k, cache_mem_v = A(cache_mem_k), A(cache_mem_v)
    w_in, w_out, w_mem_kv = A(w_in), A(w_out), A(w_mem_kv)
    prm = np.zeros((2, 128, NPRM), f)
    bc = lambda v: np.broadcast_to(np.asarray(v, f)[None, :], (128, len(v)))
    for l in range(2):
        def put(n, arr):
            a, b = PO[n]
            prm[l, :, a:b] = arr
        put("wpre", _fm(np.asarray(norm_pre_w[l], f), 32))
        put("memw", _fm(np.asarray(mem_norm_w[l], f), 32))
        cw = np.asarray(conv_w[l], f)
        put("convw", cw.reshape(4, 24, 128).transpose(2, 1, 0).reshape(128, 96))
        put("convb", _fm(np.asarray(conv_b[l], f), 24))
        put("dtb", bc(dt_bias[l]))
        put("alog", bc(a_log[l]))
        put("dsk", bc(d_skip[l]))
        put("ssmw", _fm(np.asarray(ssm_norm_w[l], f), 16))
        put("lq1", bc(lambda_q1[l])); put("lk1", bc(lambda_k1[l]))
        put("lq2", bc(lambda_q2[l])); put("lk2", bc(lambda_k2[l]))
        put("subln", bc(subln_w[l]))
    wpost = np.ascontiguousarray(np.broadcast_to(np.asarray(norm_post_w, f)[:, None, :], (2, 128, D)))
    cst = np.zeros((128, 4, 128), f)
    ii = np.arange(128)
    cst[:, 0, :] = np.eye(128)
    cst[:, 1, :] = (ii[:, None] <= ii[None, :])
    cst[:, 2, :] = (ii[:, None] > ii[None, :])
    cst[:, 3, :] = 1.0
    in_maps = []
    for c in range(NCORE):
        b = c % 4
        ss = slice(2 * c, 2 * c + 2)
        cc = cache_conv[:, ss]
        cconv = np.ascontiguousarray(cc.reshape(2, 2, 3, 24, 128).transpose(0, 4, 3, 1, 2))
        sssm = np.ascontiguousarray(state_ssm[:, ss].reshape(2, 2, 2048, 128).transpose(0, 1, 3, 2))
        ckT = np.ascontiguousarray(cache_k[:, ss].reshape(2, 2, 1024, 8, 128).transpose(0, 1, 3, 4, 2))
        cv = np.ascontiguousarray(cache_v[:, ss])
        cmkT = np.ascontiguousarray(cache_mem_k[:, ss].reshape(2, 2, 256, 8, 128).transpose(0, 1, 3, 4, 2))
        cmv = np.ascontiguousarray(cache_mem_v[:, ss])
        in_maps.append({
            "xp": x_prompt[b], "xs": np.ascontiguousarray(x_sample[ss].reshape(32, D)), "mem": mem_prompt[b],
            "cconv": cconv, "sssm": sssm, "ckT": ckT, "cv": cv, "cmkT": cmkT, "cmv": cmv,
            "w_in": w_in, "w_out": w_out, "w_kv": w_mem_kv, "prm": prm, "wpost": wpost, "cst": cst,
        })
    nc = build()
    res = run_bass_kernel_spmd(nc, in_maps, core_ids=list(range(NCORE)))
    R = res.results
    y_prompt = np.stack([R[b]["y_p"] for b in range(4)])
    y_sample = np.concatenate([R[c]["y_s"].reshape(2, 16, D) for c in range(NCORE)])
    p_conv = np.stack([R[b]["o_conv_p"].transpose(0, 3, 2, 1).reshape(2, 3, 3072) for b in range(4)], axis=1)
    p_ssm = np.stack([R[b]["o_ssm_p"].transpose(0, 2, 1).reshape(2, 32, 64, 128) for b in range(4)], axis=1)
    p_k = np.stack([R[b]["p_k"].reshape(2, SEQ, 4, 2, 128) for b in range(4)], axis=1)
    p_v = np.stack([R[b]["p_v"].reshape(2, SEQ, 4, 256) for b in range(4)], axis=1)
    p_mk = np.stack([R[b]["p_mk"].reshape(2, 256, 4, 256) for b in range(4)], axis=1)
    p_mv = np.stack([R[b]["p_mv"].reshape(2, 256, 4, 256) for b in range(4)], axis=1)
    s_conv = np.concatenate([R[c]["o_conv_s"].transpose(0, 3, 4, 2, 1).reshape(2, 2, 3, 3072) for c in range(NCORE)], axis=1)
    s_ssm = np.concatenate([R[c]["o_ssm_s"].transpose(0, 1, 3, 2).reshape(2, 2, 32, 64, 128) for c in range(NCORE)], axis=1)
    s_k = np.concatenate([R[c]["s_k"].reshape(2, 2, 16, 4, 2, 128) for c in range(NCORE)], axis=1)
    s_v = np.concatenate([R[c]["s_v"].reshape(2, 2, 16, 4, 256) for c in range(NCORE)], axis=1)
    outs = (y_prompt, y_sample, p_conv, p_ssm, p_k, p_v, p_mk, p_mv, s_conv, s_ssm, s_k, s_v)
    return tuple(np.ascontiguousarray(o, dtype=np.float32) for o in outs)
```

```python
import math
from contextlib import ExitStack
import numpy as np
import concourse.bass as bass
import concourse.mybir as mybir
from concourse.bass_utils import run_bass_kernel_spmd

F32 = mybir.dt.float32
BF16 = mybir.dt.bfloat16
AF = mybir.ActivationFunctionType
ALU = mybir.AluOpType
AX = mybir.AxisListType

D = 4096
SEQ = 4096
DIN = 11296
NCORE = 8
COMPUTE = ("pe", "act", "dve", "pool")
NSLOT = {"sp": 16, "pool": 8}
LAM_INIT = [0.8 - 0.6 * math.exp(-0.3 * l) for l in range(2)]
C_Z, C_X, C_B, C_C, C_DT, C_Q, C_K, C_V, C_G, C_QM, C_GM = 0, 2048, 4096, 4608, 5120, 5152, 6176, 7200, 8224, 9248, 10272
PO = {}
_o = 0
for _n, _w in (("wpre", 32), ("memw", 32), ("convw", 96), ("convb", 24), ("dtb", 32), ("alog", 32), ("dsk", 32),
               ("ssmw", 16), ("lq1", 128), ("lk1", 128), ("lq2", 128), ("lk2", 128), ("subln", 256)):
    PO[_n] = (_o, _o + _w)
    _o += _w
NPRM = _o


class TT:
    __slots__ = ("name", "last_w", "readers", "excl")

    def __init__(self, name):
        self.name = name
        self.last_w = None
        self.readers = {}
        self.excl = False


class Tile:
    __slots__ = ("a", "t")

    def __init__(self, a, name):
        self.a = a
        self.t = TT(name)


class Op:
    __slots__ = ("idx", "eng", "fn", "dma", "deps", "needs_inc", "cnt", "slot", "slot_val")

    def __init__(self, idx, eng, fn, dma):
        self.idx = idx
        self.eng = eng
        self.fn = fn
        self.dma = dma
        self.deps = []
        self.needs_inc = False
        self.cnt = 0
        self.slot = None
        self.slot_val = 0


def I(method, *args, **kw):
    return lambda e: getattr(e, method)(*args, **kw)


def seq(thunks):
    def fn(e):
        r = None
        for t in thunks:
            r = t(e)
        return r
    return fn


class Prog:
    def __init__(self):
        self.ops = []
        self.ndma = {"sp": 0, "pool": 0}
        self.dry = False
        self.maxops = 10 ** 9

    def add(self, eng, fn, reads=(), writes=(), dma=False):
        if self.dry or len(self.ops) >= self.maxops:
            return None
        op = Op(len(self.ops), eng, fn, dma)
        deps = {}
        for t in reads:
            t = t.t if hasattr(t, 't') else t
            if t.last_w is not None:
                deps[t.last_w.idx] = t.last_w
            if t.excl:
                for k, r in t.readers.items():
                    if k != eng:
                        deps[r.idx] = r
        for t in writes:
            t = t.t if hasattr(t, 't') else t
            if t.last_w is not None:
                deps[t.last_w.idx] = t.last_w
            for r in t.readers.values():
                deps[r.idx] = r
        op.deps = [d for d in deps.values() if d.dma or d.eng != eng or eng != "pe"]
        for d in op.deps:
            d.needs_inc = True
        for t in reads:
            t = t.t if hasattr(t, 't') else t
            t.readers[("dma", op.idx) if dma else eng] = op
        for t in writes:
            t = t.t if hasattr(t, 't') else t
            t.last_w = op
            t.readers = {}
        if dma:
            i = self.ndma[eng]
            self.ndma[eng] += 1
            op.slot = i % NSLOT[eng]
            op.slot_val = 16 * (i // NSLOT[eng] + 1)
        self.ops.append(op)
        return op

    def dma(self, q, out_ap, in_ap, reads=(), writes=(), **kw):
        return self.add(q, I("dma_start", out=out_ap, in_=in_ap, **kw), reads, writes, dma=True)

    def emit(self, nc):
        with ExitStack() as es:
            sems = {e: es.enter_context(nc.semaphore("s_" + e)) for e in COMPUTE}
            dsem = {q: [es.enter_context(nc.semaphore(f"d_{q}{i}")) for i in range(NSLOT[q])] for q in NSLOT}
            cnt = {e: 0 for e in COMPUTE}
            for op in self.ops:
                if not op.dma:
                    if op.needs_inc:
                        cnt[op.eng] += 1
                    op.cnt = cnt[op.eng]
            final_slot = {q: [0] * NSLOT[q] for q in NSLOT}
            for op in self.ops:
                if op.dma:
                    final_slot[op.eng][op.slot] = op.slot_val
            block = es.enter_context(nc.Block())
            by_eng = {e: [op for op in self.ops if op.eng == e] for e in ("pe", "act", "dve", "pool", "sp")}

            def make(ename):
                ops = by_eng[ename]

                def body(e):
                    waited = {}
                    for op in ops:
                        for d in op.deps:
                            if d.dma:
                                key = (d.eng, d.slot)
                                if waited.get(key, 0) < d.slot_val:
                                    e.wait_ge(dsem[d.eng][d.slot], d.slot_val)
                                    waited[key] = d.slot_val
                            else:
                                if waited.get(d.eng, 0) < d.cnt:
                                    e.wait_ge(sems[d.eng], d.cnt)
                                    waited[d.eng] = d.cnt
                        if op.dma:
                            if op.slot_val > 16:
                                key = (op.eng, op.slot)
                                if waited.get(key, 0) < op.slot_val - 16:
                                    e.wait_ge(dsem[op.eng][op.slot], op.slot_val - 16)
                                    waited[key] = op.slot_val - 16
                            op.fn(e).then_inc(dsem[op.eng][op.slot], 16)
                        else:
                            ins = op.fn(e)
                            if op.needs_inc:
                                ins.then_inc(sems[op.eng], 1)
                    if ename == "sp":
                        for q in NSLOT:
                            for s in range(NSLOT[q]):
                                if final_slot[q][s] > 0:
                                    e.wait_ge(dsem[q][s], final_slot[q][s])
                return body

            block.tensor(make("pe"))
            block.scalar(make("act"))
            block.vector(make("dve"))
            block.gpsimd(make("pool"))
            block.sync(make("sp"))


class Group:
    def __init__(self, name, NT, NTT, nsuper):
        self.name, self.NT, self.NTT, self.nsuper = name, NT, NTT, nsuper
        self.TS = NT * NTT
        self.sample = name == "s"


def build(cfg=None):
    cfg = cfg or {}
    STG = cfg.get('stages', ('ssd', 'attn', 'mem', 'out'))
    NSUP = cfg.get('nsuper', 16)
    LAYERS = cfg.get('layers', (0, 1))
    DO_SAMPLE = cfg.get('sample', True)
    DO_PROMPT = cfg.get('prompt', True)
    DO_MEMKV = cfg.get('memkv', True)
    CUT = cfg.get('cut', 99)
    nc = bass.Bass("TRN2", target_bir_lowering=False)
    P = Prog()
    es = ExitStack()

    def din(name, shape, dt=F32):
        return nc.dram_tensor(name, shape, dt, kind="ExternalInput").ap()

    def dout(name, shape, dt=F32):
        return nc.dram_tensor(name, shape, dt, kind="ExternalOutput").ap()

    def dscr(name, shape, dt=F32):
        return nc.dram_tensor(name, shape, dt, kind="Internal").ap()

    def sb(name, shape, dt=F32):
        return Tile(es.enter_context(nc.sbuf_tensor("sb_" + name, shape, dt)), name)

    def psb(name, shape, dt=F32):
        t = Tile(es.enter_context(nc.psum_tensor("ps_" + name, shape, dt)), name)
        t.t.excl = True
        return t

    xp_d = din("xp", [SEQ, D])
    xs_d = din("xs", [32, D])
    mem_d = din("mem", [256, D])
    cconv_d = din("cconv", [2, 128, 24, 2, 3])
    sssm_d = din("sssm", [2, 2, 128, 2048])
    ckT_d = din("ckT", [2, 2, 8, 128, 1024])
    cv_d = din("cv", [2, 2, 1024, 4, 256])
    cmkT_d = din("cmkT", [2, 2, 8, 128, 256])
    cmv_d = din("cmv", [2, 2, 256, 4, 256])
    win_d = din("w_in", [2, D, DIN])
    wout_d = din("w_out", [2, D, D])
    wkv_d = din("w_kv", [2, D, 2048])
    prm_d = din("prm", [2, 128, NPRM])
    wpost_d = din("wpost", [2, 128, D])
    cst_d = din("cst", [128, 4, 128])

    yp_d = dout("y_p", [SEQ, D])
    ys_d = dout("y_s", [32, D])
    oconvp_d = dout("o_conv_p", [2, 128, 24, 3])
    ossmp_d = dout("o_ssm_p", [2, 128, 2048])
    pk_d = dout("p_k", [2, SEQ, 1024])
    pv_d = dout("p_v", [2, SEQ, 1024])
    pmk_d = dout("p_mk", [2, 256, 1024])
    pmv_d = dout("p_mv", [2, 256, 1024])
    oconvs_d = dout("o_conv_s", [2, 128, 24, 2, 3])
    ossms_d = dout("o_ssm_s", [2, 2, 128, 2048])
    sk_d = dout("s_k", [2, 32, 1024])
    sv_d = dout("s_v", [2, 32, 1024])

    x1p_d = dscr("x1p", [SEQ, D])
    x1s_d = dscr("x1s", [32, D])
    kts_d = dscr("kts", [2, 8, 128, SEQ], BF16)
    vs_d = dscr("vs", [2, SEQ, 4, 256], BF16)
    x1p_t = [TT(f"x1p{i}") for i in range(32)]
    x1s_t = [TT(f"x1s{i}") for i in range(2)]
    kts_t = [[TT(f"kts{l}_{i}") for i in range(32)] for l in range(2)]
    vs_t = [[TT(f"vs{l}_{i}") for i in range(32)] for l in range(2)]

    cst = sb("cst", [128, 4, 128])
    identb = sb("identb", [128, 128], BF16)
    onesb = sb("onesb", [128, 128], BF16)
    prm = sb("prm", [128, NPRM])
    abc = sb("abc", [128, 32])
    neglam = sb("neglam", [128, 1])
    lamtmp = sb("lamtmp", [128, 128])
    lams = sb("lams", [128, 4])
    wpostb = sb("wpostb", [128, 1024])
    xblk = [sb(f"xblk{i}", [128, 1024]) for i in range(2)]
    hsf = [sb(f"hsf{i}", [128, D], BF16) for i in range(2)]
    junkb = sb("junkb", [128, 1024], BF16)
    ssq = sb("ssq", [128, 8])
    rstd = sb("rstd", [128, 2])
    hT = sb("hT", [128, 32, 256], BF16)
    yT = sb("yT", [128, 32, 256], BF16)

    class View:
        def __init__(self, a, t):
            self.a, self.t = a, t
    outb = [View(hT.a[:, 16 * i:16 * (i + 1), :].rearrange("p a b -> p (a b)"), hT.t) for i in range(2)]
    wbuf = [sb(f"wbuf{i}", [128, 32, 256], BF16) for i in range(2)]
    carry = sb("carry", [128, 24, 2, 3])
    xpb = sb("xpb", [128, 4, 262])
    cacc = sb("cacc", [128, 4, 256])
    xpb_t = [TT(f"xpb{i}") for i in range(4)]
    cacc_t = [TT(f"cacc{i}") for i in range(4)]
    BT = sb("BT", [128, 4, 256], BF16)
    CT = sb("CT", [128, 4, 256], BF16)
    Btm = [sb(f"Btm{i}", [128, 512], BF16) for i in range(2)]
    xsT = sb("xsT", [128, 4, 256], BF16)
    dtt = [sb(f"dtt{i}", [128, 32]) for i in range(2)]
    dat = [sb(f"dat{i}", [128, 32]) for i in range(2)]
    ecum = [sb(f"ecum{i}", [128, 32]) for i in range(2)]
    wst = [sb(f"wst{i}", [128, 32]) for i in range(2)]
    cdec = [sb(f"cdec{i}", [128, 32]) for i in range(2)]
    sptmp = sb("sptmp", [128, 32])
    xtm = sb("xtm", [128, 512], BF16)
    xdt = sb("xdt", [128, 512], BF16)
    xw = sb("xw", [128, 512], BF16)
    xD = sb("xD", [128, 512])
    Rt = sb("Rt", [128, 8, 128])
    Et = sb("Et", [128, 8, 128], BF16)
    MT = sb("MT", [128, 8, 128], BF16)
    cbm = sb("cbm", [128, 128], BF16)
    t1 = sb("t1", [128, 512])
    t3 = sb("t3", [128, 512])
    ytm = sb("ytm", [128, 512], BF16)
    Hs = [sb(f"H{g}", [128, 512]) for g in range(4)]
    Hb = [sb(f"Hb{g}", [128, 512], BF16) for g in range(4)]
    nrm = sb("nrm", [128, 256])
    sqt = sb("sqt", [128, 256], BF16)
    QT = sb("QT", [128, 8, 256], BF16)
    QmT = sb("QmT", [128, 8, 256], BF16)
    kf = [sb(f"kf{i}", [128, 512]) for i in range(2)]
    kbf = sb("kbf", [128, 512], BF16)
    KTn = sb("KTn", [128, 8, 256], BF16)
    VAn = [sb(f"VAn{i}", [128, 4, 257], BF16) for i in range(2)]
    KTb = [[sb(f"KTb{i}{m}", [128, 1024], BF16) for m in range(2)] for i in range(2)]
    VA = [sb(f"VA{i}", [128, 8, 257], BF16) for i in range(2)]
    PT = [sb(f"PT{i}", [128, 128], BF16) for i in range(4)]
    PT4 = [sb(f"PT4{i}", [128, 512], BF16) for i in range(2)]
    Osb = [sb(f"Osb{i}", [128, 257]) for i in range(2)]
    rcp = sb("rcp", [128, 4])
    od = sb("od", [128, 256])
    od2 = sb("od2", [128, 256])
    ydtm = sb("ydtm", [128, 256], BF16)
    MKT = sb("MKT", [128, 8, 256], BF16)
    MVA = [sb(f"MVA{j}", [128, 4, 257], BF16) for j in range(2)]
    acc = [psb(f"acc{i}", [128, 512]) for i in range(4)]
    tmp = [psb(f"tmp{i}", [128, 512]) for i in range(2)]
    trb = [psb(f"trb{i}", [128, 1024], BF16) for i in range(2)]
    rr = {"tmp": 0, "trb": 0, "acc": 0, "pt": 0, "kf": 0, "p4": 0}

    def nxt(kind, n):
        rr[kind] = (rr[kind] + 1) % n
        return rr[kind]

    ident = cst.a[:, 0, :]
    MLE = cst.a[:, 1, :]
    MGT = cst.a[:, 2, :]
    ones = cst.a[:, 3, :]

    def pc(name):
        a, b = PO[name]
        return prm.a[:, a:b]

    P.dma("sp", cst.a[:], cst_d, writes=[cst])
    P.add("dve", I("tensor_copy", identb.a[:], ident), [cst], [identb])
    P.add("dve", I("tensor_copy", onesb.a[:], ones), [cst], [onesb])
    for t in VAn + VA + MVA:
        P.add("pool", I("memset", t.a[:, :, 256:257], 1.0), [], [t])

    NWSCR = 160
    wscr_parts = [dscr(f"wscr{i}", [40, 128, 32, 256], BF16) for i in range(4)]
    WVN = {}

    def wscr_ap(idx):
        return wscr_parts[idx // 40][idx % 40]
    wscr_t = [TT(f"wscr{i}") for i in range(NWSCR)]

    class WStream:
        def __init__(self):
            self.descs = []
            self.issued = 0
            self.used = 0
            self.slot = {}

        def _issue(self, i):
            key, src, ncols = self.descs[i]
            b = wbuf[i % 2]
            if key not in self.slot:
                idx = len(self.slot)
                assert idx < NWSCR
                self.slot[key] = idx
                P.dma("pool", b.a[:, :, 0:ncols], src, writes=[b])
                P.dma("sp", wscr_ap(idx)[:, :, 0:ncols], b.a[:, :, 0:ncols], reads=[b], writes=[wscr_t[idx]])
            else:
                idx = self.slot[key]
                P.dma("pool", b.a[:, :, 0:ncols], wscr_ap(idx)[:, :, 0:ncols], reads=[wscr_t[idx]], writes=[b])

        def next(self, key, src, ncols):
            if P.dry:
                self.descs.append((key, src, ncols))
                return wbuf[0]
            i = self.used
            while self.issued < min(i + 2, len(self.descs)):
                self._issue(self.issued)
                self.issued += 1
            self.used += 1
            return wbuf[i % 2]

    def prenorm_A(G, rows_ap, rows_t, ntiles):
        NT = G.NT
        for tt in range(ntiles):
            src = rows_ap(tt)
            rt = rows_t(tt)
            for cb in range(4):
                xb = xblk[cb % 2]
                P.dma("sp", xb.a[:NT, :], src[:, cb * 1024:(cb + 1) * 1024], reads=rt, writes=[xb])
                P.add("act", I("activation", out=junkb.a[:NT, :], in_=xb.a[:NT, :], func=AF.Square, accum_out=ssq.a[:NT, cb:cb + 1]),
                      [xb], [junkb, ssq])
            P.add("dve", I("reduce_sum", out=rstd.a[:NT, 0:1], in_=ssq.a[:NT, 0:4], axis=AX.X), [ssq], [rstd])
            P.add("dve", I("tensor_scalar", out=rstd.a[:NT, 0:1], in0=rstd.a[:NT, 0:1], scalar1=1.0 / D, scalar2=1e-6, op0=ALU.mult, op1=ALU.add), [rstd], [rstd])
            P.add("act", I("activation", out=rstd.a[:NT, 0:1], in_=rstd.a[:NT, 0:1], func=AF.Sqrt), [rstd], [rstd])
            P.add("dve", I("reciprocal", out=rstd.a[:NT, 0:1], in_=rstd.a[:NT, 0:1]), [rstd], [rstd])
            for cb in range(4):
                xb = xblk[cb % 2]
                P.dma("sp", xb.a[:NT, :], src[:, cb * 1024:(cb + 1) * 1024], reads=rt, writes=[xb])
                P.add("dve", I("tensor_scalar_mul", out=hsf[tt].a[:NT, cb * 1024:(cb + 1) * 1024], in0=xb.a[:NT, :], scalar1=rstd.a[:NT, 0:1]), [xb, rstd], [hsf[tt]])

    def prenorm_B(G, wcol, ntiles):
        NT = G.NT
        for tt in range(ntiles):
            for cb in range(4):
                tb = trb[nxt("trb", 2)]
                P.add("pe", seq([I("transpose", out=tb.a[:, j * 128:j * 128 + NT], in_=hsf[tt].a[:NT, cb * 1024 + j * 128:cb * 1024 + (j + 1) * 128], identity=identb.a[:NT, :NT]) for j in range(8)]),
                      [hsf[tt], identb], [tb])
                o = hT.a[:, cb * 8:(cb + 1) * 8, tt * NT:(tt + 1) * NT]
                i0 = tb.a[:, :].rearrange("p (j t) -> p j t", t=128)[:, :, 0:NT]
                i1 = wcol[:, cb * 8:(cb + 1) * 8].unsqueeze(2).to_broadcast([128, 8, NT])
                P.add("dve", I("tensor_tensor", out=o, in0=i0, in1=i1, op=ALU.mult), [tb, prm], [hT])

    def fm_chunk(G, ws, wv, c0, ncols=512):
        TS = G.TS
        base = 2 * nxt("acc", 2)
        pss = [(acc[base + ct // 2], (ct % 2) * 256) for ct in range(4)]
        for half in range((ncols + 255) // 256):
            nc_ = min(256, ncols - half * 256)
            wb = ws.next((WVN[id(wv)], c0 + half * 256), wv[:, :, c0 + half * 256:c0 + half * 256 + nc_], nc_)
            th = []
            for c2 in range(nc_ // 128):
                pt, off = pss[half * 2 + c2]
                for kt in range(32):
                    th.append(I("matmul", pt.a[:, off:off + TS], wb.a[:, kt, c2 * 128:(c2 + 1) * 128], hT.a[:, kt, 0:TS], start=(kt == 0), stop=(kt == 31)))
            P.add("pe", seq(th), [wb, hT], [pss[half * 2][0]])
        return pss

    def tm_chunk(G, ws, wv, c0, ncols=512, src=None):
        NT = G.NT
        sT = hT if src is None else src
        base = 2 * nxt("acc", 2)
        pss = [acc[base], acc[base + 1]]
        for half in range((ncols + 255) // 256):
            nc_ = min(256, ncols - half * 256)
            wb = ws.next((WVN[id(wv)], c0 + half * 256), wv[:, :, c0 + half * 256:c0 + half * 256 + nc_], nc_)
            th = []
            for tt in range(G.NTT):
                for kt in range(32):
                    th.append(I("matmul", pss[tt].a[:NT, half * 256:half * 256 + nc_], sT.a[:, kt, tt * NT:(tt + 1) * NT], wb.a[:, kt, 0:nc_], start=(kt == 0), stop=(kt == 31)))
            P.add("pe", seq(th), [wb, sT], pss[:G.NTT])
        return pss

    def run_layer(l, G, xsrc_d, xsrc_t, xdst_d, xdst_t, ws):
        NT, NTT, TS = G.NT, G.NTT, G.TS
        wv_in = win_d[l].rearrange("(kt p) c -> p kt c", p=128)
        wv_out = wout_d[l].rearrange("(kt p) c -> p kt c", p=128)
        WVN[id(wv_in)] = f"in{l}"
        WVN[id(wv_out)] = f"out{l}"
        if not G.sample:
            P.add("pool", I("memset", carry.a[:], 0.0), [], [carry])
            for g in range(4):
                P.add("pool", I("memset", Hs[g].a[:], 0.0), [], [Hs[g]])
                P.add("pool", I("memset", Hb[g].a[:], 0.0), [], [Hb[g]])
        else:
            P.dma("sp", carry.a[:], cconv_d[l], writes=[carry])

        def conv_chunk(pss, ct0, dst):
            nseg = 2 if G.sample else 1
            sl = TS // nseg
            xv = xpb.a[:, :, 0:nseg * (3 + sl)].rearrange("p c (s t) -> p c s t", s=nseg)
            P.add("dve", I("tensor_copy", xv[:, :, :, 0:3], carry.a[:, ct0:ct0 + 4, 0:nseg, :]), [carry], xpb_t)
            for ct in range(4):
                pt, off = pss[ct]
                P.add("act", I("copy", out=xv[:, ct, :, 3:3 + sl], in_=pt.a[:, off:off + TS].rearrange("p (s t) -> p s t", s=nseg)), [pt], [xpb_t[ct]])
            P.add("dve", I("tensor_copy", carry.a[:, ct0:ct0 + 4, 0:nseg, :], xv[:, :, :, sl:sl + 3]), xpb_t, [carry])
            cv = cacc.a[:, :, 0:TS].rearrange("p c (s t) -> p c s t", s=nseg)
            cw = pc("convw").rearrange("p (c j) -> p c j", j=4)
            for ct in range(4):
                gct = ct0 + ct
                P.add("dve", I("tensor_scalar", out=cv[:, ct], in0=xv[:, ct, :, 0:sl], scalar1=cw[:, gct, 0:1], scalar2=pc("convb")[:, gct:gct + 1], op0=ALU.mult, op1=ALU.add), [xpb_t[ct], prm], [cacc_t[ct]])
            for j in range(1, 4):
                for ct in range(4):
                    gct = ct0 + ct
                    P.add("dve", I("scalar_tensor_tensor", out=cv[:, ct], in0=xv[:, ct, :, j:j + sl], scalar=cw[:, gct, j:j + 1], in1=cv[:, ct], op0=ALU.mult, op1=ALU.add), [xpb_t[ct], prm, cacc_t[ct]], [cacc_t[ct]])
            P.add("act", I("activation", out=dst.a[:, :, 0:TS], in_=cacc.a[:, :, 0:TS], func=AF.Silu), cacc_t, [dst])

        for s in range(G.nsuper):
            tok0 = s * TS
            gt = [s * NTT + tt for tt in range(NTT)]
            if s == 0:
                prenorm_A(G, lambda tt: xsrc_d[tok0 + tt * NT: tok0 + (tt + 1) * NT, :], lambda tt: xsrc_t(gt[tt]), NTT)
            prenorm_B(G, pc("wpre"), NTT)
            if CUT < 1:
                continue
            pss = fm_chunk(G, ws, wv_in, C_B)
            conv_chunk(pss, 16, BT)
            pss = fm_chunk(G, ws, wv_in, C_C)
            conv_chunk(pss, 20, CT)
            pss = tm_chunk(G, ws, wv_in, C_DT, 32)
            for tt in range(NTT):
                P.add("dve", I("tensor_tensor", out=sptmp.a[:NT, :], in0=pss[tt].a[:NT, 0:32], in1=pc("dtb")[:NT, :], op=ALU.add), [pss[tt], prm], [sptmp])
                P.add("act", I("activation", out=sptmp.a[:NT, :], in_=sptmp.a[:NT, :], func=AF.Exp), [sptmp], [sptmp])
                P.add("act", I("activation", out=dtt[tt].a[:NT, :], in_=sptmp.a[:NT, :], func=AF.Ln, bias=1.0), [sptmp], [dtt[tt]])
                P.add("dve", I("tensor_tensor", out=dat[tt].a[:NT, :], in0=dtt[tt].a[:NT, :], in1=abc.a[:NT, :], op=ALU.mult), [dtt[tt], abc], [dat[tt]])
                pt = tmp[nxt("tmp", 2)]
                P.add("pe", seq([
                    I("matmul", pt.a[:NT, 0:32], MLE[:NT, :NT], dat[tt].a[:NT, :], start=True, stop=True),
                    I("matmul", pt.a[:NT, 32:64], MGT[:NT, :NT], dat[tt].a[:NT, :], start=True, stop=True),
                    I("matmul", pt.a[:, 64:96], ones[:NT, :], dat[tt].a[:NT, :], start=True, stop=True)]), [cst, dat[tt]], [pt])
                P.add("act", I("activation", out=ecum[tt].a[:NT, :], in_=pt.a[:NT, 0:32], func=AF.Exp), [pt], [ecum[tt]])
                P.add("act", I("activation", out=wst[tt].a[:NT, :], in_=pt.a[:NT, 32:64], func=AF.Exp), [pt], [wst[tt]])
                P.add("act", I("activation", out=cdec[tt].a[:, :], in_=pt.a[:, 64:96], func=AF.Exp), [pt], [cdec[tt]])
                P.add("dve", I("tensor_tensor", out=wst[tt].a[:NT, :], in0=wst[tt].a[:NT, :], in1=dtt[tt].a[:NT, :], op=ALU.mult), [wst[tt], dtt[tt]], [wst[tt]])
                tb = trb[nxt("trb", 2)]
                P.add("pe", seq([I("transpose", out=tb.a[:NT, g * 128:(g + 1) * 128], in_=BT.a[:, g, tt * NT:(tt + 1) * NT], identity=identb.a[:]) for g in range(4)]), [BT, identb], [tb])
                P.add("act", I("copy", out=Btm[tt].a[:NT, :], in_=tb.a[:NT, 0:512]), [tb], [Btm[tt]])
            if CUT < 2:
                continue
            for g in range(4):
                pss = fm_chunk(G, ws, wv_in, C_Z + g * 512)
                for ct in range(4):
                    pt, off = pss[ct]
                    P.add("act", I("activation", out=yT.a[:, 4 * g + ct, 0:TS], in_=pt.a[:, off:off + TS], func=AF.Silu), [pt], [yT])
                pss = fm_chunk(G, ws, wv_in, C_X + g * 512)
                conv_chunk(pss, 4 * g, xsT)
                for tt in range(NTT):
                    tsl = slice(tt * NT, (tt + 1) * NT)
                    if G.sample:
                        P.dma("sp", Hs[g].a[:], sssm_d[l, tt, :, g * 512:(g + 1) * 512], writes=[Hs[g]])
                        P.add("act", I("copy", out=Hb[g].a[:], in_=Hs[g].a[:]), [Hs[g]], [Hb[g]])
                    tb = trb[nxt("trb", 2)]
                    P.add("pe", seq([I("transpose", out=tb.a[:NT, c * 128:(c + 1) * 128], in_=xsT.a[:, c, tsl], identity=identb.a[:]) for c in range(4)]), [xsT, identb], [tb])
                    P.add("act", I("copy", out=xtm.a[:NT, :], in_=tb.a[:NT, 0:512]), [tb], [xtm])
                    hs8 = slice(8 * g, 8 * g + 8)
                    x3 = xtm.a[:NT, :].rearrange("p (h d) -> p h d", d=64)
                    P.add("dve", I("tensor_tensor", out=xdt.a[:NT, :].rearrange("p (h d) -> p h d", d=64), in0=x3, in1=dtt[tt].a[:NT, hs8].unsqueeze(2).to_broadcast([NT, 8, 64]), op=ALU.mult), [xtm, dtt[tt]], [xdt])
                    P.add("dve", I("tensor_tensor", out=xw.a[:NT, :].rearrange("p (h d) -> p h d", d=64), in0=x3, in1=wst[tt].a[:NT, hs8].unsqueeze(2).to_broadcast([NT, 8, 64]), op=ALU.mult), [xtm, wst[tt]], [xw])
                    P.add("dve", I("tensor_tensor", out=xD.a[:NT, :].rearrange("p (h d) -> p h d", d=64), in0=x3, in1=pc("dsk")[:NT, hs8].unsqueeze(2).to_broadcast([NT, 8, 64]), op=ALU.mult), [xtm, prm], [xD])
                    P.add("dve", I("tensor_tensor", out=Rt.a[:NT, :, :NT], in0=dat[tt].a[:NT, hs8].unsqueeze(2).to_broadcast([NT, 8, NT]), in1=MLE[:NT, :NT].unsqueeze(1).to_broadcast([NT, 8, NT]), op=ALU.mult), [dat[tt], cst], [Rt])
                    sg = [tmp[0], tmp[1]]
                    P.add("pe", seq([I("matmul", sg[hh // 4].a[:NT, (hh % 4) * 128:(hh % 4) * 128 + NT], MGT[:NT, :NT], Rt.a[:NT, hh, :NT], start=True, stop=True) for hh in range(8)]), [cst, Rt], sg)
                    for q in range(2):
                        P.add("act", I("activation", out=Et.a[:NT, 4 * q:4 * q + 4, :NT], in_=sg[q].a[:NT, :].rearrange("p (h t) -> p h t", t=128)[:, :, :NT], func=AF.Exp), [sg[q]], [Et])
                    pcb = acc[2 * rr["acc"] + 0]
                    pa = [acc[(2 * rr["acc"] + 2) % 4], acc[(2 * rr["acc"] + 3) % 4]]
                    P.add("pe", I("matmul", pcb.a[:NT, 0:NT], BT.a[:, g, tsl], CT.a[:, g, tsl], start=True, stop=True), [BT, CT], [pcb])
                    P.add("dve", I("tensor_tensor", out=cbm.a[:NT, :NT], in0=pcb.a[:NT, 0:NT], in1=MLE[:NT, :NT], op=ALU.mult), [pcb, cst], [cbm])
                    P.add("dve", I("tensor_tensor", out=MT.a[:NT, :, :NT], in0=Et.a[:NT, :, :NT], in1=cbm.a[:NT, :NT].unsqueeze(1).to_broadcast([NT, 8, NT]), op=ALU.mult), [Et, cbm], [MT])
                    P.add("pe", seq([I("matmul", pa[0].a[:NT, hh * 64:(hh + 1) * 64], MT.a[:NT, hh, :NT], xdt.a[:NT, hh * 64:(hh + 1) * 64], start=True, stop=True) for hh in range(8)]), [MT, xdt], [pa[0]])
                    P.add("pe", I("matmul", pa[1].a[:NT, :], CT.a[:, g, tsl], Hb[g].a[:], start=True, stop=True), [CT, Hb[g]], [pa[1]])
                    P.add("dve", I("tensor_tensor", out=t1.a[:NT, :].rearrange("p (h d) -> p h d", d=64), in0=pa[1].a[:NT, :].rearrange("p (h d) -> p h d", d=64), in1=ecum[tt].a[:NT, hs8].unsqueeze(2).to_broadcast([NT, 8, 64]), op=ALU.mult), [pa[1], ecum[tt]], [t1])
                    P.add("dve", I("tensor_tensor", out=t3.a[:NT, :], in0=t1.a[:NT, :], in1=xD.a[:NT, :], op=ALU.add), [t1, xD], [t3])
                    P.add("dve", I("tensor_tensor", out=ytm.a[:NT, :], in0=t3.a[:NT, :], in1=pa[0].a[:NT, :], op=ALU.add), [t3, pa[0]], [ytm])
                    pS = tmp[0]
                    P.add("pe", I("matmul", pS.a[:, :], Btm[tt].a[:NT, g * 128:(g + 1) * 128], xw.a[:NT, :], start=True, stop=True), [Btm[tt], xw], [pS])
                    P.add("dve", I("tensor_tensor", out=Hs[g].a[:].rearrange("p (h d) -> p h d", d=64), in0=Hs[g].a[:].rearrange("p (h d) -> p h d", d=64), in1=cdec[tt].a[:, hs8].unsqueeze(2).to_broadcast([128, 8, 64]), op=ALU.mult), [Hs[g], cdec[tt]], [Hs[g]])
                    P.add("dve", I("tensor_tensor", out=Hs[g].a[:], in0=Hs[g].a[:], in1=pS.a[:, :], op=ALU.add), [Hs[g], pS], [Hs[g]])
                    P.add("act", I("copy", out=Hb[g].a[:], in_=Hs[g].a[:]), [Hs[g]], [Hb[g]])
                    if G.sample:
                        P.dma("sp", ossms_d[l, tt, :, g * 512:(g + 1) * 512], Hs[g].a[:], reads=[Hs[g]])
                    elif s == G.nsuper - 1 and tt == NTT - 1:
                        P.dma("sp", ossmp_d[l, :, g * 512:(g + 1) * 512], Hs[g].a[:], reads=[Hs[g]])
                    tb = trb[nxt("trb", 2)]
                    P.add("pe", seq([I("transpose", out=tb.a[:, c * 128:c * 128 + NT], in_=ytm.a[:NT, c * 128:(c + 1) * 128], identity=identb.a[:NT, :NT]) for c in range(4)]), [ytm, identb], [tb])
                    P.add("dve", I("tensor_tensor", out=yT.a[:, 4 * g:4 * g + 4, tsl], in0=yT.a[:, 4 * g:4 * g + 4, tsl], in1=tb.a[:, 0:512].rearrange("p (c t) -> p c t", t=128)[:, :, :NT], op=ALU.mult), [yT, tb], [yT])
                pn = tmp[1]
                for c in range(4):
                    P.add("dve", I("tensor_tensor", out=sqt.a[:, 0:TS], in0=yT.a[:, 4 * g + c, 0:TS], in1=yT.a[:, 4 * g + c, 0:TS], op=ALU.mult), [yT], [sqt])
                    P.add("pe", I("matmul", pn.a[:, 0:TS], onesb.a[:], sqt.a[:, 0:TS], start=(c == 0), stop=(c == 3)), [onesb, sqt], [pn])
                P.add("dve", I("tensor_scalar", out=nrm.a[:, 0:TS], in0=pn.a[:, 0:TS], scalar1=1.0 / 512, scalar2=1e-6, op0=ALU.mult, op1=ALU.add), [pn], [nrm])
                P.add("act", I("activation", out=nrm.a[:, 0:TS], in_=nrm.a[:, 0:TS], func=AF.Sqrt), [nrm], [nrm])
                P.add("dve", I("reciprocal", out=nrm.a[:, 0:TS], in_=nrm.a[:, 0:TS]), [nrm], [nrm])
                for c in range(4):
                    P.add("dve", I("scalar_tensor_tensor", out=yT.a[:, 4 * g + c, 0:TS], in0=yT.a[:, 4 * g + c, 0:TS], scalar=pc("ssmw")[:, 4 * g + c:4 * g + c + 1], in1=nrm.a[:, 0:TS], op0=ALU.mult, op1=ALU.mult), [yT, prm, nrm], [yT])
            if CUT < 3:
                continue
            if G.sample:
                P.dma("sp", oconvs_d[l], carry.a[:], reads=[carry])
            elif s == G.nsuper - 1:
                P.dma("sp", oconvp_d[l], carry.a[:, :, 0, :], reads=[carry])

            for c in range(2):
                pss = fm_chunk(G, ws, wv_in, C_Q + c * 512)
                for ct in range(4):
                    pt, off = pss[ct]
                    P.add("act", I("copy", out=QT.a[:, 4 * c + ct, 0:TS], in_=pt.a[:, off:off + TS]), [pt], [QT])
            kdst = sk_d if G.sample else pk_d
            vdst = sv_d if G.sample else pv_d
            for c in range(2):
                pss = tm_chunk(G, ws, wv_in, C_K + c * 512)
                for tt in range(NTT):
                    k32 = kf[nxt("kf", 2)]
                    P.add("act", I("copy", out=k32.a[:NT, :], in_=pss[tt].a[:NT, :]), [pss[tt]], [k32])
                    P.dma("sp", kdst[l, tok0 + tt * NT: tok0 + (tt + 1) * NT, c * 512:(c + 1) * 512], k32.a[:NT, :], reads=[k32])
                    P.add("dve", I("tensor_copy", kbf.a[:NT, :], pss[tt].a[:NT, :]), [pss[tt]], [kbf])
                    tb = trb[nxt("trb", 2)]
                    P.add("pe", seq([I("transpose", out=tb.a[:, j * 128:j * 128 + NT], in_=kbf.a[:NT, j * 128:(j + 1) * 128], identity=identb.a[:NT, :NT]) for j in range(4)]), [kbf, identb], [tb])
                    P.add("act", I("copy", out=KTn.a[:, 4 * c:4 * c + 4, tt * NT:(tt + 1) * NT], in_=tb.a[:, 0:512].rearrange("p (j t) -> p j t", t=128)[:, :, :NT]), [tb], [KTn])
            if not G.sample:
                for tt in range(NTT):
                    P.dma("sp", kts_d[l, :, :, tok0 + tt * NT: tok0 + (tt + 1) * NT].rearrange("j p t -> p j t"), KTn.a[:, :, tt * NT:(tt + 1) * NT], reads=[KTn], writes=[kts_t[l][gt[tt]]])
            for c in range(2):
                pss = tm_chunk(G, ws, wv_in, C_V + c * 512)
                for tt in range(NTT):
                    k32 = kf[nxt("kf", 2)]
                    P.add("act", I("copy", out=k32.a[:NT, :], in_=pss[tt].a[:NT, :]), [pss[tt]], [k32])
                    P.dma("sp", vdst[l, tok0 + tt * NT: tok0 + (tt + 1) * NT, c * 512:(c + 1) * 512], k32.a[:NT, :], reads=[k32])
                    P.add("dve", I("tensor_copy", VAn[tt].a[:NT, 2 * c:2 * c + 2, 0:256], pss[tt].a[:NT, :].rearrange("p (h e) -> p h e", e=256)), [pss[tt]], [VAn[tt]])
            if not G.sample:
                for tt in range(NTT):
                    P.dma("sp", vs_d[l, tok0 + tt * NT: tok0 + (tt + 1) * NT, :, :], VAn[tt].a[:NT, :, 0:256], reads=[VAn[tt]], writes=[vs_t[l][gt[tt]]])
            for c in range(2):
                pss = fm_chunk(G, ws, wv_in, C_G + c * 512)
                for ct in range(4):
                    pt, off = pss[ct]
                    P.add("act", I("activation", out=yT.a[:, 16 + 4 * c + ct, 0:TS], in_=pt.a[:, off:off + TS], func=AF.Silu), [pt], [yT])

            if CUT < 4:
                continue
            sc_d = 128 ** -0.5
            for tt in range(NTT):
                qsl = slice(tt * NT, (tt + 1) * NT)
                for h in range(4):
                    if G.sample:
                        pieces = [(0, 8)]
                    else:
                        nh = gt[tt]
                        pieces = [(a, min(8, nh - a)) for a in range(0, nh, 8)]
                    O = [acc[0], acc[1]] if (rr["acc"] == 0) else [acc[2], acc[3]]
                    nxt("acc", 2)
                    kcount = 0
                    pend = []
                    for (k0, nkt) in pieces:
                        bi = nxt("pt", 2)
                        kb = KTb[bi]
                        va = VA[bi]
                        if G.sample:
                            for m in range(2):
                                P.dma("pool", kb[m].a[:, :], ckT_d[l, tt, 2 * h + m], writes=[kb[m]])
                            P.dma("pool", va.a[:, :, 0:256], cv_d[l, tt, :, h, :].rearrange("(kt p) e -> p kt e", p=128), writes=[va])
                        else:
                            rd_k = [kts_t[l][k0 + i] for i in range(nkt)]
                            rd_v = [vs_t[l][k0 + i] for i in range(nkt)]
                            for m in range(2):
                                P.dma("sp", kb[m].a[:, 0:nkt * 128], kts_d[l, 2 * h + m, :, k0 * 128:(k0 + nkt) * 128], reads=rd_k, writes=[kb[m]])
                            P.dma("sp", va.a[:, 0:nkt, 0:256], vs_d[l, k0 * 128:(k0 + nkt) * 128, h, :].rearrange("(kt p) e -> p kt e", p=128), reads=rd_v, writes=[va])
                        for kt0 in range(0, nkt, 2):
                            nb = min(2, nkt - kt0)
                            st = tmp[nxt("tmp", 2)]
                            p4 = PT4[nxt("p4", 2)]
                            th = []
                            for j in range(nb):
                                for m in range(2):
                                    b_ = j * 2 + m
                                    th.append(I("matmul", st.a[:, b_ * 128:b_ * 128 + NT], kb[m].a[:, (kt0 + j) * 128:(kt0 + j + 1) * 128], QT.a[:, 2 * h + m, qsl], start=True, stop=True))
                            P.add("pe", seq(th), [kb[0], kb[1], QT], [st])
                            vin = st.a[:, 0:nb * 256].rearrange("p (b t) -> p b t", t=128)[:, :, :NT]
                            vout = p4.a[:, 0:nb * 256].rearrange("p (b t) -> p b t", t=128)[:, :, :NT]
                            P.add("act", I("activation", out=vout, in_=vin, func=AF.Exp, scale=sc_d), [st], [p4])
                            while pend:
                                pend.pop(0)()

                            def mk(p4=p4, va=va, kt0=kt0, nb=nb, first=(kcount == 0), O=O):
                                def f():
                                    for m in range(2):
                                        th2 = [I("matmul", O[m].a[:NT, 0:257], p4.a[:, (j * 2 + m) * 128:(j * 2 + m) * 128 + NT], va.a[:, kt0 + j, :], start=(first and j == 0), stop=False) for j in range(nb)]
                                        P.add("pe", seq(th2), [p4, va], [O[m]])
                                return f
                            pend.append(mk())
                            kcount += nb
                    while pend:
                        pend.pop(0)()
                    for m in range(2):
                        st = tmp[nxt("tmp", 2)]
                        P.add("pe", I("matmul", st.a[:NT, 0:NT], KTn.a[:, 2 * h + m, qsl], QT.a[:, 2 * h + m, qsl], start=True, stop=True), [KTn, QT], [st])
                        pt_ = PT[2 * m + (kcount % 2)]
                        P.add("act", I("activation", out=pt_.a[:NT, 0:NT], in_=st.a[:NT, 0:NT], func=AF.Exp, scale=sc_d), [st], [pt_])
                        if not G.sample:
                            P.add("dve", I("memset", pt_.a[64:128, 0:64], 0.0), [], [pt_])
                        P.add("pe", I("matmul", O[m].a[:NT, 0:257], pt_.a[:NT, 0:NT], VAn[tt].a[:NT, h, :], start=(kcount == 0), stop=True), [pt_, VAn[tt]], [O[m]])
                    for m in range(2):
                        P.add("act", I("copy", out=Osb[m].a[:NT, :], in_=O[m].a[:NT, 0:257]), [O[m]], [Osb[m]])
                        P.add("dve", I("reciprocal", out=rcp.a[:NT, m:m + 1], in_=Osb[m].a[:NT, 256:257]), [Osb[m]], [rcp])
                    P.add("dve", I("tensor_tensor", out=rcp.a[:NT, 1:2], in0=rcp.a[:NT, 1:2], in1=neglam.a[:NT, 0:1], op=ALU.mult), [rcp, neglam], [rcp])
                    P.add("dve", I("tensor_scalar_mul", out=od.a[:NT, :], in0=Osb[0].a[:NT, 0:256], scalar1=rcp.a[:NT, 0:1]), [Osb[0], rcp], [od])
                    P.add("dve", I("scalar_tensor_tensor", out=od.a[:NT, :], in0=Osb[1].a[:NT, 0:256], scalar=rcp.a[:NT, 1:2], in1=od.a[:NT, :], op0=ALU.mult, op1=ALU.add), [Osb[1], rcp, od], [od])
                    P.add("act", I("activation", out=od2.a[:NT, :], in_=od.a[:NT, :], func=AF.Square, accum_out=rcp.a[:NT, 2:3]), [od], [od2, rcp])
                    P.add("dve", I("tensor_scalar", out=rcp.a[:NT, 2:3], in0=rcp.a[:NT, 2:3], scalar1=1.0 / 256, scalar2=1e-5, op0=ALU.mult, op1=ALU.add), [rcp], [rcp])
                    P.add("act", I("activation", out=rcp.a[:NT, 2:3], in_=rcp.a[:NT, 2:3], func=AF.Sqrt), [rcp], [rcp])
                    P.add("dve", I("reciprocal", out=rcp.a[:NT, 3:4], in_=rcp.a[:NT, 2:3]), [rcp], [rcp])
                    P.add("dve", I("tensor_scalar", out=od2.a[:NT, :], in0=od.a[:NT, :], scalar1=rcp.a[:NT, 3:4], scalar2=1.0 - LAM_INIT[l], op0=ALU.mult, op1=ALU.mult), [od, rcp], [od2])
                    P.add("dve", I("tensor_tensor", out=ydtm.a[:NT, :], in0=od2.a[:NT, :], in1=pc("subln")[:NT, :], op=ALU.mult), [od2, prm], [ydtm])
                    tb = trb[nxt("trb", 2)]
                    P.add("pe", seq([I("transpose", out=tb.a[:, c * 128:c * 128 + NT], in_=ydtm.a[:NT, c * 128:(c + 1) * 128], identity=identb.a[:NT, :NT]) for c in range(2)]), [ydtm, identb], [tb])
                    P.add("dve", I("tensor_tensor", out=yT.a[:, 16 + 2 * h:16 + 2 * h + 2, qsl], in0=yT.a[:, 16 + 2 * h:16 + 2 * h + 2, qsl], in1=tb.a[:, 0:256].rearrange("p (c t) -> p c t", t=128)[:, :, :NT], op=ALU.mult), [yT, tb], [yT])

            if CUT < 5:
                continue
            for c in range(2):
                pss = fm_chunk(G, ws, wv_in, C_QM + c * 512)
                for ct in range(4):
                    pt, off = pss[ct]
                    P.add("act", I("copy", out=QmT.a[:, 4 * c + ct, 0:TS], in_=pt.a[:, off:off + TS]), [pt], [QmT])
            for c in range(2):
                pss = fm_chunk(G, ws, wv_in, C_GM + c * 512)
                for ct in range(4):
                    pt, off = pss[ct]
                    P.add("act", I("activation", out=yT.a[:, 24 + 4 * c + ct, 0:TS], in_=pt.a[:, off:off + TS], func=AF.Silu), [pt], [yT])
            sc_m = 256 ** -0.5
            for tt in range(NTT):
                qsl = slice(tt * NT, (tt + 1) * NT)
                if G.sample:
                    P.dma("pool", MKT.a[:], cmkT_d[l, tt].rearrange("j p t -> p j t"), writes=[MKT])
                    for j in range(2):
                        P.dma("pool", MVA[j].a[:, :, 0:256], cmv_d[l, tt, j * 128:(j + 1) * 128, :, :], writes=[MVA[j]])
                for h in range(4):
                    Om = acc[2 * nxt("acc", 2)]
                    for j in range(2):
                        st = tmp[nxt("tmp", 2)]
                        P.add("pe", seq([I("matmul", st.a[:, 0:NT], MKT.a[:, 2 * h + dt_, j * 128:(j + 1) * 128], QmT.a[:, 2 * h + dt_, qsl], start=(dt_ == 0), stop=(dt_ == 1)) for dt_ in range(2)]), [MKT, QmT], [st])
                        pt_ = PT[nxt("pt", 2)]
                        P.add("act", I("activation", out=pt_.a[:, 0:NT], in_=st.a[:, 0:NT], func=AF.Exp, scale=sc_m), [st], [pt_])
                        P.add("pe", I("matmul", Om.a[:NT, 0:257], pt_.a[:, 0:NT], MVA[j].a[:, h, :], start=(j == 0), stop=(j == 1)), [pt_, MVA[j]], [Om])
                    P.add("act", I("copy", out=Osb[0].a[:NT, :], in_=Om.a[:NT, 0:257]), [Om], [Osb[0]])
                    P.add("dve", I("reciprocal", out=rcp.a[:NT, 0:1], in_=Osb[0].a[:NT, 256:257]), [Osb[0]], [rcp])
                    P.add("dve", I("tensor_scalar_mul", out=ydtm.a[:NT, :], in0=Osb[0].a[:NT, 0:256], scalar1=rcp.a[:NT, 0:1]), [Osb[0], rcp], [ydtm])
                    tb = trb[nxt("trb", 2)]
                    P.add("pe", seq([I("transpose", out=tb.a[:, c * 128:c * 128 + NT], in_=ydtm.a[:NT, c * 128:(c + 1) * 128], identity=identb.a[:NT, :NT]) for c in range(2)]), [ydtm, identb], [tb])
                    P.add("dve", I("tensor_tensor", out=yT.a[:, 24 + 2 * h:24 + 2 * h + 2, qsl], in0=yT.a[:, 24 + 2 * h:24 + 2 * h + 2, qsl], in1=tb.a[:, 0:256].rearrange("p (c t) -> p c t", t=128)[:, :, :NT], op=ALU.mult), [yT, tb], [yT])

            if CUT < 6:
                continue
            if s + 1 < G.nsuper:
                prenorm_A(G, (lambda tt, t1_=tok0 + TS: xsrc_d[t1_ + tt * NT: t1_ + (tt + 1) * NT, :]), (lambda tt, g1_=(s + 1) * NTT: xsrc_t(g1_ + tt)), NTT)
            for c in range(8):
                pss = tm_chunk(G, ws, wv_out, c * 512, 512, src=yT)
                for tt in range(NTT):
                    P.add("act", I("activation", out=junkb.a[:NT, 0:512], in_=pss[tt].a[:NT, :], func=AF.Square, accum_out=ssq.a[:NT, c:c + 1]), [pss[tt]], [junkb, ssq]) if tt == 0 else \
                        P.add("act", I("activation", out=junkb.a[:NT, 512:1024], in_=pss[tt].a[:NT, :], func=AF.Square, accum_out=nrm.a[:NT, c:c + 1]), [pss[tt]], [junkb, nrm])
                    P.add("dve", I("tensor_copy", outb[tt].a[:NT, c * 512:(c + 1) * 512], pss[tt].a[:NT, :]), [pss[tt]], [outb[tt]])
            for tt in range(NTT):
                sq_src = ssq if tt == 0 else nrm
                P.add("dve", I("reduce_sum", out=rstd.a[:NT, 0:1], in_=sq_src.a[:NT, 0:8], axis=AX.X), [sq_src], [rstd])
                P.add("dve", I("tensor_scalar", out=rstd.a[:NT, 0:1], in0=rstd.a[:NT, 0:1], scalar1=1.0 / D, scalar2=1e-6, op0=ALU.mult, op1=ALU.add), [rstd], [rstd])
                P.add("act", I("activation", out=rstd.a[:NT, 0:1], in_=rstd.a[:NT, 0:1], func=AF.Sqrt), [rstd], [rstd])
                P.add("dve", I("reciprocal", out=rstd.a[:NT, 0:1], in_=rstd.a[:NT, 0:1]), [rstd], [rstd])
                r0 = tok0 + tt * NT
                for cb in range(4):
                    xb = xblk[cb % 2]
                    csl = slice(cb * 1024, (cb + 1) * 1024)
                    P.dma("sp", wpostb.a[:, :], wpost_d[l, :, csl], writes=[wpostb])
                    P.dma("sp", xb.a[:NT, :], xsrc_d[r0:r0 + NT, csl], reads=xsrc_t(gt[tt]), writes=[xb])
                    P.add("dve", I("scalar_tensor_tensor", out=t1.a[:NT, :], in0=outb[tt].a[:NT, cb * 1024:cb * 1024 + 512], scalar=rstd.a[:NT, 0:1], in1=wpostb.a[:NT, 0:512], op0=ALU.mult, op1=ALU.mult), [outb[tt], rstd, wpostb], [t1])
                    P.add("dve", I("scalar_tensor_tensor", out=t3.a[:NT, :], in0=outb[tt].a[:NT, cb * 1024 + 512:cb * 1024 + 1024], scalar=rstd.a[:NT, 0:1], in1=wpostb.a[:NT, 512:1024], op0=ALU.mult, op1=ALU.mult), [outb[tt], rstd, wpostb], [t3])
                    P.add("dve", I("tensor_tensor", out=xb.a[:NT, 0:512], in0=xb.a[:NT, 0:512], in1=t1.a[:NT, :], op=ALU.add), [xb, t1], [xb])
                    P.add("dve", I("tensor_tensor", out=xb.a[:NT, 512:1024], in0=xb.a[:NT, 512:1024], in1=t3.a[:NT, :], op=ALU.add), [xb, t3], [xb])
                    P.dma("sp", xdst_d[r0:r0 + NT, csl], xb.a[:NT, :], reads=[xb], writes=xdst_t(gt[tt]))

    Gp = Group("p", 128, 2, NSUP)
    Gs = Group("s", 16, 2, 1)
    Gm = Group("m", 128, 2, 1)
    def program():
        for l in LAYERS:
            P.dma("sp", prm.a[:], prm_d[l], writes=[prm])
            P.add("act", I("activation", out=abc.a[:], in_=pc("alog"), func=AF.Exp), [prm], [abc])
            P.add("dve", I("tensor_scalar_mul", out=abc.a[:], in0=abc.a[:], scalar1=-1.0), [abc], [abc])
            for i, (a, b) in enumerate((("lq1", "lk1"), ("lq2", "lk2"))):
                P.add("dve", I("tensor_tensor", out=lamtmp.a[:], in0=pc(a), in1=pc(b), op=ALU.mult), [prm], [lamtmp])
                P.add("dve", I("reduce_sum", out=lams.a[:, i:i + 1], in_=lamtmp.a[:], axis=AX.X), [lamtmp], [lams])
            P.add("act", I("activation", out=lams.a[:, 2:4], in_=lams.a[:, 0:2], func=AF.Exp), [lams], [lams])
            P.add("dve", I("tensor_tensor", out=neglam.a[:], in0=lams.a[:, 3:4], in1=lams.a[:, 2:3], op=ALU.subtract), [lams], [neglam])
            P.add("dve", I("tensor_scalar_add", out=neglam.a[:], in0=neglam.a[:], scalar1=-LAM_INIT[l]), [neglam], [neglam])
            prenorm_A(Gm, lambda tt: mem_d[tt * 128:(tt + 1) * 128, :], lambda tt: [], 2)
            prenorm_B(Gm, pc("memw"), 2)
            wv_kv = wkv_d[l].rearrange("(kt p) c -> p kt c", p=128)
            WVN[id(wv_kv)] = f"kv{l}"
            for c in range(4):
                pss = tm_chunk(Gm, WS, wv_kv, c * 512)
                for tt in range(2):
                    k32 = kf[nxt("kf", 2)]
                    P.add("act", I("copy", out=k32.a[:, :], in_=pss[tt].a[:, :]), [pss[tt]], [k32])
                    dst = pmk_d if c < 2 else pmv_d
                    P.dma("sp", dst[l, tt * 128:(tt + 1) * 128, (c % 2) * 512:(c % 2 + 1) * 512], k32.a[:, :], reads=[k32])
                    if c < 2:
                        P.add("dve", I("tensor_copy", kbf.a[:, :], pss[tt].a[:, :]), [pss[tt]], [kbf])
                        tb = trb[nxt("trb", 2)]
                        P.add("pe", seq([I("transpose", out=tb.a[:, j * 128:(j + 1) * 128], in_=kbf.a[:, j * 128:(j + 1) * 128], identity=identb.a[:]) for j in range(4)]), [kbf, identb], [tb])
                        P.add("act", I("copy", out=MKT.a[:, 4 * c:4 * c + 4, tt * 128:(tt + 1) * 128], in_=tb.a[:, 0:512].rearrange("p (j t) -> p j t", t=128)), [tb], [MKT])
                    else:
                        cc = c - 2
                        P.add("dve", I("tensor_copy", MVA[tt].a[:, 2 * cc:2 * cc + 2, 0:256], pss[tt].a[:, :].rearrange("p (h e) -> p h e", e=256)), [pss[tt]], [MVA[tt]])
            last = (l == LAYERS[-1])
            if DO_PROMPT:
                run_layer(l, Gp, xp_d if l == 0 else x1p_d, (lambda i: []) if l == 0 else (lambda i: [x1p_t[i]]),
                          yp_d if last else x1p_d, (lambda i: []) if last else (lambda i: [x1p_t[i]]), WS)
            if DO_SAMPLE:
                run_layer(l, Gs, xs_d if l == 0 else x1s_d, (lambda i: []) if l == 0 else (lambda i: [x1s_t[i]]),
                          ys_d if last else x1s_d, (lambda i: []) if last else (lambda i: [x1s_t[i]]), WS)

    P.maxops = cfg.get('maxops', 10 ** 9)
    WS = WStream()
    P.dry = True
    rr0 = dict(rr)
    program()
    P.dry = False
    rr.update(rr0)
    program()

    print('nops', len(P.ops), 'sbuf_free', nc.sbuf_bytes_remaining, flush=True)
    if cfg.get('dump'):
        for op in P.ops:
            print(op.idx, op.eng, 'dma' if op.dma else '', [d.idx for d in op.deps])
    P.emit(nc)
    return nc


def _fm(v, ntile):
    return np.ascontiguousarray(v.reshape(ntile, 128).T)


def kernel(x_prompt, x_sample, mem_prompt, cache_conv, state_ssm, cache_k, cache_v, cache_mem_k, cache_mem_v,
           norm_pre_w, norm_post_w, w_in, conv_w, conv_b, dt_bias, a_log, d_skip, ssm_norm_w,
           lambda_q1, lambda_k1, lambda_q2, lambda_k2, subln_w, mem_norm_w, w_mem_kv, w_out):
    f = np.float32
    A = lambda a: np.ascontiguousarray(np.asarray(a, dtype=f))
    x_prompt, x_sample, mem_prompt = A(x_prompt), A(x_sample), A(mem_prompt)
    cache_conv, state_ssm, cache_k, cache_v = A(cache_conv), A(state_ssm), A(cache_k), A(cache_v)
    cache_mem_k, cache_mem_v = A(cache_mem_k), A(cache_mem_v)
    w_in, w_out, w_mem_kv = A(w_in), A(w_out), A(w_mem_kv)
    prm = np.zeros((2, 128, NPRM), f)
    bc = lambda v: np.broadcast_to(np.asarray(v, f)[None, :], (128, len(v)))
    for l in range(2):
        def put(n, arr):
            a, b = PO[n]
            prm[l, :, a:b] = arr
        put("wpre", _fm(np.asarray(norm_pre_w[l], f), 32))
        put("memw", _fm(np.asarray(mem_norm_w[l], f), 32))
        cw = np.asarray(conv_w[l], f)
        put("convw", cw.reshape(4, 24, 128).transpose(2, 1, 0).reshape(128, 96))
        put("convb", _fm(np.asarray(conv_b[l], f), 24))
        put("dtb", bc(dt_bias[l]))
        put("alog", bc(a_log[l]))
        put("dsk", bc(d_skip[l]))
        put("ssmw", _fm(np.asarray(ssm_norm_w[l], f), 16))
        put("lq1", bc(lambda_q1[l])); put("lk1", bc(lambda_k1[l]))
        put("lq2", bc(lambda_q2[l])); put("lk2", bc(lambda_k2[l]))
        put("subln", bc(subln_w[l]))
    wpost = np.ascontiguousarray(np.broadcast_to(np.asarray(norm_post_w, f)[:, None, :], (2, 128, D)))
    cst = np.zeros((128, 4, 128), f)
    ii = np.arange(128)
    cst[:, 0, :] = np.eye(128)
    cst[:, 1, :] = (ii[:, None] <= ii[None, :])
    cst[:, 2, :] = (ii[:, None] > ii[None, :])
    cst[:, 3, :] = 1.0
    in_maps = []
    for c in range(NCORE):
        b = c % 4
        ss = slice(2 * c, 2 * c + 2)
        cc = cache_conv[:, ss]
        cconv = np.ascontiguousarray(cc.reshape(2, 2, 3, 24, 128).transpose(0, 4, 3, 1, 2))
        sssm = np.ascontiguousarray(state_ssm[:, ss].reshape(2, 2, 2048, 128).transpose(0, 1, 3, 2))
        ckT = np.ascontiguousarray(cache_k[:, ss].reshape(2, 2, 1024, 8, 128).transpose(0, 1, 3, 4, 2))
        cv = np.ascontiguousarray(cache_v[:, ss])
        cmkT = np.ascontiguousarray(cache_mem_k[:, ss].reshape(2, 2, 256, 8, 128).transpose(0, 1, 3, 4, 2))
        cmv = np.ascontiguousarray(cache_mem_v[:, ss])
        in_maps.append({
            "xp": x_prompt[b], "xs": np.ascontiguousarray(x_sample[ss].reshape(32, D)), "mem": mem_prompt[b],
            "cconv": cconv, "sssm": sssm, "ckT": ckT, "cv": cv, "cmkT": cmkT, "cmv": cmv,
            "w_in": w_in, "w_out": w_out, "w_kv": w_mem_kv, "prm": prm, "wpost": wpost, "cst": cst,
        })
    nc = build()
    res = run_bass_kernel_spmd(nc, in_maps, core_ids=list(range(NCORE)))
    R = res.results
    y_prompt = np.stack([R[b]["y_p"] for b in range(4)])
    y_sample = np.concatenate([R[c]["y_s"].reshape(2, 16, D) for c in range(NCORE)])
    p_conv = np.stack([R[b]["o_conv_p"].transpose(0, 3, 2, 1).reshape(2, 3, 3072) for b in range(4)], axis=1)
    p_ssm = np.stack([R[b]["o_ssm_p"].transpose(0, 2, 1).reshape(2, 32, 64, 128) for b in range(4)], axis=1)
    p_k = np.stack([R[b]["p_k"].reshape(2, SEQ, 4, 2, 128) for b in range(4)], axis=1)
    p_v = np.stack([R[b]["p_v"].reshape(2, SEQ, 4, 256) for b in range(4)], axis=1)
    p_mk = np.stack([R[b]["p_mk"].reshape(2, 256, 4, 256) for b in range(4)], axis=1)
    p_mv = np.stack([R[b]["p_mv"].reshape(2, 256, 4, 256) for b in range(4)], axis=1)
    s_conv = np.concatenate([R[c]["o_conv_s"].transpose(0, 3, 4, 2, 1).reshape(2, 2, 3, 3072) for c in range(NCORE)], axis=1)
    s_ssm = np.concatenate([R[c]["o_ssm_s"].transpose(0, 1, 3, 2).reshape(2, 2, 32, 64, 128) for c in range(NCORE)], axis=1)
    s_k = np.concatenate([R[c]["s_k"].reshape(2, 2, 16, 4, 2, 128) for c in range(NCORE)], axis=1)
    s_v = np.concatenate([R[c]["s_v"].reshape(2, 2, 16, 4, 256) for c in range(NCORE)], axis=1)
    outs = (y_prompt, y_sample, p_conv, p_ssm, p_k, p_v, p_mk, p_mv, s_conv, s_ssm, s_k, s_v)
    return tuple(np.ascontiguousarray(o, dtype=np.float32) for o in outs)
```

```python
import math
from contextlib import ExitStack
import numpy as np
import concourse.bass as bass
import concourse.mybir as mybir
from concourse.bass_utils import run_bass_kernel_spmd

F32 = mybir.dt.float32
BF16 = mybir.dt.bfloat16
AF = mybir.ActivationFunctionType
ALU = mybir.AluOpType
AX = mybir.AxisListType

D = 4096
SEQ = 4096
DIN = 11296
NCORE = 8
COMPUTE = ("pe", "act", "dve", "pool")
NSLOT = {"sp": 16, "pool": 8}
LAM_INIT = [0.8 - 0.6 * math.exp(-0.3 * l) for l in range(2)]
C_Z, C_X, C_B, C_C, C_DT, C_Q, C_K, C_V, C_G, C_QM, C_GM = 0, 2048, 4096, 4608, 5120, 5152, 6176, 7200, 8224, 9248, 10272
PO = {}
_o = 0
for _n, _w in (("wpre", 32), ("memw", 32), ("convw", 96), ("convb", 24), ("dtb", 32), ("alog", 32), ("dsk", 32),
               ("ssmw", 16), ("lq1", 128), ("lk1", 128), ("lq2", 128), ("lk2", 128), ("subln", 256)):
    PO[_n] = (_o, _o + _w)
    _o += _w
NPRM = _o


class TT:
    __slots__ = ("name", "last_w", "readers", "excl")

    def __init__(self, name):
        self.name = name
        self.last_w = None
        self.readers = {}
        self.excl = False


class Tile:
    __slots__ = ("a", "t")

    def __init__(self, a, name):
        self.a = a
        self.t = TT(name)


class Op:
    __slots__ = ("idx", "eng", "fn", "dma", "deps", "needs_inc", "cnt", "slot", "slot_val")

    def __init__(self, idx, eng, fn, dma):
        self.idx = idx
        self.eng = eng
        self.fn = fn
        self.dma = dma
        self.deps = []
        self.needs_inc = False
        self.cnt = 0
        self.slot = None
        self.slot_val = 0


def I(method, *args, **kw):
    return lambda e: getattr(e, method)(*args, **kw)


def seq(thunks):
    def fn(e):
        r = None
        for t in thunks:
            r = t(e)
        return r
    return fn


class Prog:
    def __init__(self):
        self.ops = []
        self.ndma = {"sp": 0, "pool": 0}
        self.dry = False
        self.maxops = 10 ** 9

    def add(self, eng, fn, reads=(), writes=(), dma=False):
        if self.dry or len(self.ops) >= self.maxops:
            return None
        op = Op(len(self.ops), eng, fn, dma)
        deps = {}
        for t in reads:
            t = t.t if hasattr(t, 't') else t
            if t.last_w is not None:
                deps[t.last_w.idx] = t.last_w
            if t.excl:
                for k, r in t.readers.items():
                    if k != eng:
                        deps[r.idx] = r
        for t in writes:
            t = t.t if hasattr(t, 't') else t
            if t.last_w is not None:
                deps[t.last_w.idx] = t.last_w
            for r in t.readers.values():
                deps[r.idx] = r
        op.deps = [d for d in deps.values() if d.dma or d.eng != eng or eng != "pe"]
        for d in op.deps:
            d.needs_inc = True
        for t in reads:
            t = t.t if hasattr(t, 't') else t
            t.readers[("dma", op.idx) if dma else eng] = op
        for t in writes:
            t = t.t if hasattr(t, 't') else t
            t.last_w = op
            t.readers = {}
        if dma:
            i = self.ndma[eng]
            self.ndma[eng] += 1
            op.slot = i % NSLOT[eng]
            op.slot_val = 16 * (i // NSLOT[eng] + 1)
        self.ops.append(op)
        return op

    def dma(self, q, out_ap, in_ap, reads=(), writes=(), **kw):
        return self.add(q, I("dma_start", out=out_ap, in_=in_ap, **kw), reads, writes, dma=True)

    def emit(self, nc):
        with ExitStack() as es:
            sems = {e: es.enter_context(nc.semaphore("s_" + e)) for e in COMPUTE}
            dsem = {q: [es.enter_context(nc.semaphore(f"d_{q}{i}")) for i in range(NSLOT[q])] for q in NSLOT}
            cnt = {e: 0 for e in COMPUTE}
            for op in self.ops:
                if not op.dma:
                    if op.needs_inc:
                        cnt[op.eng] += 1
                    op.cnt = cnt[op.eng]
            final_slot = {q: [0] * NSLOT[q] for q in NSLOT}
            for op in self.ops:
                if op.dma:
                    final_slot[op.eng][op.slot] = op.slot_val
            block = es.enter_context(nc.Block())
            by_eng = {e: [op for op in self.ops if op.eng == e] for e in ("pe", "act", "dve", "pool", "sp")}

            def make(ename):
                ops = by_eng[ename]

                def body(e):
                    waited = {}
                    for op in ops:
                        for d in op.deps:
                            if d.dma:
                                key = (d.eng, d.slot)
                                if waited.get(key, 0) < d.slot_val:
                                    e.wait_ge(dsem[d.eng][d.slot], d.slot_val)
                                    waited[key] = d.slot_val
                            else:
                                if waited.get(d.eng, 0) < d.cnt:
                                    e.wait_ge(sems[d.eng], d.cnt)
                                    waited[d.eng] = d.cnt
                        if op.dma:
                            if op.slot_val > 16:
                                key = (op.eng, op.slot)
                                if waited.get(key, 0) < op.slot_val - 16:
                                    e.wait_ge(dsem[op.eng][op.slot], op.slot_val - 16)
                                    waited[key] = op.slot_val - 16
                            op.fn(e).then_inc(dsem[op.eng][op.slot], 16)
                        else:
                            ins = op.fn(e)
                            if op.needs_inc:
                                ins.then_inc(sems[op.eng], 1)
                    if ename == "sp":
                        for q in NSLOT:
                            for s in range(NSLOT[q]):
                                if final_slot[q][s] > 0:
                                    e.wait_ge(dsem[q][s], final_slot[q][s])
                return body

            block.tensor(make("pe"))
            block.scalar(make("act"))
            block.vector(make("dve"))
            block.gpsimd(make("pool"))
            block.sync(make("sp"))


class Group:
    def __init__(self, name, NT, NTT, nsuper):
        self.name, self.NT, self.NTT, self.nsuper = name, NT, NTT, nsuper
        self.TS = NT * NTT
        self.sample = name == "s"


def build(cfg=None):
    cfg = cfg or {}
    STG = cfg.get('stages', ('ssd', 'attn', 'mem', 'out'))
    NSUP = cfg.get('nsuper', 16)
    LAYERS = cfg.get('layers', (0, 1))
    DO_SAMPLE = cfg.get('sample', True)
    DO_PROMPT = cfg.get('prompt', True)
    DO_MEMKV = cfg.get('memkv', True)
    CUT = cfg.get('cut', 99)
    nc = bass.Bass("TRN2", target_bir_lowering=False)
    P = Prog()
    es = ExitStack()

    def din(name, shape, dt=F32):
        return nc.dram_tensor(name, shape, dt, kind="ExternalInput").ap()

    def dout(name, shape, dt=F32):
        return nc.dram_tensor(name, shape, dt, kind="ExternalOutput").ap()

    def dscr(name, shape, dt=F32):
        return nc.dram_tensor(name, shape, dt, kind="Internal").ap()

    def sb(name, shape, dt=F32):
        return Tile(es.enter_context(nc.sbuf_tensor("sb_" + name, shape, dt)), name)

    def psb(name, shape, dt=F32):
        t = Tile(es.enter_context(nc.psum_tensor("ps_" + name, shape, dt)), name)
        t.t.excl = True
        return t

    xp_d = din("xp", [SEQ, D])
    xs_d = din("xs", [32, D])
    mem_d = din("mem", [256, D])
    cconv_d = din("cconv", [2, 128, 24, 2, 3])
    sssm_d = din("sssm", [2, 2, 128, 2048])
    ckT_d = din("ckT", [2, 2, 8, 128, 1024])
    cv_d = din("cv", [2, 2, 1024, 4, 256])
    cmkT_d = din("cmkT", [2, 2, 8, 128, 256])
    cmv_d = din("cmv", [2, 2, 256, 4, 256])
    win_d = din("w_in", [2, D, DIN])
    wout_d = din("w_out", [2, D, D])
    wkv_d = din("w_kv", [2, D, 2048])
    prm_d = din("prm", [2, 128, NPRM])
    wpost_d = din("wpost", [2, 128, D])
    cst_d = din("cst", [128, 4, 128])

    yp_d = dout("y_p", [SEQ, D])
    ys_d = dout("y_s", [32, D])
    oconvp_d = dout("o_conv_p", [2, 128, 24, 3])
    ossmp_d = dout("o_ssm_p", [2, 128, 2048])
    pk_d = dout("p_k", [2, SEQ, 1024])
    pv_d = dout("p_v", [2, SEQ, 1024])
    pmk_d = dout("p_mk", [2, 256, 1024])
    pmv_d = dout("p_mv", [2, 256, 1024])
    oconvs_d = dout("o_conv_s", [2, 128, 24, 2, 3])
    ossms_d = dout("o_ssm_s", [2, 2, 128, 2048])
    sk_d = dout("s_k", [2, 32, 1024])
    sv_d = dout("s_v", [2, 32, 1024])

    x1p_d = dscr("x1p", [SEQ, D])
    x1s_d = dscr("x1s", [32, D])
    kts_d = dscr("kts", [2, 8, 128, SEQ], BF16)
    vs_d = dscr("vs", [2, SEQ, 4, 256], BF16)
    x1p_t = [TT(f"x1p{i}") for i in range(32)]
    x1s_t = [TT(f"x1s{i}") for i in range(2)]
    kts_t = [[TT(f"kts{l}_{i}") for i in range(32)] for l in range(2)]
    vs_t = [[TT(f"vs{l}_{i}") for i in range(32)] for l in range(2)]

    cst = sb("cst", [128, 4, 128])
    identb = sb("identb", [128, 128], BF16)
    onesb = sb("onesb", [128, 128], BF16)
    prm = sb("prm", [128, NPRM])
    abc = sb("abc", [128, 32])
    neglam = sb("neglam", [128, 1])
    lamtmp = sb("lamtmp", [128, 128])
    lams = sb("lams", [128, 4])
    wpostb = sb("wpostb", [128, 1024])
    xblk = [sb(f"xblk{i}", [128, 1024]) for i in range(2)]
    hsf = [sb(f"hsf{i}", [128, D], BF16) for i in range(2)]
    junkb = sb("junkb", [128, 1024], BF16)
    ssq = sb("ssq", [128, 8])
    rstd = sb("rstd", [128, 2])
    hT = sb("hT", [128, 32, 256], BF16)
    yT = sb("yT", [128, 32, 256], BF16)

    class View:
        def __init__(self, a, t):
            self.a, self.t = a, t
    outb = [View(hT.a[:, 16 * i:16 * (i + 1), :].rearrange("p a b -> p (a b)"), hT.t) for i in range(2)]
    wbuf = [sb(f"wbuf{i}", [128, 32, 256], BF16) for i in range(2)]
    carry = sb("carry", [128, 24, 2, 3])
    xpb = sb("xpb", [128, 4, 262])
    cacc = sb("cacc", [128, 4, 256])
    xpb_t = [TT(f"xpb{i}") for i in range(4)]
    cacc_t = [TT(f"cacc{i}") for i in range(4)]
    BT = sb("BT", [128, 4, 256], BF16)
    CT = sb("CT", [128, 4, 256], BF16)
    Btm = [sb(f"Btm{i}", [128, 512], BF16) for i in range(2)]
    xsT = sb("xsT", [128, 4, 256], BF16)
    dtt = [sb(f"dtt{i}", [128, 32]) for i in range(2)]
    dat = [sb(f"dat{i}", [128, 32]) for i in range(2)]
    ecum = [sb(f"ecum{i}", [128, 32]) for i in range(2)]
    wst = [sb(f"wst{i}", [128, 32]) for i in range(2)]
    cdec = [sb(f"cdec{i}", [128, 32]) for i in range(2)]
    sptmp = sb("sptmp", [128, 32])
    xtm = sb("xtm", [128, 512], BF16)
    xdt = sb("xdt", [128, 512], BF16)
    xw = sb("xw", [128, 512], BF16)
    xD = sb("xD", [128, 512])
    Rt = sb("Rt", [128, 8, 128])
    Et = sb("Et", [128, 8, 128], BF16)
    MT = sb("MT", [128, 8, 128], BF16)
    cbm = sb("cbm", [128, 128], BF16)
    t1 = sb("t1", [128, 512])
    t3 = sb("t3", [128, 512])
    ytm = sb("ytm", [128, 512], BF16)
    Hs = [sb(f"H{g}", [128, 512]) for g in range(4)]
    Hb = [sb(f"Hb{g}", [128, 512], BF16) for g in range(4)]
    nrm = sb("nrm", [128, 256])
    sqt = sb("sqt", [128, 256], BF16)
    QT = sb("QT", [128, 8, 256], BF16)
    QmT = sb("QmT", [128, 8, 256], BF16)
    kf = [sb(f"kf{i}", [128, 512]) for i in range(2)]
    kbf = sb("kbf", [128, 512], BF16)
    KTn = sb("KTn", [128, 8, 256], BF16)
    VAn = [sb(f"VAn{i}", [128, 4, 257], BF16) for i in range(2)]
    KTb = [[sb(f"KTb{i}{m}", [128, 1024], BF16) for m in range(2)] for i in range(2)]
    VA = [sb(f"VA{i}", [128, 8, 257], BF16) for i in range(2)]
    PT = [sb(f"PT{i}", [128, 128], BF16) for i in range(4)]
    PT4 = [sb(f"PT4{i}", [128, 512], BF16) for i in range(2)]
    Osb = [sb(f"Osb{i}", [128, 257]) for i in range(2)]
    rcp = sb("rcp", [128, 4])
    od = sb("od", [128, 256])
    od2 = sb("od2", [128, 256])
    ydtm = sb("ydtm", [128, 256], BF16)
    MKT = sb("MKT", [128, 8, 256], BF16)
    MVA = [sb(f"MVA{j}", [128, 4, 257], BF16) for j in range(2)]
    acc = [psb(f"acc{i}", [128, 512]) for i in range(4)]
    tmp = [psb(f"tmp{i}", [128, 512]) for i in range(2)]
    trb = [psb(f"trb{i}", [128, 1024], BF16) for i in range(2)]
    rr = {"tmp": 0, "trb": 0, "acc": 0, "pt": 0, "kf": 0, "p4": 0}

    def nxt(kind, n):
        rr[kind] = (rr[kind] + 1) % n
        return rr[kind]

    ident = cst.a[:, 0, :]
    MLE = cst.a[:, 1, :]
    MGT = cst.a[:, 2, :]
    ones = cst.a[:, 3, :]

    def pc(name):
        a, b = PO[name]
        return prm.a[:, a:b]

    P.dma("sp", cst.a[:], cst_d, writes=[cst])
    P.add("dve", I("tensor_copy", identb.a[:], ident), [cst], [identb])
    P.add("dve", I("tensor_copy", onesb.a[:], ones), [cst], [onesb])
    for t in VAn + VA + MVA:
        P.add("pool", I("memset", t.a[:, :, 256:257], 1.0), [], [t])

    NWSCR = 160
    wscr_parts = [dscr(f"wscr{i}", [40, 128, 32, 256], BF16) for i in range(4)]
    WVN = {}

    def wscr_ap(idx):
        return wscr_parts[idx // 40][idx % 40]
    wscr_t = [TT(f"wscr{i}") for i in range(NWSCR)]

    class WStream:
        def __init__(self):
            self.descs = []
            self.issued = 0
            self.used = 0
            self.slot = {}

        def _issue(self, i):
            key, src, ncols = self.descs[i]
            b = wbuf[i % 2]
            if key not in self.slot:
                idx = len(self.slot)
                assert idx < NWSCR
                self.slot[key] = idx
                P.dma("pool", b.a[:, :, 0:ncols], src, writes=[b])
                P.dma("sp", wscr_ap(idx)[:, :, 0:ncols], b.a[:, :, 0:ncols], reads=[b], writes=[wscr_t[idx]])
            else:
                idx = self.slot[key]
                P.dma("pool", b.a[:, :, 0:ncols], wscr_ap(idx)[:, :, 0:ncols], reads=[wscr_t[idx]], writes=[b])

        def next(self, key, src, ncols):
            if P.dry:
                self.descs.append((key, src, ncols))
                return wbuf[0]
            i = self.used
            while self.issued < min(i + 2, len(self.descs)):
                self._issue(self.issued)
                self.issued += 1
            self.used += 1
            return wbuf[i % 2]

    def prenorm_A(G, rows_ap, rows_t, ntiles):
        NT = G.NT
        for tt in range(ntiles):
            src = rows_ap(tt)
            rt = rows_t(tt)
            for cb in range(4):
                xb = xblk[cb % 2]
                P.dma("sp", xb.a[:NT, :], src[:, cb * 1024:(cb + 1) * 1024], reads=rt, writes=[xb])
                P.add("act", I("activation", out=junkb.a[:NT, :], in_=xb.a[:NT, :], func=AF.Square, accum_out=ssq.a[:NT, cb:cb + 1]),
                      [xb], [junkb, ssq])
            P.add("dve", I("reduce_sum", out=rstd.a[:NT, 0:1], in_=ssq.a[:NT, 0:4], axis=AX.X), [ssq], [rstd])
            P.add("dve", I("tensor_scalar", out=rstd.a[:NT, 0:1], in0=rstd.a[:NT, 0:1], scalar1=1.0 / D, scalar2=1e-6, op0=ALU.mult, op1=ALU.add), [rstd], [rstd])
            P.add("act", I("activation", out=rstd.a[:NT, 0:1], in_=rstd.a[:NT, 0:1], func=AF.Sqrt), [rstd], [rstd])
            P.add("dve", I("reciprocal", out=rstd.a[:NT, 0:1], in_=rstd.a[:NT, 0:1]), [rstd], [rstd])
            for cb in range(4):
                xb = xblk[cb % 2]
                P.dma("sp", xb.a[:NT, :], src[:, cb * 1024:(cb + 1) * 1024], reads=rt, writes=[xb])
                P.add("dve", I("tensor_scalar_mul", out=hsf[tt].a[:NT, cb * 1024:(cb + 1) * 1024], in0=xb.a[:NT, :], scalar1=rstd.a[:NT, 0:1]), [xb, rstd], [hsf[tt]])

    def prenorm_B(G, wcol, ntiles):
        NT = G.NT
        for tt in range(ntiles):
            for cb in range(4):
                tb = trb[nxt("trb", 2)]
                P.add("pe", seq([I("transpose", out=tb.a[:, j * 128:j * 128 + NT], in_=hsf[tt].a[:NT, cb * 1024 + j * 128:cb * 1024 + (j + 1) * 128], identity=identb.a[:NT, :NT]) for j in range(8)]),
                      [hsf[tt], identb], [tb])
                o = hT.a[:, cb * 8:(cb + 1) * 8, tt * NT:(tt + 1) * NT]
                i0 = tb.a[:, :].rearrange("p (j t) -> p j t", t=128)[:, :, 0:NT]
                i1 = wcol[:, cb * 8:(cb + 1) * 8].unsqueeze(2).to_broadcast([128, 8, NT])
                P.add("dve", I("tensor_tensor", out=o, in0=i0, in1=i1, op=ALU.mult), [tb, prm], [hT])

    def fm_chunk(G, ws, wv, c0, ncols=512):
        TS = G.TS
        base = 2 * nxt("acc", 2)
        pss = [(acc[base + ct // 2], (ct % 2) * 256) for ct in range(4)]
        for half in range((ncols + 255) // 256):
            nc_ = min(256, ncols - half * 256)
            wb = ws.next((WVN[id(wv)], c0 + half * 256), wv[:, :, c0 + half * 256:c0 + half * 256 + nc_], nc_)
            th = []
            for c2 in range(nc_ // 128):
                pt, off = pss[half * 2 + c2]
                for kt in range(32):
                    th.append(I("matmul", pt.a[:, off:off + TS], wb.a[:, kt, c2 * 128:(c2 + 1) * 128], hT.a[:, kt, 0:TS], start=(kt == 0), stop=(kt == 31)))
            P.add("pe", seq(th), [wb, hT], [pss[half * 2][0]])
        return pss

    def tm_chunk(G, ws, wv, c0, ncols=512, src=None):
        NT = G.NT
        sT = hT if src is None else src
        base = 2 * nxt("acc", 2)
        pss = [acc[base], acc[base + 1]]
        for half in range((ncols + 255) // 256):
            nc_ = min(256, ncols - half * 256)
            wb = ws.next((WVN[id(wv)], c0 + half * 256), wv[:, :, c0 + half * 256:c0 + half * 256 + nc_], nc_)
            th = []
            for tt in range(G.NTT):
                for kt in range(32):
                    th.append(I("matmul", pss[tt].a[:NT, half * 256:half * 256 + nc_], sT.a[:, kt, tt * NT:(tt + 1) * NT], wb.a[:, kt, 0:nc_], start=(kt == 0), stop=(kt == 31)))
            P.add("pe", seq(th), [wb, sT], pss[:G.NTT])
        return pss

    def run_layer(l, G, xsrc_d, xsrc_t, xdst_d, xdst_t, ws):
        NT, NTT, TS = G.NT, G.NTT, G.TS
        wv_in = win_d[l].rearrange("(kt p) c -> p kt c", p=128)
        wv_out = wout_d[l].rearrange("(kt p) c -> p kt c", p=128)
        WVN[id(wv_in)] = f"in{l}"
        WVN[id(wv_out)] = f"out{l}"
        if not G.sample:
            P.add("pool", I("memset", carry.a[:], 0.0), [], [carry])
            for g in range(4):
                P.add("pool", I("memset", Hs[g].a[:], 0.0), [], [Hs[g]])
                P.add("pool", I("memset", Hb[g].a[:], 0.0), [], [Hb[g]])
        else:
            P.dma("sp", carry.a[:], cconv_d[l], writes=[carry])

        def conv_chunk(pss, ct0, dst):
            nseg = 2 if G.sample else 1
            sl = TS // nseg
            xv = xpb.a[:, :, 0:nseg * (3 + sl)].rearrange("p c (s t) -> p c s t", s=nseg)
            P.add("dve", I("tensor_copy", xv[:, :, :, 0:3], carry.a[:, ct0:ct0 + 4, 0:nseg, :]), [carry], xpb_t)
            for ct in range(4):
                pt, off = pss[ct]
                P.add("act", I("copy", out=xv[:, ct, :, 3:3 + sl], in_=pt.a[:, off:off + TS].rearrange("p (s t) -> p s t", s=nseg)), [pt], [xpb_t[ct]])
            P.add("dve", I("tensor_copy", carry.a[:, ct0:ct0 + 4, 0:nseg, :], xv[:, :, :, sl:sl + 3]), xpb_t, [carry])
            cv = cacc.a[:, :, 0:TS].rearrange("p c (s t) -> p c s t", s=nseg)
            cw = pc("convw").rearrange("p (c j) -> p c j", j=4)
            for ct in range(4):
                gct = ct0 + ct
                P.add("dve", I("tensor_scalar", out=cv[:, ct], in0=xv[:, ct, :, 0:sl], scalar1=cw[:, gct, 0:1], scalar2=pc("convb")[:, gct:gct + 1], op0=ALU.mult, op1=ALU.add), [xpb_t[ct], prm], [cacc_t[ct]])
            for j in range(1, 4):
                for ct in range(4):
                    gct = ct0 + ct
                    P.add("dve", I("scalar_tensor_tensor", out=cv[:, ct], in0=xv[:, ct, :, j:j + sl], scalar=cw[:, gct, j:j + 1], in1=cv[:, ct], op0=ALU.mult, op1=ALU.add), [xpb_t[ct], prm, cacc_t[ct]], [cacc_t[ct]])
            P.add("act", I("activation", out=dst.a[:, :, 0:TS], in_=cacc.a[:, :, 0:TS], func=AF.Silu), cacc_t, [dst])

        for s in range(G.nsuper):
            tok0 = s * TS
            gt = [s * NTT + tt for tt in range(NTT)]
            if s == 0:
                prenorm_A(G, lambda tt: xsrc_d[tok0 + tt * NT: tok0 + (tt + 1) * NT, :], lambda tt: xsrc_t(gt[tt]), NTT)
            prenorm_B(G, pc("wpre"), NTT)
            if CUT < 1:
                continue
            pss = fm_chunk(G, ws, wv_in, C_B)
            conv_chunk(pss, 16, BT)
            pss = fm_chunk(G, ws, wv_in, C_C)
            conv_chunk(pss, 20, CT)
            pss = tm_chunk(G, ws, wv_in, C_DT, 32)
            for tt in range(NTT):
                P.add("dve", I("tensor_tensor", out=sptmp.a[:NT, :], in0=pss[tt].a[:NT, 0:32], in1=pc("dtb")[:NT, :], op=ALU.add), [pss[tt], prm], [sptmp])
                P.add("act", I("activation", out=sptmp.a[:NT, :], in_=sptmp.a[:NT, :], func=AF.Exp), [sptmp], [sptmp])
                P.add("act", I("activation", out=dtt[tt].a[:NT, :], in_=sptmp.a[:NT, :], func=AF.Ln, bias=1.0), [sptmp], [dtt[tt]])
                P.add("dve", I("tensor_tensor", out=dat[tt].a[:NT, :], in0=dtt[tt].a[:NT, :], in1=abc.a[:NT, :], op=ALU.mult), [dtt[tt], abc], [dat[tt]])
                pt = tmp[nxt("tmp", 2)]
                P.add("pe", seq([
                    I("matmul", pt.a[:NT, 0:32], MLE[:NT, :NT], dat[tt].a[:NT, :], start=True, stop=True),
                    I("matmul", pt.a[:NT, 32:64], MGT[:NT, :NT], dat[tt].a[:NT, :], start=True, stop=True),
                    I("matmul", pt.a[:, 64:96], ones[:NT, :], dat[tt].a[:NT, :], start=True, stop=True)]), [cst, dat[tt]], [pt])
                P.add("act", I("activation", out=ecum[tt].a[:NT, :], in_=pt.a[:NT, 0:32], func=AF.Exp), [pt], [ecum[tt]])
                P.add("act", I("activation", out=wst[tt].a[:NT, :], in_=pt.a[:NT, 32:64], func=AF.Exp), [pt], [wst[tt]])
                P.add("act", I("activation", out=cdec[tt].a[:, :], in_=pt.a[:, 64:96], func=AF.Exp), [pt], [cdec[tt]])
                P.add("dve", I("tensor_tensor", out=wst[tt].a[:NT, :], in0=wst[tt].a[:NT, :], in1=dtt[tt].a[:NT, :], op=ALU.mult), [wst[tt], dtt[tt]], [wst[tt]])
                tb = trb[nxt("trb", 2)]
                P.add("pe", seq([I("transpose", out=tb.a[:NT, g * 128:(g + 1) * 128], in_=BT.a[:, g, tt * NT:(tt + 1) * NT], identity=identb.a[:]) for g in range(4)]), [BT, identb], [tb])
                P.add("act", I("copy", out=Btm[tt].a[:NT, :], in_=tb.a[:NT, 0:512]), [tb], [Btm[tt]])
            if CUT < 2:
                continue
            for g in range(4):
                pss = fm_chunk(G, ws, wv_in, C_Z + g * 512)
                for ct in range(4):
                    pt, off = pss[ct]
                    P.add("act", I("activation", out=yT.a[:, 4 * g + ct, 0:TS], in_=pt.a[:, off:off + TS], func=AF.Silu), [pt], [yT])
                pss = fm_chunk(G, ws, wv_in, C_X + g * 512)
                conv_chunk(pss, 4 * g, xsT)
                for tt in range(NTT):
                    tsl = slice(tt * NT, (tt + 1) * NT)
                    if G.sample:
                        P.dma("sp", Hs[g].a[:], sssm_d[l, tt, :, g * 512:(g + 1) * 512], writes=[Hs[g]])
                        P.add("act", I("copy", out=Hb[g].a[:], in_=Hs[g].a[:]), [Hs[g]], [Hb[g]])
                    tb = trb[nxt("trb", 2)]
                    P.add("pe", seq([I("transpose", out=tb.a[:NT, c * 128:(c + 1) * 128], in_=xsT.a[:, c, tsl], identity=identb.a[:]) for c in range(4)]), [xsT, identb], [tb])
                    P.add("act", I("copy", out=xtm.a[:NT, :], in_=tb.a[:NT, 0:512]), [tb], [xtm])
                    hs8 = slice(8 * g, 8 * g + 8)
                    x3 = xtm.a[:NT, :].rearrange("p (h d) -> p h d", d=64)
                    P.add("dve", I("tensor_tensor", out=xdt.a[:NT, :].rearrange("p (h d) -> p h d", d=64), in0=x3, in1=dtt[tt].a[:NT, hs8].unsqueeze(2).to_broadcast([NT, 8, 64]), op=ALU.mult), [xtm, dtt[tt]], [xdt])
                    P.add("dve", I("tensor_tensor", out=xw.a[:NT, :].rearrange("p (h d) -> p h d", d=64), in0=x3, in1=wst[tt].a[:NT, hs8].unsqueeze(2).to_broadcast([NT, 8, 64]), op=ALU.mult), [xtm, wst[tt]], [xw])
                    P.add("dve", I("tensor_tensor", out=xD.a[:NT, :].rearrange("p (h d) -> p h d", d=64), in0=x3, in1=pc("dsk")[:NT, hs8].unsqueeze(2).to_broadcast([NT, 8, 64]), op=ALU.mult), [xtm, prm], [xD])
                    P.add("dve", I("tensor_tensor", out=Rt.a[:NT, :, :NT], in0=dat[tt].a[:NT, hs8].unsqueeze(2).to_broadcast([NT, 8, NT]), in1=MLE[:NT, :NT].unsqueeze(1).to_broadcast([NT, 8, NT]), op=ALU.mult), [dat[tt], cst], [Rt])
                    sg = [tmp[0], tmp[1]]
                    P.add("pe", seq([I("matmul", sg[hh // 4].a[:NT, (hh % 4) * 128:(hh % 4) * 128 + NT], MGT[:NT, :NT], Rt.a[:NT, hh, :NT], start=True, stop=True) for hh in range(8)]), [cst, Rt], sg)
                    for q in range(2):
                        P.add("act", I("activation", out=Et.a[:NT, 4 * q:4 * q + 4, :NT], in_=sg[q].a[:NT, :].rearrange("p (h t) -> p h t", t=128)[:, :, :NT], func=AF.Exp), [sg[q]], [Et])
                    pcb = acc[2 * rr["acc"] + 0]
                    pa = [acc[(2 * rr["acc"] + 2) % 4], acc[(2 * rr["acc"] + 3) % 4]]
                    P.add("pe", I("matmul", pcb.a[:NT, 0:NT], BT.a[:, g, tsl], CT.a[:, g, tsl], start=True, stop=True), [BT, CT], [pcb])
                    P.add("dve", I("tensor_tensor", out=cbm.a[:NT, :NT], in0=pcb.a[:NT, 0:NT], in1=MLE[:NT, :NT], op=ALU.mult), [pcb, cst], [cbm])
                    P.add("dve", I("tensor_tensor", out=MT.a[:NT, :, :NT], in0=Et.a[:NT, :, :NT], in1=cbm.a[:NT, :NT].unsqueeze(1).to_broadcast([NT, 8, NT]), op=ALU.mult), [Et, cbm], [MT])
                    P.add("pe", seq([I("matmul", pa[0].a[:NT, hh * 64:(hh + 1) * 64], MT.a[:NT, hh, :NT], xdt.a[:NT, hh * 64:(hh + 1) * 64], start=True, stop=True) for hh in range(8)]), [MT, xdt], [pa[0]])
                    P.add("pe", I("matmul", pa[1].a[:NT, :], CT.a[:, g, tsl], Hb[g].a[:], start=True, stop=True), [CT, Hb[g]], [pa[1]])
                    P.add("dve", I("tensor_tensor", out=t1.a[:NT, :].rearrange("p (h d) -> p h d", d=64), in0=pa[1].a[:NT, :].rearrange("p (h d) -> p h d", d=64), in1=ecum[tt].a[:NT, hs8].unsqueeze(2).to_broadcast([NT, 8, 64]), op=ALU.mult), [pa[1], ecum[tt]], [t1])
                    P.add("dve", I("tensor_tensor", out=t3.a[:NT, :], in0=t1.a[:NT, :], in1=xD.a[:NT, :], op=ALU.add), [t1, xD], [t3])
                    P.add("dve", I("tensor_tensor", out=ytm.a[:NT, :], in0=t3.a[:NT, :], in1=pa[0].a[:NT, :], op=ALU.add), [t3, pa[0]], [ytm])
                    pS = tmp[0]
                    P.add("pe", I("matmul", pS.a[:, :], Btm[tt].a[:NT, g * 128:(g + 1) * 128], xw.a[:NT, :], start=True, stop=True), [Btm[tt], xw], [pS])
                    P.add("dve", I("tensor_tensor", out=Hs[g].a[:].rearrange("p (h d) -> p h d", d=64), in0=Hs[g].a[:].rearrange("p (h d) -> p h d", d=64), in1=cdec[tt].a[:, hs8].unsqueeze(2).to_broadcast([128, 8, 64]), op=ALU.mult), [Hs[g], cdec[tt]], [Hs[g]])
                    P.add("dve", I("tensor_tensor", out=Hs[g].a[:], in0=Hs[g].a[:], in1=pS.a[:, :], op=ALU.add), [Hs[g], pS], [Hs[g]])
                    P.add("act", I("copy", out=Hb[g].a[:], in_=Hs[g].a[:]), [Hs[g]], [Hb[g]])
                    if G.sample:
                        P.dma("sp", ossms_d[l, tt, :, g * 512:(g + 1) * 512], Hs[g].a[:], reads=[Hs[g]])
                    elif s == G.nsuper - 1 and tt == NTT - 1:
                        P.dma("sp", ossmp_d[l, :, g * 512:(g + 1) * 512], Hs[g].a[:], reads=[Hs[g]])
                    tb = trb[nxt("trb", 2)]
                    P.add("pe", seq([I("transpose", out=tb.a[:, c * 128:c * 128 + NT], in_=ytm.a[:NT, c * 128:(c + 1) * 128], identity=identb.a[:NT, :NT]) for c in range(4)]), [ytm, identb], [tb])
                    P.add("dve", I("tensor_tensor", out=yT.a[:, 4 * g:4 * g + 4, tsl], in0=yT.a[:, 4 * g:4 * g + 4, tsl], in1=tb.a[:, 0:512].rearrange("p (c t) -> p c t", t=128)[:, :, :NT], op=ALU.mult), [yT, tb], [yT])
                pn = tmp[1]
                for c in range(4):
                    P.add("dve", I("tensor_tensor", out=sqt.a[:, 0:TS], in0=yT.a[:, 4 * g + c, 0:TS], in1=yT.a[:, 4 * g + c, 0:TS], op=ALU.mult), [yT], [sqt])
                    P.add("pe", I("matmul", pn.a[:, 0:TS], onesb.a[:], sqt.a[:, 0:TS], start=(c == 0), stop=(c == 3)), [onesb, sqt], [pn])
                P.add("dve", I("tensor_scalar", out=nrm.a[:, 0:TS], in0=pn.a[:, 0:TS], scalar1=1.0 / 512, scalar2=1e-6, op0=ALU.mult, op1=ALU.add), [pn], [nrm])
                P.add("act", I("activation", out=nrm.a[:, 0:TS], in_=nrm.a[:, 0:TS], func=AF.Sqrt), [nrm], [nrm])
                P.add("dve", I("reciprocal", out=nrm.a[:, 0:TS], in_=nrm.a[:, 0:TS]), [nrm], [nrm])
                for c in range(4):
                    P.add("dve", I("scalar_tensor_tensor", out=yT.a[:, 4 * g + c, 0:TS], in0=yT.a[:, 4 * g + c, 0:TS], scalar=pc("ssmw")[:, 4 * g + c:4 * g + c + 1], in1=nrm.a[:, 0:TS], op0=ALU.mult, op1=ALU.mult), [yT, prm, nrm], [yT])
            if CUT < 3:
                continue
            if G.sample:
                P.dma("sp", oconvs_d[l], carry.a[:], reads=[carry])
            elif s == G.nsuper - 1:
                P.dma("sp", oconvp_d[l], carry.a[:, :, 0, :], reads=[carry])

            for c in range(2):
                pss = fm_chunk(G, ws, wv_in, C_Q + c * 512)
                for ct in range(4):
                    pt, off = pss[ct]
                    P.add("act", I("copy", out=QT.a[:, 4 * c + ct, 0:TS], in_=pt.a[:, off:off + TS]), [pt], [QT])
            kdst = sk_d if G.sample else pk_d
            vdst = sv_d if G.sample else pv_d
            for c in range(2):
                pss = tm_chunk(G, ws, wv_in, C_K + c * 512)
                for tt in range(NTT):
                    k32 = kf[nxt("kf", 2)]
                    P.add("act", I("copy", out=k32.a[:NT, :], in_=pss[tt].a[:NT, :]), [pss[tt]], [k32])
                    P.dma("sp", kdst[l, tok0 + tt * NT: tok0 + (tt + 1) * NT, c * 512:(c + 1) * 512], k32.a[:NT, :], reads=[k32])
                    P.add("dve", I("tensor_copy", kbf.a[:NT, :], pss[tt].a[:NT, :]), [pss[tt]], [kbf])
                    tb = trb[nxt("trb", 2)]
                    P.add("pe", seq([I("transpose", out=tb.a[:, j * 128:j * 128 + NT], in_=kbf.a[:NT, j * 128:(j + 1) * 128], identity=identb.a[:NT, :NT]) for j in range(4)]), [kbf, identb], [tb])
                    P.add("act", I("copy", out=KTn.a[:, 4 * c:4 * c + 4, tt * NT:(tt + 1) * NT], in_=tb.a[:, 0:512].rearrange("p (j t) -> p j t", t=128)[:, :, :NT]), [tb], [KTn])
            if not G.sample:
                for tt in range(NTT):
                    P.dma("sp", kts_d[l, :, :, tok0 + tt * NT: tok0 + (tt + 1) * NT].rearrange("j p t -> p j t"), KTn.a[:, :, tt * NT:(tt + 1) * NT], reads=[KTn], writes=[kts_t[l][gt[tt]]])
            for c in range(2):
                pss = tm_chunk(G, ws, wv_in, C_V + c * 512)
                for tt in range(NTT):
                    k32 = kf[nxt("kf", 2)]
                    P.add("act", I("copy", out=k32.a[:NT, :], in_=pss[tt].a[:NT, :]), [pss[tt]], [k32])
                    P.dma("sp", vdst[l, tok0 + tt * NT: tok0 + (tt + 1) * NT, c * 512:(c + 1) * 512], k32.a[:NT, :], reads=[k32])
                    P.add("dve", I("tensor_copy", VAn[tt].a[:NT, 2 * c:2 * c + 2, 0:256], pss[tt].a[:NT, :].rearrange("p (h e) -> p h e", e=256)), [pss[tt]], [VAn[tt]])
            if not G.sample:
                for tt in range(NTT):
                    P.dma("sp", vs_d[l, tok0 + tt * NT: tok0 + (tt + 1) * NT, :, :], VAn[tt].a[:NT, :, 0:256], reads=[VAn[tt]], writes=[vs_t[l][gt[tt]]])
            for c in range(2):
                pss = fm_chunk(G, ws, wv_in, C_G + c * 512)
                for ct in range(4):
                    pt, off = pss[ct]
                    P.add("act", I("activation", out=yT.a[:, 16 + 4 * c + ct, 0:TS], in_=pt.a[:, off:off + TS], func=AF.Silu), [pt], [yT])

            if CUT < 4:
                continue
            sc_d = 128 ** -0.5
            pend_comb = []
            for tt in range(NTT):
                qsl = slice(tt * NT, (tt + 1) * NT)
                for h in range(4):
                    if G.sample:
                        pieces = [(0, 8)]
                    else:
                        nh = gt[tt]
                        pieces = [(a, min(8, nh - a)) for a in range(0, nh, 8)]
                    O = [acc[0], acc[1]] if (rr["acc"] == 0) else [acc[2], acc[3]]
                    nxt("acc", 2)
                    kcount = 0
                    pend = []
                    for (k0, nkt) in pieces:
                        bi = nxt("pt", 2)
                        kb = KTb[bi]
                        va = VA[bi]
                        if G.sample:
                            for m in range(2):
                                P.dma("pool", kb[m].a[:, :], ckT_d[l, tt, 2 * h + m], writes=[kb[m]])
                            P.dma("pool", va.a[:, :, 0:256], cv_d[l, tt, :, h, :].rearrange("(kt p) e -> p kt e", p=128), writes=[va])
                        else:
                            rd_k = [kts_t[l][k0 + i] for i in range(nkt)]
                            rd_v = [vs_t[l][k0 + i] for i in range(nkt)]
                            for m in range(2):
                                P.dma("sp", kb[m].a[:, 0:nkt * 128], kts_d[l, 2 * h + m, :, k0 * 128:(k0 + nkt) * 128], reads=rd_k, writes=[kb[m]])
                            P.dma("sp", va.a[:, 0:nkt, 0:256], vs_d[l, k0 * 128:(k0 + nkt) * 128, h, :].rearrange("(kt p) e -> p kt e", p=128), reads=rd_v, writes=[va])
                        for kt0 in range(0, nkt, 2):
                            nb = min(2, nkt - kt0)
                            st = tmp[nxt("tmp", 2)]
                            p4 = PT4[nxt("p4", 2)]
                            th = []
                            for j in range(nb):
                                for m in range(2):
                                    b_ = j * 2 + m
                                    th.append(I("matmul", st.a[:, b_ * 128:b_ * 128 + NT], kb[m].a[:, (kt0 + j) * 128:(kt0 + j + 1) * 128], QT.a[:, 2 * h + m, qsl], start=True, stop=True))
                            P.add("pe", seq(th), [kb[0], kb[1], QT], [st])
                            vin = st.a[:, 0:nb * 256].rearrange("p (b t) -> p b t", t=128)[:, :, :NT]
                            vout = p4.a[:, 0:nb * 256].rearrange("p (b t) -> p b t", t=128)[:, :, :NT]
                            P.add("act", I("activation", out=vout, in_=vin, func=AF.Exp, scale=sc_d), [st], [p4])
                            while pend:
                                pend.pop(0)()

                            def mk(p4=p4, va=va, kt0=kt0, nb=nb, first=(kcount == 0), O=O):
                                def f():
                                    for m in range(2):
                                        th2 = [I("matmul", O[m].a[:NT, 0:257], p4.a[:, (j * 2 + m) * 128:(j * 2 + m) * 128 + NT], va.a[:, kt0 + j, :], start=(first and j == 0), stop=False) for j in range(nb)]
                                        P.add("pe", seq(th2), [p4, va], [O[m]])
                                return f
                            pend.append(mk())
                            kcount += nb
                    while pend:
                        pend.pop(0)()
                    for m in range(2):
                        st = tmp[nxt("tmp", 2)]
                        P.add("pe", I("matmul", st.a[:NT, 0:NT], KTn.a[:, 2 * h + m, qsl], QT.a[:, 2 * h + m, qsl], start=True, stop=True), [KTn, QT], [st])
                        pt_ = PT[2 * m + (kcount % 2)]
                        P.add("act", I("activation", out=pt_.a[:NT, 0:NT], in_=st.a[:NT, 0:NT], func=AF.Exp, scale=sc_d), [st], [pt_])
                        if not G.sample:
                            P.add("dve", I("memset", pt_.a[64:128, 0:64], 0.0), [], [pt_])
                        P.add("pe", I("matmul", O[m].a[:NT, 0:257], pt_.a[:NT, 0:NT], VAn[tt].a[:NT, h, :], start=(kcount == 0), stop=True), [pt_, VAn[tt]], [O[m]])
                    def mk_comb(O=O, h=h, qsl=qsl, tt=tt):
                        def f():
                            for m in range(2):
                                P.add("act", I("copy", out=Osb[m].a[:NT, :], in_=O[m].a[:NT, 0:257]), [O[m]], [Osb[m]])
                                P.add("dve", I("reciprocal", out=rcp.a[:NT, m:m + 1], in_=Osb[m].a[:NT, 256:257]), [Osb[m]], [rcp])
                            P.add("dve", I("tensor_tensor", out=rcp.a[:NT, 1:2], in0=rcp.a[:NT, 1:2], in1=neglam.a[:NT, 0:1], op=ALU.mult), [rcp, neglam], [rcp])
                            P.add("dve", I("tensor_scalar_mul", out=od.a[:NT, :], in0=Osb[0].a[:NT, 0:256], scalar1=rcp.a[:NT, 0:1]), [Osb[0], rcp], [od])
                            P.add("dve", I("scalar_tensor_tensor", out=od.a[:NT, :], in0=Osb[1].a[:NT, 0:256], scalar=rcp.a[:NT, 1:2], in1=od.a[:NT, :], op0=ALU.mult, op1=ALU.add), [Osb[1], rcp, od], [od])
                            P.add("act", I("activation", out=od2.a[:NT, :], in_=od.a[:NT, :], func=AF.Square, accum_out=rcp.a[:NT, 2:3]), [od], [od2, rcp])
                            P.add("dve", I("tensor_scalar", out=rcp.a[:NT, 2:3], in0=rcp.a[:NT, 2:3], scalar1=1.0 / 256, scalar2=1e-5, op0=ALU.mult, op1=ALU.add), [rcp], [rcp])
                            P.add("act", I("activation", out=rcp.a[:NT, 2:3], in_=rcp.a[:NT, 2:3], func=AF.Sqrt), [rcp], [rcp])
                            P.add("dve", I("reciprocal", out=rcp.a[:NT, 3:4], in_=rcp.a[:NT, 2:3]), [rcp], [rcp])
                            P.add("dve", I("tensor_scalar", out=od2.a[:NT, :], in0=od.a[:NT, :], scalar1=rcp.a[:NT, 3:4], scalar2=1.0 - LAM_INIT[l], op0=ALU.mult, op1=ALU.mult), [od, rcp], [od2])
                            P.add("dve", I("tensor_tensor", out=ydtm.a[:NT, :], in0=od2.a[:NT, :], in1=pc("subln")[:NT, :], op=ALU.mult), [od2, prm], [ydtm])
                            tb = trb[nxt("trb", 2)]
                            P.add("pe", seq([I("transpose", out=tb.a[:, c * 128:c * 128 + NT], in_=ydtm.a[:NT, c * 128:(c + 1) * 128], identity=identb.a[:NT, :NT]) for c in range(2)]), [ydtm, identb], [tb])
                            P.add("dve", I("tensor_tensor", out=yT.a[:, 16 + 2 * h:16 + 2 * h + 2, qsl], in0=yT.a[:, 16 + 2 * h:16 + 2 * h + 2, qsl], in1=tb.a[:, 0:256].rearrange("p (c t) -> p c t", t=128)[:, :, :NT], op=ALU.mult), [yT, tb], [yT])
                        return f
                    while pend_comb:
                        pend_comb.pop(0)()
                    pend_comb.append(mk_comb())
            while pend_comb:
                pend_comb.pop(0)()

            if CUT < 5:
                continue
            for c in range(2):
                pss = fm_chunk(G, ws, wv_in, C_QM + c * 512)
                for ct in range(4):
                    pt, off = pss[ct]
                    P.add("act", I("copy", out=QmT.a[:, 4 * c + ct, 0:TS], in_=pt.a[:, off:off + TS]), [pt], [QmT])
            for c in range(2):
                pss = fm_chunk(G, ws, wv_in, C_GM + c * 512)
                for ct in range(4):
                    pt, off = pss[ct]
                    P.add("act", I("activation", out=yT.a[:, 24 + 4 * c + ct, 0:TS], in_=pt.a[:, off:off + TS], func=AF.Silu), [pt], [yT])
            sc_m = 256 ** -0.5
            for tt in range(NTT):
                qsl = slice(tt * NT, (tt + 1) * NT)
                if G.sample:
                    P.dma("pool", MKT.a[:], cmkT_d[l, tt].rearrange("j p t -> p j t"), writes=[MKT])
                    for j in range(2):
                        P.dma("pool", MVA[j].a[:, :, 0:256], cmv_d[l, tt, j * 128:(j + 1) * 128, :, :], writes=[MVA[j]])
                for h in range(4):
                    Om = acc[2 * nxt("acc", 2)]
                    st = tmp[nxt("tmp", 2)]
                    p4 = PT4[nxt("p4", 2)]
                    P.add("pe", seq([I("matmul", st.a[:, j * 128:j * 128 + NT], MKT.a[:, 2 * h + dt_, j * 128:(j + 1) * 128], QmT.a[:, 2 * h + dt_, qsl], start=(dt_ == 0), stop=(dt_ == 1)) for j in range(2) for dt_ in range(2)]), [MKT, QmT], [st])
                    P.add("act", I("activation", out=p4.a[:, 0:256].rearrange("p (b t) -> p b t", t=128)[:, :, :NT], in_=st.a[:, 0:256].rearrange("p (b t) -> p b t", t=128)[:, :, :NT], func=AF.Exp, scale=sc_m), [st], [p4])
                    P.add("pe", seq([I("matmul", Om.a[:NT, 0:257], p4.a[:, j * 128:j * 128 + NT], MVA[j].a[:, h, :], start=(j == 0), stop=(j == 1)) for j in range(2)]), [p4, MVA[0], MVA[1]], [Om])

                    def mk_mcomb(Om=Om, h=h, qsl=qsl):
                        def f():
                            P.add("act", I("copy", out=Osb[0].a[:NT, :], in_=Om.a[:NT, 0:257]), [Om], [Osb[0]])
                            P.add("dve", I("reciprocal", out=rcp.a[:NT, 0:1], in_=Osb[0].a[:NT, 256:257]), [Osb[0]], [rcp])
                            P.add("dve", I("tensor_scalar_mul", out=ydtm.a[:NT, :], in0=Osb[0].a[:NT, 0:256], scalar1=rcp.a[:NT, 0:1]), [Osb[0], rcp], [ydtm])
                            tb = trb[nxt("trb", 2)]
                            P.add("pe", seq([I("transpose", out=tb.a[:, c * 128:c * 128 + NT], in_=ydtm.a[:NT, c * 128:(c + 1) * 128], identity=identb.a[:NT, :NT]) for c in range(2)]), [ydtm, identb], [tb])
                            P.add("dve", I("tensor_tensor", out=yT.a[:, 24 + 2 * h:24 + 2 * h + 2, qsl], in0=yT.a[:, 24 + 2 * h:24 + 2 * h + 2, qsl], in1=tb.a[:, 0:256].rearrange("p (c t) -> p c t", t=128)[:, :, :NT], op=ALU.mult), [yT, tb], [yT])
                        return f
                    while pend_comb:
                        pend_comb.pop(0)()
                    pend_comb.append(mk_mcomb())
            while pend_comb:
                pend_comb.pop(0)()

            if CUT < 6:
                continue
            if s + 1 < G.nsuper:
                prenorm_A(G, (lambda tt, t1_=tok0 + TS: xsrc_d[t1_ + tt * NT: t1_ + (tt + 1) * NT, :]), (lambda tt, g1_=(s + 1) * NTT: xsrc_t(g1_ + tt)), NTT)
            for c in range(8):
                pss = tm_chunk(G, ws, wv_out, c * 512, 512, src=yT)
                for tt in range(NTT):
                    P.add("act", I("activation", out=junkb.a[:NT, 0:512], in_=pss[tt].a[:NT, :], func=AF.Square, accum_out=ssq.a[:NT, c:c + 1]), [pss[tt]], [junkb, ssq]) if tt == 0 else \
                        P.add("act", I("activation", out=junkb.a[:NT, 512:1024], in_=pss[tt].a[:NT, :], func=AF.Square, accum_out=nrm.a[:NT, c:c + 1]), [pss[tt]], [junkb, nrm])
                    P.add("dve", I("tensor_copy", outb[tt].a[:NT, c * 512:(c + 1) * 512], pss[tt].a[:NT, :]), [pss[tt]], [outb[tt]])
            for tt in range(NTT):
                sq_src = ssq if tt == 0 else nrm
                rc = rstd.a[:NT, tt:tt + 1]
                P.add("dve", I("reduce_sum", out=rc, in_=sq_src.a[:NT, 0:8], axis=AX.X), [sq_src], [rstd])
                P.add("dve", I("tensor_scalar", out=rc, in0=rc, scalar1=1.0 / D, scalar2=1e-6, op0=ALU.mult, op1=ALU.add), [rstd], [rstd])
                P.add("act", I("activation", out=rc, in_=rc, func=AF.Sqrt), [rstd], [rstd])
                P.add("dve", I("reciprocal", out=rc, in_=rc), [rstd], [rstd])
            for cb in range(4):
                csl = slice(cb * 1024, (cb + 1) * 1024)
                P.dma("sp", wpostb.a[:, :], wpost_d[l, :, csl], writes=[wpostb])
                for tt in range(NTT):
                    r0 = tok0 + tt * NT
                    xb = xblk[tt % 2]
                    rs = rstd.a[:NT, tt:tt + 1]
                    P.dma("sp", xb.a[:NT, :], xsrc_d[r0:r0 + NT, csl], reads=xsrc_t(gt[tt]), writes=[xb])
                    P.add("dve", I("scalar_tensor_tensor", out=t1.a[:NT, :], in0=outb[tt].a[:NT, cb * 1024:cb * 1024 + 512], scalar=rs, in1=wpostb.a[:NT, 0:512], op0=ALU.mult, op1=ALU.mult), [outb[tt], rstd, wpostb], [t1])
                    P.add("dve", I("scalar_tensor_tensor", out=t3.a[:NT, :], in0=outb[tt].a[:NT, cb * 1024 + 512:cb * 1024 + 1024], scalar=rs, in1=wpostb.a[:NT, 512:1024], op0=ALU.mult, op1=ALU.mult), [outb[tt], rstd, wpostb], [t3])
                    P.add("dve", I("tensor_tensor", out=xb.a[:NT, 0:512], in0=xb.a[:NT, 0:512], in1=t1.a[:NT, :], op=ALU.add), [xb, t1], [xb])
                    P.add("dve", I("tensor_tensor", out=xb.a[:NT, 512:1024], in0=xb.a[:NT, 512:1024], in1=t3.a[:NT, :], op=ALU.add), [xb, t3], [xb])
                    P.dma("sp", xdst_d[r0:r0 + NT, csl], xb.a[:NT, :], reads=[xb], writes=xdst_t(gt[tt]))

    Gp = Group("p", 128, 2, NSUP)
    Gs = Group("s", 16, 2, 1)
    Gm = Group("m", 128, 2, 1)
    def program():
        for l in LAYERS:
            P.dma("sp", prm.a[:], prm_d[l], writes=[prm])
            P.add("act", I("activation", out=abc.a[:], in_=pc("alog"), func=AF.Exp), [prm], [abc])
            P.add("dve", I("tensor_scalar_mul", out=abc.a[:], in0=abc.a[:], scalar1=-1.0), [abc], [abc])
            for i, (a, b) in enumerate((("lq1", "lk1"), ("lq2", "lk2"))):
                P.add("dve", I("tensor_tensor", out=lamtmp.a[:], in0=pc(a), in1=pc(b), op=ALU.mult), [prm], [lamtmp])
                P.add("dve", I("reduce_sum", out=lams.a[:, i:i + 1], in_=lamtmp.a[:], axis=AX.X), [lamtmp], [lams])
            P.add("act", I("activation", out=lams.a[:, 2:4], in_=lams.a[:, 0:2], func=AF.Exp), [lams], [lams])
            P.add("dve", I("tensor_tensor", out=neglam.a[:], in0=lams.a[:, 3:4], in1=lams.a[:, 2:3], op=ALU.subtract), [lams], [neglam])
            P.add("dve", I("tensor_scalar_add", out=neglam.a[:], in0=neglam.a[:], scalar1=-LAM_INIT[l]), [neglam], [neglam])
            prenorm_A(Gm, lambda tt: mem_d[tt * 128:(tt + 1) * 128, :], lambda tt: [], 2)
            prenorm_B(Gm, pc("memw"), 2)
            wv_kv = wkv_d[l].rearrange("(kt p) c -> p kt c", p=128)
            WVN[id(wv_kv)] = f"kv{l}"
            for c in range(4):
                pss = tm_chunk(Gm, WS, wv_kv, c * 512)
                for tt in range(2):
                    k32 = kf[nxt("kf", 2)]
                    P.add("act", I("copy", out=k32.a[:, :], in_=pss[tt].a[:, :]), [pss[tt]], [k32])
                    dst = pmk_d if c < 2 else pmv_d
                    P.dma("sp", dst[l, tt * 128:(tt + 1) * 128, (c % 2) * 512:(c % 2 + 1) * 512], k32.a[:, :], reads=[k32])
                    if c < 2:
                        P.add("dve", I("tensor_copy", kbf.a[:, :], pss[tt].a[:, :]), [pss[tt]], [kbf])
                        tb = trb[nxt("trb", 2)]
                        P.add("pe", seq([I("transpose", out=tb.a[:, j * 128:(j + 1) * 128], in_=kbf.a[:, j * 128:(j + 1) * 128], identity=identb.a[:]) for j in range(4)]), [kbf, identb], [tb])
                        P.add("act", I("copy", out=MKT.a[:, 4 * c:4 * c + 4, tt * 128:(tt + 1) * 128], in_=tb.a[:, 0:512].rearrange("p (j t) -> p j t", t=128)), [tb], [MKT])
                    else:
                        cc = c - 2
                        P.add("dve", I("tensor_copy", MVA[tt].a[:, 2 * cc:2 * cc + 2, 0:256], pss[tt].a[:, :].rearrange("p (h e) -> p h e", e=256)), [pss[tt]], [MVA[tt]])
            last = (l == LAYERS[-1])
            if DO_PROMPT:
                run_layer(l, Gp, xp_d if l == 0 else x1p_d, (lambda i: []) if l == 0 else (lambda i: [x1p_t[i]]),
                          yp_d if last else x1p_d, (lambda i: []) if last else (lambda i: [x1p_t[i]]), WS)
            if DO_SAMPLE:
                run_layer(l, Gs, xs_d if l == 0 else x1s_d, (lambda i: []) if l == 0 else (lambda i: [x1s_t[i]]),
                          ys_d if last else x1s_d, (lambda i: []) if last else (lambda i: [x1s_t[i]]), WS)

    P.maxops = cfg.get('maxops', 10 ** 9)
    WS = WStream()
    P.dry = True
    rr0 = dict(rr)
    program()
    P.dry = False
    rr.update(rr0)
    program()

    print('nops', len(P.ops), 'sbuf_free', nc.sbuf_bytes_remaining, flush=True)
    if cfg.get('dump'):
        for op in P.ops:
            print(op.idx, op.eng, 'dma' if op.dma else '', [d.idx for d in op.deps])
    P.emit(nc)
    return nc


def _fm(v, ntile):
    return np.ascontiguousarray(v.reshape(ntile, 128).T)


def kernel(x_prompt, x_sample, mem_prompt, cache_conv, state_ssm, cache_k, cache_v, cache_mem_k, cache_mem_v,
           norm_pre_w, norm_post_w, w_in, conv_w, conv_b, dt_bias, a_log, d_skip, ssm_norm_w,
           lambda_q1, lambda_k1, lambda_q2, lambda_k2, subln_w, mem_norm_w, w_mem_kv, w_out):
    f = np.float32
    A = lambda a: np.ascontiguousarray(np.asarray(a, dtype=f))
    x_prompt, x_sample, mem_prompt = A(x_prompt), A(x_sample), A(mem_prompt)
    cache_conv, state_ssm, cache_k, cache_v = A(cache_conv), A(state_ssm), A(cache_k), A(cache_v)
    cache_mem_k, cache_mem_v = A(cache_mem_k), A(cache_mem_v)
    w_in, w_out, w_mem_kv = A(w_in), A(w_out), A(w_mem_kv)
    prm = np.zeros((2, 128, NPRM), f)
    bc = lambda v: np.broadcast_to(np.asarray(v, f)[None, :], (128, len(v)))
    for l in range(2):
        def put(n, arr):
            a, b = PO[n]
            prm[l, :, a:b] = arr
        put("wpre", _fm(np.asarray(norm_pre_w[l], f), 32))
        put("memw", _fm(np.asarray(mem_norm_w[l], f), 32))
        cw = np.asarray(conv_w[l], f)
        put("convw", cw.reshape(4, 24, 128).transpose(2, 1, 0).reshape(128, 96))
        put("convb", _fm(np.asarray(conv_b[l], f), 24))
        put("dtb", bc(dt_bias[l]))
        put("alog", bc(a_log[l]))
        put("dsk", bc(d_skip[l]))
        put("ssmw", _fm(np.asarray(ssm_norm_w[l], f), 16))
        put("lq1", bc(lambda_q1[l])); put("lk1", bc(lambda_k1[l]))
        put("lq2", bc(lambda_q2[l])); put("lk2", bc(lambda_k2[l]))
        put("subln", bc(subln_w[l]))
    wpost = np.ascontiguousarray(np.broadcast_to(np.asarray(norm_post_w, f)[:, None, :], (2, 128, D)))
    cst = np.zeros((128, 4, 128), f)
    ii = np.arange(128)
    cst[:, 0, :] = np.eye(128)
    cst[:, 1, :] = (ii[:, None] <= ii[None, :])
    cst[:, 2, :] = (ii[:, None] > ii[None, :])
    cst[:, 3, :] = 1.0
    in_maps = []
    for c in range(NCORE):
        b = c % 4
        ss = slice(2 * c, 2 * c + 2)
        cc = cache_conv[:, ss]
        cconv = np.ascontiguousarray(cc.reshape(2, 2, 3, 24, 128).transpose(0, 4, 3, 1, 2))
        sssm = np.ascontiguousarray(state_ssm[:, ss].reshape(2, 2, 2048, 128).transpose(0, 1, 3, 2))
        ckT = np.ascontiguousarray(cache_k[:, ss].reshape(2, 2, 1024, 8, 128).transpose(0, 1, 3, 4, 2))
        cv = np.ascontiguousarray(cache_v[:, ss])
        cmkT = np.ascontiguousarray(cache_mem_k[:, ss].reshape(2, 2, 256, 8, 128).transpose(0, 1, 3, 4, 2))
        cmv = np.ascontiguousarray(cache_mem_v[:, ss])
        in_maps.append({
            "xp": x_prompt[b], "xs": np.ascontiguousarray(x_sample[ss].reshape(32, D)), "mem": mem_prompt[b],
            "cconv": cconv, "sssm": sssm, "ckT": ckT, "cv": cv, "cmkT": cmkT, "cmv": cmv,
            "w_in": w_in, "w_out": w_out, "w_kv": w_mem_kv, "prm": prm, "wpost": wpost, "cst": cst,
        })
    nc = build()
    res = run_bass_kernel_spmd(nc, in_maps, core_ids=list(range(NCORE)))
    R = res.results
    y_prompt = np.stack([R[b]["y_p"] for b in range(4)])
    y_sample = np.concatenate([R[c]["y_s"].reshape(2, 16, D) for c in range(NCORE)])
    p_conv = np.stack([R[b]["o_conv_p"].transpose(0, 3, 2, 1).reshape(2, 3, 3072) for b in range(4)], axis=1)
    p_ssm = np.stack([R[b]["o_ssm_p"].transpose(0, 2, 1).reshape(2, 32, 64, 128) for b in range(4)], axis=1)
    p_k = np.stack([R[b]["p_k"].reshape(2, SEQ, 4, 2, 128) for b in range(4)], axis=1)
    p_v = np.stack([R[b]["p_v"].reshape(2, SEQ, 4, 256) for b in range(4)], axis=1)
    p_mk = np.stack([R[b]["p_mk"].reshape(2, 256, 4, 256) for b in range(4)], axis=1)
    p_mv = np.stack([R[b]["p_mv"].reshape(2, 256, 4, 256) for b in range(4)], axis=1)
    s_conv = np.concatenate([R[c]["o_conv_s"].transpose(0, 3, 4, 2, 1).reshape(2, 2, 3, 3072) for c in range(NCORE)], axis=1)
    s_ssm = np.concatenate([R[c]["o_ssm_s"].transpose(0, 1, 3, 2).reshape(2, 2, 32, 64, 128) for c in range(NCORE)], axis=1)
    s_k = np.concatenate([R[c]["s_k"].reshape(2, 2, 16, 4, 2, 128) for c in range(NCORE)], axis=1)
    s_v = np.concatenate([R[c]["s_v"].reshape(2, 2, 16, 4, 256) for c in range(NCORE)], axis=1)
    outs = (y_prompt, y_sample, p_conv, p_ssm, p_k, p_v, p_mk, p_mv, s_conv, s_ssm, s_k, s_v)
    return tuple(np.ascontiguousarray(o, dtype=np.float32) for o in outs)
```

```python
import math
from contextlib import ExitStack
import numpy as np
import concourse.bass as bass
import concourse.mybir as mybir
from concourse.bass_utils import run_bass_kernel_spmd

F32 = mybir.dt.float32
BF16 = mybir.dt.bfloat16
AF = mybir.ActivationFunctionType
ALU = mybir.AluOpType
AX = mybir.AxisListType

D = 4096
SEQ = 4096
DIN = 11296
NCORE = 8
COMPUTE = ("pe", "act", "dve", "pool")
NSLOT = {"sp": 16, "pool": 8}
LAM_INIT = [0.8 - 0.6 * math.exp(-0.3 * l) for l in range(2)]
C_Z, C_X, C_B, C_C, C_DT, C_Q, C_K, C_V, C_G, C_QM, C_GM = 0, 2048, 4096, 4608, 5120, 5152, 6176, 7200, 8224, 9248, 10272
PO = {}
_o = 0
for _n, _w in (("wpre", 32), ("memw", 32), ("convw", 96), ("convb", 24), ("dtb", 32), ("alog", 32), ("dsk", 32),
               ("ssmw", 16), ("lq1", 128), ("lk1", 128), ("lq2", 128), ("lk2", 128), ("subln", 256)):
    PO[_n] = (_o, _o + _w)
    _o += _w
NPRM = _o


class TT:
    __slots__ = ("name", "last_w", "readers", "excl")

    def __init__(self, name):
        self.name = name
        self.last_w = None
        self.readers = {}
        self.excl = False


class Tile:
    __slots__ = ("a", "t")

    def __init__(self, a, name):
        self.a = a
        self.t = TT(name)


class Op:
    __slots__ = ("idx", "eng", "fn", "dma", "deps", "needs_inc", "cnt", "slot", "slot_val")

    def __init__(self, idx, eng, fn, dma):
        self.idx = idx
        self.eng = eng
        self.fn = fn
        self.dma = dma
        self.deps = []
        self.needs_inc = False
        self.cnt = 0
        self.slot = None
        self.slot_val = 0


def I(method, *args, **kw):
    return lambda e: getattr(e, method)(*args, **kw)


def seq(thunks):
    def fn(e):
        r = None
        for t in thunks:
            r = t(e)
        return r
    return fn


class Prog:
    def __init__(self):
        self.ops = []
        self.ndma = {"sp": 0, "pool": 0}
        self.dry = False
        self.maxops = 10 ** 9

    def add(self, eng, fn, reads=(), writes=(), dma=False):
        if self.dry or len(self.ops) >= self.maxops:
            return None
        op = Op(len(self.ops), eng, fn, dma)
        deps = {}
        for t in reads:
            t = t.t if hasattr(t, 't') else t
            if t.last_w is not None:
                deps[t.last_w.idx] = t.last_w
            if t.excl:
                for k, r in t.readers.items():
                    if k != eng:
                        deps[r.idx] = r
        for t in writes:
            t = t.t if hasattr(t, 't') else t
            if t.last_w is not None:
                deps[t.last_w.idx] = t.last_w
            for r in t.readers.values():
                deps[r.idx] = r
        op.deps = [d for d in deps.values() if d.dma or d.eng != eng or eng != "pe"]
        for d in op.deps:
            d.needs_inc = True
        for t in reads:
            t = t.t if hasattr(t, 't') else t
            t.readers[("dma", op.idx) if dma else eng] = op
        for t in writes:
            t = t.t if hasattr(t, 't') else t
            t.last_w = op
            t.readers = {}
        if dma:
            i = self.ndma[eng]
            self.ndma[eng] += 1
            op.slot = i % NSLOT[eng]
            op.slot_val = 16 * (i // NSLOT[eng] + 1)
        self.ops.append(op)
        return op

    def dma(self, q, out_ap, in_ap, reads=(), writes=(), **kw):
        return self.add(q, I("dma_start", out=out_ap, in_=in_ap, **kw), reads, writes, dma=True)

    def emit(self, nc):
        with ExitStack() as es:
            sems = {e: es.enter_context(nc.semaphore("s_" + e)) for e in COMPUTE}
            dsem = {q: [es.enter_context(nc.semaphore(f"d_{q}{i}")) for i in range(NSLOT[q])] for q in NSLOT}
            cnt = {e: 0 for e in COMPUTE}
            for op in self.ops:
                if not op.dma:
                    if op.needs_inc:
                        cnt[op.eng] += 1
                    op.cnt = cnt[op.eng]
            final_slot = {q: [0] * NSLOT[q] for q in NSLOT}
            for op in self.ops:
                if op.dma:
                    final_slot[op.eng][op.slot] = op.slot_val
            block = es.enter_context(nc.Block())
            by_eng = {e: [op for op in self.ops if op.eng == e] for e in ("pe", "act", "dve", "pool", "sp")}

            def make(ename):
                ops = by_eng[ename]

                def body(e):
                    waited = {}
                    for op in ops:
                        for d in op.deps:
                            if d.dma:
                                key = (d.eng, d.slot)
                                if waited.get(key, 0) < d.slot_val:
                                    e.wait_ge(dsem[d.eng][d.slot], d.slot_val)
                                    waited[key] = d.slot_val
                            else:
                                if waited.get(d.eng, 0) < d.cnt:
                                    e.wait_ge(sems[d.eng], d.cnt)
                                    waited[d.eng] = d.cnt
                        if op.dma:
                            if op.slot_val > 16:
                                key = (op.eng, op.slot)
                                if waited.get(key, 0) < op.slot_val - 16:
                                    e.wait_ge(dsem[op.eng][op.slot], op.slot_val - 16)
                                    waited[key] = op.slot_val - 16
                            op.fn(e).then_inc(dsem[op.eng][op.slot], 16)
                        else:
                            ins = op.fn(e)
                            if op.needs_inc:
                                ins.then_inc(sems[op.eng], 1)
                    if ename == "sp":
                        for q in NSLOT:
                            for s in range(NSLOT[q]):
                                if final_slot[q][s] > 0:
                                    e.wait_ge(dsem[q][s], final_slot[q][s])
                return body

            block.tensor(make("pe"))
            block.scalar(make("act"))
            block.vector(make("dve"))
            block.gpsimd(make("pool"))
            block.sync(make("sp"))


class Group:
    def __init__(self, name, NT, NTT, nsuper):
        self.name, self.NT, self.NTT, self.nsuper = name, NT, NTT, nsuper
        self.TS = NT * NTT
        self.sample = name == "s"


def build(cfg=None):
    cfg = cfg or {}
    STG = cfg.get('stages', ('ssd', 'attn', 'mem', 'out'))
    NSUP = cfg.get('nsuper', 16)
    LAYERS = cfg.get('layers', (0, 1))
    DO_SAMPLE = cfg.get('sample', True)
    DO_PROMPT = cfg.get('prompt', True)
    DO_MEMKV = cfg.get('memkv', True)
    CUT = cfg.get('cut', 99)
    nc = bass.Bass("TRN2", target_bir_lowering=False)
    P = Prog()
    es = ExitStack()

    def din(name, shape, dt=F32):
        return nc.dram_tensor(name, shape, dt, kind="ExternalInput").ap()

    def dout(name, shape, dt=F32):
        return nc.dram_tensor(name, shape, dt, kind="ExternalOutput").ap()

    def dscr(name, shape, dt=F32):
        return nc.dram_tensor(name, shape, dt, kind="Internal").ap()

    def sb(name, shape, dt=F32):
        return Tile(es.enter_context(nc.sbuf_tensor("sb_" + name, shape, dt)), name)

    def psb(name, shape, dt=F32):
        t = Tile(es.enter_context(nc.psum_tensor("ps_" + name, shape, dt)), name)
        t.t.excl = True
        return t

    xp_d = din("xp", [SEQ, D])
    xs_d = din("xs", [32, D])
    mem_d = din("mem", [256, D])
    cconv_d = din("cconv", [2, 128, 24, 2, 3])
    sssm_d = din("sssm", [2, 2, 128, 2048])
    ckT_d = din("ckT", [2, 2, 8, 128, 1024])
    cv_d = din("cv", [2, 2, 1024, 4, 256])
    cmkT_d = din("cmkT", [2, 2, 8, 128, 256])
    cmv_d = din("cmv", [2, 2, 256, 4, 256])
    win_d = din("w_in", [2, D, DIN])
    wout_d = din("w_out", [2, D, D])
    wkv_d = din("w_kv", [2, D, 2048])
    prm_d = din("prm", [2, 128, NPRM])
    wpost_d = din("wpost", [2, 128, D])
    cst_d = din("cst", [128, 4, 128])

    yp_d = dout("y_p", [SEQ, D])
    ys_d = dout("y_s", [32, D])
    oconvp_d = dout("o_conv_p", [2, 128, 24, 3])
    ossmp_d = dout("o_ssm_p", [2, 128, 2048])
    pk_d = dout("p_k", [2, SEQ, 1024])
    pv_d = dout("p_v", [2, SEQ, 1024])
    pmk_d = dout("p_mk", [2, 256, 1024])
    pmv_d = dout("p_mv", [2, 256, 1024])
    oconvs_d = dout("o_conv_s", [2, 128, 24, 2, 3])
    ossms_d = dout("o_ssm_s", [2, 2, 128, 2048])
    sk_d = dout("s_k", [2, 32, 1024])
    sv_d = dout("s_v", [2, 32, 1024])

    x1p_d = dscr("x1p", [SEQ, D])
    x1s_d = dscr("x1s", [32, D])
    kts_d = dscr("kts", [2, 8, 128, SEQ], BF16)
    vs_d = dscr("vs", [2, SEQ, 4, 256], BF16)
    x1p_t = [TT(f"x1p{i}") for i in range(32)]
    x1s_t = [TT(f"x1s{i}") for i in range(2)]
    kts_t = [[TT(f"kts{l}_{i}") for i in range(32)] for l in range(2)]
    vs_t = [[TT(f"vs{l}_{i}") for i in range(32)] for l in range(2)]

    cst = sb("cst", [128, 4, 128])
    identb = sb("identb", [128, 128], BF16)
    onesb = sb("onesb", [128, 128], BF16)
    prm = sb("prm", [128, NPRM])
    abc = sb("abc", [128, 32])
    neglam = sb("neglam", [128, 1])
    lamtmp = sb("lamtmp", [128, 128])
    lams = sb("lams", [128, 4])
    wpostb = sb("wpostb", [128, 1024])
    xblk = [sb(f"xblk{i}", [128, 1024]) for i in range(2)]
    hsf = [sb(f"hsf{i}", [128, D], BF16) for i in range(2)]
    junkb = sb("junkb", [128, 1024], BF16)
    ssq = sb("ssq", [128, 8])
    rstd = sb("rstd", [128, 2])
    hT = sb("hT", [128, 32, 256], BF16)
    yT = sb("yT", [128, 32, 256], BF16)

    class View:
        def __init__(self, a, t):
            self.a, self.t = a, t
    outb = [View(hT.a[:, 16 * i:16 * (i + 1), :].rearrange("p a b -> p (a b)"), hT.t) for i in range(2)]
    wbuf = [sb(f"wbuf{i}", [128, 32, 256], BF16) for i in range(2)]
    carry = sb("carry", [128, 24, 2, 3])
    xpb = sb("xpb", [128, 4, 262])
    cacc = sb("cacc", [128, 4, 256])
    xpb_t = [TT(f"xpb{i}") for i in range(4)]
    cacc_t = [TT(f"cacc{i}") for i in range(4)]
    BT = sb("BT", [128, 4, 256], BF16)
    CT = sb("CT", [128, 4, 256], BF16)
    Btm = [sb(f"Btm{i}", [128, 512], BF16) for i in range(2)]
    xsT = sb("xsT", [128, 4, 256], BF16)
    dtt = [sb(f"dtt{i}", [128, 32]) for i in range(2)]
    dat = [sb(f"dat{i}", [128, 32]) for i in range(2)]
    ecum = [sb(f"ecum{i}", [128, 32]) for i in range(2)]
    wst = [sb(f"wst{i}", [128, 32]) for i in range(2)]
    cdec = [sb(f"cdec{i}", [128, 32]) for i in range(2)]
    sptmp = sb("sptmp", [128, 32])
    xtm = sb("xtm", [128, 512], BF16)
    xdt = sb("xdt", [128, 512], BF16)
    xw = sb("xw", [128, 512], BF16)
    xD = sb("xD", [128, 512])
    Rt = sb("Rt", [128, 8, 128])
    Et = sb("Et", [128, 8, 128], BF16)
    MT = sb("MT", [128, 8, 128], BF16)
    cbm = sb("cbm", [128, 128], BF16)
    t1 = sb("t1", [128, 512])
    t3 = sb("t3", [128, 512])
    ytm = sb("ytm", [128, 512], BF16)
    Hs = [sb(f"H{g}", [128, 512]) for g in range(4)]
    Hb = [sb(f"Hb{g}", [128, 512], BF16) for g in range(4)]
    nrm = sb("nrm", [128, 256])
    sqt = sb("sqt", [128, 256], BF16)
    QT = sb("QT", [128, 8, 256], BF16)
    QmT = sb("QmT", [128, 8, 256], BF16)
    kf = [sb(f"kf{i}", [128, 512]) for i in range(2)]
    kbf = sb("kbf", [128, 512], BF16)
    KTn = sb("KTn", [128, 8, 256], BF16)
    VAn = [sb(f"VAn{i}", [128, 4, 257], BF16) for i in range(2)]
    KTb = [[sb(f"KTb{i}{m}", [128, 1024], BF16) for m in range(2)] for i in range(2)]
    VA = [sb(f"VA{i}", [128, 8, 257], BF16) for i in range(2)]
    PT = [sb(f"PT{i}", [128, 128], BF16) for i in range(4)]
    PT4 = [sb(f"PT4{i}", [128, 512], BF16) for i in range(2)]
    Osb = [sb(f"Osb{i}", [128, 257]) for i in range(2)]
    rcp = sb("rcp", [128, 4])
    od = sb("od", [128, 256])
    od2 = sb("od2", [128, 256])
    ydtm = sb("ydtm", [128, 256], BF16)
    MKT = sb("MKT", [128, 8, 256], BF16)
    MVA = [sb(f"MVA{j}", [128, 4, 257], BF16) for j in range(2)]
    acc = [psb(f"acc{i}", [128, 512]) for i in range(4)]
    tmp = [psb(f"tmp{i}", [128, 512]) for i in range(2)]
    trb = [psb(f"trb{i}", [128, 1024], BF16) for i in range(2)]
    rr = {"tmp": 0, "trb": 0, "acc": 0, "pt": 0, "kf": 0, "p4": 0}

    def nxt(kind, n):
        rr[kind] = (rr[kind] + 1) % n
        return rr[kind]

    ident = cst.a[:, 0, :]
    MLE = cst.a[:, 1, :]
    MGT = cst.a[:, 2, :]
    ones = cst.a[:, 3, :]

    def pc(name):
        a, b = PO[name]
        return prm.a[:, a:b]

    P.dma("sp", cst.a[:], cst_d, writes=[cst])
    P.add("dve", I("tensor_copy", identb.a[:], ident), [cst], [identb])
    P.add("dve", I("tensor_copy", onesb.a[:], ones), [cst], [onesb])
    for t in VAn + VA + MVA:
        P.add("pool", I("memset", t.a[:, :, 256:257], 1.0), [], [t])

    NWSCR = 160
    wscr_parts = [dscr(f"wscr{i}", [40, 128, 32, 256], BF16) for i in range(4)]
    WVN = {}

    def wscr_ap(idx):
        return wscr_parts[idx // 40][idx % 40]
    wscr_t = [TT(f"wscr{i}") for i in range(NWSCR)]

    class WStream:
        def __init__(self):
            self.descs = []
            self.issued = 0
            self.used = 0
            self.slot = {}

        def _issue(self, i):
            key, src, ncols = self.descs[i]
            b = wbuf[i % 2]
            if key not in self.slot:
                idx = len(self.slot)
                assert idx < NWSCR
                self.slot[key] = idx
                P.dma("pool", b.a[:, :, 0:ncols], src, writes=[b])
                P.dma("sp", wscr_ap(idx)[:, :, 0:ncols], b.a[:, :, 0:ncols], reads=[b], writes=[wscr_t[idx]])
            else:
                idx = self.slot[key]
                P.dma("pool", b.a[:, :, 0:ncols], wscr_ap(idx)[:, :, 0:ncols], reads=[wscr_t[idx]], writes=[b])

        def next(self, key, src, ncols):
            if P.dry:
                self.descs.append((key, src, ncols))
                return wbuf[0]
            i = self.used
            while self.issued < min(i + 2, len(self.descs)):
                self._issue(self.issued)
                self.issued += 1
            self.used += 1
            return wbuf[i % 2]

    def prenorm_A(G, rows_ap, rows_t, ntiles):
        NT = G.NT
        for tt in range(ntiles):
            src = rows_ap(tt)
            rt = rows_t(tt)
            for cb in range(4):
                xb = xblk[cb % 2]
                P.dma("sp", xb.a[:NT, :], src[:, cb * 1024:(cb + 1) * 1024], reads=rt, writes=[xb])
                P.add("act", I("activation", out=junkb.a[:NT, :], in_=xb.a[:NT, :], func=AF.Square, accum_out=ssq.a[:NT, cb:cb + 1]),
                      [xb], [junkb, ssq])
            P.add("dve", I("reduce_sum", out=rstd.a[:NT, 0:1], in_=ssq.a[:NT, 0:4], axis=AX.X), [ssq], [rstd])
            P.add("dve", I("tensor_scalar", out=rstd.a[:NT, 0:1], in0=rstd.a[:NT, 0:1], scalar1=1.0 / D, scalar2=1e-6, op0=ALU.mult, op1=ALU.add), [rstd], [rstd])
            P.add("act", I("activation", out=rstd.a[:NT, 0:1], in_=rstd.a[:NT, 0:1], func=AF.Sqrt), [rstd], [rstd])
            P.add("dve", I("reciprocal", out=rstd.a[:NT, 0:1], in_=rstd.a[:NT, 0:1]), [rstd], [rstd])
            for cb in range(4):
                xb = xblk[cb % 2]
                P.dma("sp", xb.a[:NT, :], src[:, cb * 1024:(cb + 1) * 1024], reads=rt, writes=[xb])
                P.add("dve", I("tensor_scalar_mul", out=hsf[tt].a[:NT, cb * 1024:(cb + 1) * 1024], in0=xb.a[:NT, :], scalar1=rstd.a[:NT, 0:1]), [xb, rstd], [hsf[tt]])

    def prenorm_B(G, wcol, ntiles):
        NT = G.NT
        for tt in range(ntiles):
            for cb in range(4):
                tb = trb[nxt("trb", 2)]
                P.add("pe", seq([I("transpose", out=tb.a[:, j * 128:j * 128 + NT], in_=hsf[tt].a[:NT, cb * 1024 + j * 128:cb * 1024 + (j + 1) * 128], identity=identb.a[:NT, :NT]) for j in range(8)]),
                      [hsf[tt], identb], [tb])
                o = hT.a[:, cb * 8:(cb + 1) * 8, tt * NT:(tt + 1) * NT]
                i0 = tb.a[:, :].rearrange("p (j t) -> p j t", t=128)[:, :, 0:NT]
                i1 = wcol[:, cb * 8:(cb + 1) * 8].unsqueeze(2).to_broadcast([128, 8, NT])
                P.add("dve", I("tensor_tensor", out=o, in0=i0, in1=i1, op=ALU.mult), [tb, prm], [hT])

    def fm_chunk(G, ws, wv, c0, ncols=512):
        TS = G.TS
        base = 2 * nxt("acc", 2)
        pss = [(acc[base + ct // 2], (ct % 2) * 256) for ct in range(4)]
        for half in range((ncols + 255) // 256):
            nc_ = min(256, ncols - half * 256)
            wb = ws.next((WVN[id(wv)], c0 + half * 256), wv[:, :, c0 + half * 256:c0 + half * 256 + nc_], nc_)
            th = []
            for c2 in range(nc_ // 128):
                pt, off = pss[half * 2 + c2]
                for kt in range(32):
                    th.append(I("matmul", pt.a[:, off:off + TS], wb.a[:, kt, c2 * 128:(c2 + 1) * 128], hT.a[:, kt, 0:TS], start=(kt == 0), stop=(kt == 31)))
            P.add("pe", seq(th), [wb, hT], [pss[half * 2][0]])
        return pss

    def tm_chunk(G, ws, wv, c0, ncols=512, src=None):
        NT = G.NT
        sT = hT if src is None else src
        base = 2 * nxt("acc", 2)
        pss = [acc[base], acc[base + 1]]
        for half in range((ncols + 255) // 256):
            nc_ = min(256, ncols - half * 256)
            wb = ws.next((WVN[id(wv)], c0 + half * 256), wv[:, :, c0 + half * 256:c0 + half * 256 + nc_], nc_)
            th = []
            for tt in range(G.NTT):
                for kt in range(32):
                    th.append(I("matmul", pss[tt].a[:NT, half * 256:half * 256 + nc_], sT.a[:, kt, tt * NT:(tt + 1) * NT], wb.a[:, kt, 0:nc_], start=(kt == 0), stop=(kt == 31)))
            P.add("pe", seq(th), [wb, sT], pss[:G.NTT])
        return pss

    def run_layer(l, G, xsrc_d, xsrc_t, xdst_d, xdst_t, ws):
        NT, NTT, TS = G.NT, G.NTT, G.TS
        wv_in = win_d[l].rearrange("(kt p) c -> p kt c", p=128)
        wv_out = wout_d[l].rearrange("(kt p) c -> p kt c", p=128)
        WVN[id(wv_in)] = f"in{l}"
        WVN[id(wv_out)] = f"out{l}"
        if not G.sample:
            P.add("pool", I("memset", carry.a[:], 0.0), [], [carry])
            for g in range(4):
                P.add("pool", I("memset", Hs[g].a[:], 0.0), [], [Hs[g]])
                P.add("pool", I("memset", Hb[g].a[:], 0.0), [], [Hb[g]])
        else:
            P.dma("sp", carry.a[:], cconv_d[l], writes=[carry])

        def conv_chunk(pss, ct0, dst):
            nseg = 2 if G.sample else 1
            sl = TS // nseg
            xv = xpb.a[:, :, 0:nseg * (3 + sl)].rearrange("p c (s t) -> p c s t", s=nseg)
            P.add("dve", I("tensor_copy", xv[:, :, :, 0:3], carry.a[:, ct0:ct0 + 4, 0:nseg, :]), [carry], xpb_t)
            for ct in range(4):
                pt, off = pss[ct]
                P.add("act", I("copy", out=xv[:, ct, :, 3:3 + sl], in_=pt.a[:, off:off + TS].rearrange("p (s t) -> p s t", s=nseg)), [pt], [xpb_t[ct]])
            P.add("dve", I("tensor_copy", carry.a[:, ct0:ct0 + 4, 0:nseg, :], xv[:, :, :, sl:sl + 3]), xpb_t, [carry])
            cv = cacc.a[:, :, 0:TS].rearrange("p c (s t) -> p c s t", s=nseg)
            cw = pc("convw").rearrange("p (c j) -> p c j", j=4)
            for ct in range(4):
                gct = ct0 + ct
                P.add("dve", I("tensor_scalar", out=cv[:, ct], in0=xv[:, ct, :, 0:sl], scalar1=cw[:, gct, 0:1], scalar2=pc("convb")[:, gct:gct + 1], op0=ALU.mult, op1=ALU.add), [xpb_t[ct], prm], [cacc_t[ct]])
            for j in range(1, 4):
                for ct in range(4):
                    gct = ct0 + ct
                    P.add("dve", I("scalar_tensor_tensor", out=cv[:, ct], in0=xv[:, ct, :, j:j + sl], scalar=cw[:, gct, j:j + 1], in1=cv[:, ct], op0=ALU.mult, op1=ALU.add), [xpb_t[ct], prm, cacc_t[ct]], [cacc_t[ct]])
            P.add("act", I("activation", out=dst.a[:, :, 0:TS], in_=cacc.a[:, :, 0:TS], func=AF.Silu), cacc_t, [dst])

        for s in range(G.nsuper):
            tok0 = s * TS
            gt = [s * NTT + tt for tt in range(NTT)]
            if s == 0:
                prenorm_A(G, lambda tt: xsrc_d[tok0 + tt * NT: tok0 + (tt + 1) * NT, :], lambda tt: xsrc_t(gt[tt]), NTT)
            prenorm_B(G, pc("wpre"), NTT)
            if CUT < 1:
                continue
            pss = fm_chunk(G, ws, wv_in, C_B)
            conv_chunk(pss, 16, BT)
            pss = fm_chunk(G, ws, wv_in, C_C)
            conv_chunk(pss, 20, CT)
            pss = tm_chunk(G, ws, wv_in, C_DT, 32)
            for tt in range(NTT):
                P.add("dve", I("tensor_tensor", out=sptmp.a[:NT, :], in0=pss[tt].a[:NT, 0:32], in1=pc("dtb")[:NT, :], op=ALU.add), [pss[tt], prm], [sptmp])
                P.add("act", I("activation", out=sptmp.a[:NT, :], in_=sptmp.a[:NT, :], func=AF.Exp), [sptmp], [sptmp])
                P.add("act", I("activation", out=dtt[tt].a[:NT, :], in_=sptmp.a[:NT, :], func=AF.Ln, bias=1.0), [sptmp], [dtt[tt]])
                P.add("dve", I("tensor_tensor", out=dat[tt].a[:NT, :], in0=dtt[tt].a[:NT, :], in1=abc.a[:NT, :], op=ALU.mult), [dtt[tt], abc], [dat[tt]])
                pt = tmp[nxt("tmp", 2)]
                P.add("pe", seq([
                    I("matmul", pt.a[:NT, 0:32], MLE[:NT, :NT], dat[tt].a[:NT, :], start=True, stop=True),
                    I("matmul", pt.a[:NT, 32:64], MGT[:NT, :NT], dat[tt].a[:NT, :], start=True, stop=True),
                    I("matmul", pt.a[:, 64:96], ones[:NT, :], dat[tt].a[:NT, :], start=True, stop=True)]), [cst, dat[tt]], [pt])
                P.add("act", I("activation", out=ecum[tt].a[:NT, :], in_=pt.a[:NT, 0:32], func=AF.Exp), [pt], [ecum[tt]])
                P.add("act", I("activation", out=wst[tt].a[:NT, :], in_=pt.a[:NT, 32:64], func=AF.Exp), [pt], [wst[tt]])
                P.add("act", I("activation", out=cdec[tt].a[:, :], in_=pt.a[:, 64:96], func=AF.Exp), [pt], [cdec[tt]])
                P.add("dve", I("tensor_tensor", out=wst[tt].a[:NT, :], in0=wst[tt].a[:NT, :], in1=dtt[tt].a[:NT, :], op=ALU.mult), [wst[tt], dtt[tt]], [wst[tt]])
                tb = trb[nxt("trb", 2)]
                P.add("pe", seq([I("transpose", out=tb.a[:NT, g * 128:(g + 1) * 128], in_=BT.a[:, g, tt * NT:(tt + 1) * NT], identity=identb.a[:]) for g in range(4)]), [BT, identb], [tb])
                P.add("act", I("copy", out=Btm[tt].a[:NT, :], in_=tb.a[:NT, 0:512]), [tb], [Btm[tt]])
            if CUT < 2:
                continue
            for g in range(4):
                pss = fm_chunk(G, ws, wv_in, C_X + g * 512)
                conv_chunk(pss, 4 * g, xsT)
                pss = fm_chunk(G, ws, wv_in, C_Z + g * 512)
                for ct in range(4):
                    pt, off = pss[ct]
                    P.add("act", I("activation", out=yT.a[:, 4 * g + ct, 0:TS], in_=pt.a[:, off:off + TS], func=AF.Silu), [pt], [yT])
                for tt in range(NTT):
                    tsl = slice(tt * NT, (tt + 1) * NT)
                    if G.sample:
                        P.dma("sp", Hs[g].a[:], sssm_d[l, tt, :, g * 512:(g + 1) * 512], writes=[Hs[g]])
                        P.add("act", I("copy", out=Hb[g].a[:], in_=Hs[g].a[:]), [Hs[g]], [Hb[g]])
                    hs8 = slice(8 * g, 8 * g + 8)
                    P.add("dve", I("tensor_tensor", out=Rt.a[:NT, :, :NT], in0=dat[tt].a[:NT, hs8].unsqueeze(2).to_broadcast([NT, 8, NT]), in1=MLE[:NT, :NT].unsqueeze(1).to_broadcast([NT, 8, NT]), op=ALU.mult), [dat[tt], cst], [Rt])
                    tb = trb[nxt("trb", 2)]
                    P.add("pe", seq([I("transpose", out=tb.a[:NT, c * 128:(c + 1) * 128], in_=xsT.a[:, c, tsl], identity=identb.a[:]) for c in range(4)]), [xsT, identb], [tb])
                    P.add("act", I("copy", out=xtm.a[:NT, :], in_=tb.a[:NT, 0:512]), [tb], [xtm])
                    x3 = xtm.a[:NT, :].rearrange("p (h d) -> p h d", d=64)
                    P.add("dve", I("tensor_tensor", out=xdt.a[:NT, :].rearrange("p (h d) -> p h d", d=64), in0=x3, in1=dtt[tt].a[:NT, hs8].unsqueeze(2).to_broadcast([NT, 8, 64]), op=ALU.mult), [xtm, dtt[tt]], [xdt])
                    P.add("dve", I("tensor_tensor", out=xw.a[:NT, :].rearrange("p (h d) -> p h d", d=64), in0=x3, in1=wst[tt].a[:NT, hs8].unsqueeze(2).to_broadcast([NT, 8, 64]), op=ALU.mult), [xtm, wst[tt]], [xw])
                    P.add("dve", I("tensor_tensor", out=xD.a[:NT, :].rearrange("p (h d) -> p h d", d=64), in0=x3, in1=pc("dsk")[:NT, hs8].unsqueeze(2).to_broadcast([NT, 8, 64]), op=ALU.mult), [xtm, prm], [xD])
                    sg = [tmp[0], tmp[1]]
                    P.add("pe", seq([I("matmul", sg[hh // 4].a[:NT, (hh % 4) * 128:(hh % 4) * 128 + NT], MGT[:NT, :NT], Rt.a[:NT, hh, :NT], start=True, stop=True) for hh in range(8)]), [cst, Rt], sg)
                    for q in range(2):
                        P.add("act", I("activation", out=Et.a[:NT, 4 * q:4 * q + 4, :NT], in_=sg[q].a[:NT, :].rearrange("p (h t) -> p h t", t=128)[:, :, :NT], func=AF.Exp), [sg[q]], [Et])
                    pcb = acc[2 * rr["acc"] + 0]
                    pa = [acc[(2 * rr["acc"] + 2) % 4], acc[(2 * rr["acc"] + 3) % 4]]
                    P.add("pe", I("matmul", pcb.a[:NT, 0:NT], BT.a[:, g, tsl], CT.a[:, g, tsl], start=True, stop=True), [BT, CT], [pcb])
                    P.add("dve", I("tensor_tensor", out=cbm.a[:NT, :NT], in0=pcb.a[:NT, 0:NT], in1=MLE[:NT, :NT], op=ALU.mult), [pcb, cst], [cbm])
                    P.add("dve", I("tensor_tensor", out=MT.a[:NT, :, :NT], in0=Et.a[:NT, :, :NT], in1=cbm.a[:NT, :NT].unsqueeze(1).to_broadcast([NT, 8, NT]), op=ALU.mult), [Et, cbm], [MT])
                    P.add("pe", seq([I("matmul", pa[0].a[:NT, hh * 64:(hh + 1) * 64], MT.a[:NT, hh, :NT], xdt.a[:NT, hh * 64:(hh + 1) * 64], start=True, stop=True) for hh in range(8)]), [MT, xdt], [pa[0]])
                    P.add("pe", I("matmul", pa[1].a[:NT, :], CT.a[:, g, tsl], Hb[g].a[:], start=True, stop=True), [CT, Hb[g]], [pa[1]])
                    P.add("dve", I("tensor_tensor", out=t1.a[:NT, :].rearrange("p (h d) -> p h d", d=64), in0=pa[1].a[:NT, :].rearrange("p (h d) -> p h d", d=64), in1=ecum[tt].a[:NT, hs8].unsqueeze(2).to_broadcast([NT, 8, 64]), op=ALU.mult), [pa[1], ecum[tt]], [t1])
                    P.add("dve", I("tensor_tensor", out=t3.a[:NT, :], in0=t1.a[:NT, :], in1=xD.a[:NT, :], op=ALU.add), [t1, xD], [t3])
                    P.add("dve", I("tensor_tensor", out=ytm.a[:NT, :], in0=t3.a[:NT, :], in1=pa[0].a[:NT, :], op=ALU.add), [t3, pa[0]], [ytm])
                    pS = tmp[0]
                    P.add("pe", I("matmul", pS.a[:, :], Btm[tt].a[:NT, g * 128:(g + 1) * 128], xw.a[:NT, :], start=True, stop=True), [Btm[tt], xw], [pS])
                    P.add("dve", I("tensor_tensor", out=Hs[g].a[:].rearrange("p (h d) -> p h d", d=64), in0=Hs[g].a[:].rearrange("p (h d) -> p h d", d=64), in1=cdec[tt].a[:, hs8].unsqueeze(2).to_broadcast([128, 8, 64]), op=ALU.mult), [Hs[g], cdec[tt]], [Hs[g]])
                    P.add("dve", I("tensor_tensor", out=Hs[g].a[:], in0=Hs[g].a[:], in1=pS.a[:, :], op=ALU.add), [Hs[g], pS], [Hs[g]])
                    P.add("act", I("copy", out=Hb[g].a[:], in_=Hs[g].a[:]), [Hs[g]], [Hb[g]])
                    if G.sample:
                        P.dma("sp", ossms_d[l, tt, :, g * 512:(g + 1) * 512], Hs[g].a[:], reads=[Hs[g]])
                    elif s == G.nsuper - 1 and tt == NTT - 1:
                        P.dma("sp", ossmp_d[l, :, g * 512:(g + 1) * 512], Hs[g].a[:], reads=[Hs[g]])
                    tb = trb[nxt("trb", 2)]
                    P.add("pe", seq([I("transpose", out=tb.a[:, c * 128:c * 128 + NT], in_=ytm.a[:NT, c * 128:(c + 1) * 128], identity=identb.a[:NT, :NT]) for c in range(4)]), [ytm, identb], [tb])
                    P.add("dve", I("tensor_tensor", out=yT.a[:, 4 * g:4 * g + 4, tsl], in0=yT.a[:, 4 * g:4 * g + 4, tsl], in1=tb.a[:, 0:512].rearrange("p (c t) -> p c t", t=128)[:, :, :NT], op=ALU.mult), [yT, tb], [yT])
                pn = tmp[1]
                for c in range(4):
                    P.add("dve", I("tensor_tensor", out=sqt.a[:, 0:TS], in0=yT.a[:, 4 * g + c, 0:TS], in1=yT.a[:, 4 * g + c, 0:TS], op=ALU.mult), [yT], [sqt])
                    P.add("pe", I("matmul", pn.a[:, 0:TS], onesb.a[:], sqt.a[:, 0:TS], start=(c == 0), stop=(c == 3)), [onesb, sqt], [pn])
                P.add("dve", I("tensor_scalar", out=nrm.a[:, 0:TS], in0=pn.a[:, 0:TS], scalar1=1.0 / 512, scalar2=1e-6, op0=ALU.mult, op1=ALU.add), [pn], [nrm])
                P.add("act", I("activation", out=nrm.a[:, 0:TS], in_=nrm.a[:, 0:TS], func=AF.Sqrt), [nrm], [nrm])
                P.add("dve", I("reciprocal", out=nrm.a[:, 0:TS], in_=nrm.a[:, 0:TS]), [nrm], [nrm])
                for c in range(4):
                    P.add("dve", I("scalar_tensor_tensor", out=yT.a[:, 4 * g + c, 0:TS], in0=yT.a[:, 4 * g + c, 0:TS], scalar=pc("ssmw")[:, 4 * g + c:4 * g + c + 1], in1=nrm.a[:, 0:TS], op0=ALU.mult, op1=ALU.mult), [yT, prm, nrm], [yT])
            if CUT < 3:
                continue
            if G.sample:
                P.dma("sp", oconvs_d[l], carry.a[:], reads=[carry])
            elif s == G.nsuper - 1:
                P.dma("sp", oconvp_d[l], carry.a[:, :, 0, :], reads=[carry])

            for c in range(2):
                pss = fm_chunk(G, ws, wv_in, C_Q + c * 512)
                for ct in range(4):
                    pt, off = pss[ct]
                    P.add("act", I("copy", out=QT.a[:, 4 * c + ct, 0:TS], in_=pt.a[:, off:off + TS]), [pt], [QT])
            kdst = sk_d if G.sample else pk_d
            vdst = sv_d if G.sample else pv_d
            for c in range(2):
                pss = tm_chunk(G, ws, wv_in, C_K + c * 512)
                for tt in range(NTT):
                    k32 = kf[nxt("kf", 2)]
                    P.add("act", I("copy", out=k32.a[:NT, :], in_=pss[tt].a[:NT, :]), [pss[tt]], [k32])
                    P.dma("sp", kdst[l, tok0 + tt * NT: tok0 + (tt + 1) * NT, c * 512:(c + 1) * 512], k32.a[:NT, :], reads=[k32])
                    P.add("dve", I("tensor_copy", kbf.a[:NT, :], pss[tt].a[:NT, :]), [pss[tt]], [kbf])
                    tb = trb[nxt("trb", 2)]
                    P.add("pe", seq([I("transpose", out=tb.a[:, j * 128:j * 128 + NT], in_=kbf.a[:NT, j * 128:(j + 1) * 128], identity=identb.a[:NT, :NT]) for j in range(4)]), [kbf, identb], [tb])
                    P.add("act", I("copy", out=KTn.a[:, 4 * c:4 * c + 4, tt * NT:(tt + 1) * NT], in_=tb.a[:, 0:512].rearrange("p (j t) -> p j t", t=128)[:, :, :NT]), [tb], [KTn])
            if not G.sample:
                for tt in range(NTT):
                    P.dma("sp", kts_d[l, :, :, tok0 + tt * NT: tok0 + (tt + 1) * NT].rearrange("j p t -> p j t"), KTn.a[:, :, tt * NT:(tt + 1) * NT], reads=[KTn], writes=[kts_t[l][gt[tt]]])
            for c in range(2):
                pss = tm_chunk(G, ws, wv_in, C_V + c * 512)
                for tt in range(NTT):
                    k32 = kf[nxt("kf", 2)]
                    P.add("act", I("copy", out=k32.a[:NT, :], in_=pss[tt].a[:NT, :]), [pss[tt]], [k32])
                    P.dma("sp", vdst[l, tok0 + tt * NT: tok0 + (tt + 1) * NT, c * 512:(c + 1) * 512], k32.a[:NT, :], reads=[k32])
                    P.add("dve", I("tensor_copy", VAn[tt].a[:NT, 2 * c:2 * c + 2, 0:256], pss[tt].a[:NT, :].rearrange("p (h e) -> p h e", e=256)), [pss[tt]], [VAn[tt]])
            if not G.sample:
                for tt in range(NTT):
                    P.dma("sp", vs_d[l, tok0 + tt * NT: tok0 + (tt + 1) * NT, :, :], VAn[tt].a[:NT, :, 0:256], reads=[VAn[tt]], writes=[vs_t[l][gt[tt]]])
            for c in range(2):
                pss = fm_chunk(G, ws, wv_in, C_G + c * 512)
                for ct in range(4):
                    pt, off = pss[ct]
                    P.add("act", I("activation", out=yT.a[:, 16 + 4 * c + ct, 0:TS], in_=pt.a[:, off:off + TS], func=AF.Silu), [pt], [yT])

            if CUT < 4:
                continue
            sc_d = 128 ** -0.5
            pend_comb = []
            for tt in range(NTT):
                qsl = slice(tt * NT, (tt + 1) * NT)
                for h in range(4):
                    if G.sample:
                        pieces = [(0, 8)]
                    else:
                        nh = gt[tt]
                        pieces = [(a, min(8, nh - a)) for a in range(0, nh, 8)]
                    O = [acc[0], acc[1]] if (rr["acc"] == 0) else [acc[2], acc[3]]
                    nxt("acc", 2)
                    kcount = 0
                    pend = []
                    for (k0, nkt) in pieces:
                        bi = nxt("pt", 2)
                        kb = KTb[bi]
                        va = VA[bi]
                        if G.sample:
                            for m in range(2):
                                P.dma("pool", kb[m].a[:, :], ckT_d[l, tt, 2 * h + m], writes=[kb[m]])
                            P.dma("pool", va.a[:, :, 0:256], cv_d[l, tt, :, h, :].rearrange("(kt p) e -> p kt e", p=128), writes=[va])
                        else:
                            rd_k = [kts_t[l][k0 + i] for i in range(nkt)]
                            rd_v = [vs_t[l][k0 + i] for i in range(nkt)]
                            for m in range(2):
                                P.dma("sp", kb[m].a[:, 0:nkt * 128], kts_d[l, 2 * h + m, :, k0 * 128:(k0 + nkt) * 128], reads=rd_k, writes=[kb[m]])
                            P.dma("sp", va.a[:, 0:nkt, 0:256], vs_d[l, k0 * 128:(k0 + nkt) * 128, h, :].rearrange("(kt p) e -> p kt e", p=128), reads=rd_v, writes=[va])
                        for kt0 in range(0, nkt, 2):
                            nb = min(2, nkt - kt0)
                            st = tmp[nxt("tmp", 2)]
                            p4 = PT4[nxt("p4", 2)]
                            th = []
                            for j in range(nb):
                                for m in range(2):
                                    b_ = j * 2 + m
                                    th.append(I("matmul", st.a[:, b_ * 128:b_ * 128 + NT], kb[m].a[:, (kt0 + j) * 128:(kt0 + j + 1) * 128], QT.a[:, 2 * h + m, qsl], start=True, stop=True))
                            P.add("pe", seq(th), [kb[0], kb[1], QT], [st])
                            vin = st.a[:, 0:nb * 256].rearrange("p (b t) -> p b t", t=128)[:, :, :NT]
                            vout = p4.a[:, 0:nb * 256].rearrange("p (b t) -> p b t", t=128)[:, :, :NT]
                            P.add("act", I("activation", out=vout, in_=vin, func=AF.Exp, scale=sc_d), [st], [p4])
                            while pend:
                                pend.pop(0)()

                            def mk(p4=p4, va=va, kt0=kt0, nb=nb, first=(kcount == 0), O=O):
                                def f():
                                    for m in range(2):
                                        th2 = [I("matmul", O[m].a[:NT, 0:257], p4.a[:, (j * 2 + m) * 128:(j * 2 + m) * 128 + NT], va.a[:, kt0 + j, :], start=(first and j == 0), stop=False) for j in range(nb)]
                                        P.add("pe", seq(th2), [p4, va], [O[m]])
                                return f
                            pend.append(mk())
                            kcount += nb
                    while pend:
                        pend.pop(0)()
                    for m in range(2):
                        st = tmp[nxt("tmp", 2)]
                        P.add("pe", I("matmul", st.a[:NT, 0:NT], KTn.a[:, 2 * h + m, qsl], QT.a[:, 2 * h + m, qsl], start=True, stop=True), [KTn, QT], [st])
                        pt_ = PT[2 * m + (kcount % 2)]
                        P.add("act", I("activation", out=pt_.a[:NT, 0:NT], in_=st.a[:NT, 0:NT], func=AF.Exp, scale=sc_d), [st], [pt_])
                        if not G.sample:
                            P.add("dve", I("memset", pt_.a[64:128, 0:64], 0.0), [], [pt_])
                        P.add("pe", I("matmul", O[m].a[:NT, 0:257], pt_.a[:NT, 0:NT], VAn[tt].a[:NT, h, :], start=(kcount == 0), stop=True), [pt_, VAn[tt]], [O[m]])
                    def mk_comb(O=O, h=h, qsl=qsl, tt=tt):
                        def f():
                            for m in range(2):
                                P.add("act", I("copy", out=Osb[m].a[:NT, :], in_=O[m].a[:NT, 0:257]), [O[m]], [Osb[m]])
                                P.add("dve", I("reciprocal", out=rcp.a[:NT, m:m + 1], in_=Osb[m].a[:NT, 256:257]), [Osb[m]], [rcp])
                            P.add("dve", I("tensor_tensor", out=rcp.a[:NT, 1:2], in0=rcp.a[:NT, 1:2], in1=neglam.a[:NT, 0:1], op=ALU.mult), [rcp, neglam], [rcp])
                            P.add("dve", I("tensor_scalar_mul", out=od.a[:NT, :], in0=Osb[0].a[:NT, 0:256], scalar1=rcp.a[:NT, 0:1]), [Osb[0], rcp], [od])
                            P.add("dve", I("scalar_tensor_tensor", out=od.a[:NT, :], in0=Osb[1].a[:NT, 0:256], scalar=rcp.a[:NT, 1:2], in1=od.a[:NT, :], op0=ALU.mult, op1=ALU.add), [Osb[1], rcp, od], [od])
                            P.add("act", I("activation", out=od2.a[:NT, :], in_=od.a[:NT, :], func=AF.Square, accum_out=rcp.a[:NT, 2:3]), [od], [od2, rcp])
                            P.add("dve", I("tensor_scalar", out=rcp.a[:NT, 2:3], in0=rcp.a[:NT, 2:3], scalar1=1.0 / 256, scalar2=1e-5, op0=ALU.mult, op1=ALU.add), [rcp], [rcp])
                            P.add("act", I("activation", out=rcp.a[:NT, 2:3], in_=rcp.a[:NT, 2:3], func=AF.Sqrt), [rcp], [rcp])
                            P.add("dve", I("reciprocal", out=rcp.a[:NT, 3:4], in_=rcp.a[:NT, 2:3]), [rcp], [rcp])
                            P.add("dve", I("tensor_scalar", out=od2.a[:NT, :], in0=od.a[:NT, :], scalar1=rcp.a[:NT, 3:4], scalar2=1.0 - LAM_INIT[l], op0=ALU.mult, op1=ALU.mult), [od, rcp], [od2])
                            P.add("dve", I("tensor_tensor", out=ydtm.a[:NT, :], in0=od2.a[:NT, :], in1=pc("subln")[:NT, :], op=ALU.mult), [od2, prm], [ydtm])
                            tb = trb[nxt("trb", 2)]
                            P.add("pe", seq([I("transpose", out=tb.a[:, c * 128:c * 128 + NT], in_=ydtm.a[:NT, c * 128:(c + 1) * 128], identity=identb.a[:NT, :NT]) for c in range(2)]), [ydtm, identb], [tb])
                            P.add("dve", I("tensor_tensor", out=yT.a[:, 16 + 2 * h:16 + 2 * h + 2, qsl], in0=yT.a[:, 16 + 2 * h:16 + 2 * h + 2, qsl], in1=tb.a[:, 0:256].rearrange("p (c t) -> p c t", t=128)[:, :, :NT], op=ALU.mult), [yT, tb], [yT])
                        return f
                    while pend_comb:
                        pend_comb.pop(0)()
                    pend_comb.append(mk_comb())
            while pend_comb:
                pend_comb.pop(0)()

            if CUT < 5:
                continue
            for c in range(2):
                pss = fm_chunk(G, ws, wv_in, C_QM + c * 512)
                for ct in range(4):
                    pt, off = pss[ct]
                    P.add("act", I("copy", out=QmT.a[:, 4 * c + ct, 0:TS], in_=pt.a[:, off:off + TS]), [pt], [QmT])
            for c in range(2):
                pss = fm_chunk(G, ws, wv_in, C_GM + c * 512)
                for ct in range(4):
                    pt, off = pss[ct]
                    P.add("act", I("activation", out=yT.a[:, 24 + 4 * c + ct, 0:TS], in_=pt.a[:, off:off + TS], func=AF.Silu), [pt], [yT])
            sc_m = 256 ** -0.5
            for tt in range(NTT):
                qsl = slice(tt * NT, (tt + 1) * NT)
                if G.sample:
                    P.dma("pool", MKT.a[:], cmkT_d[l, tt].rearrange("j p t -> p j t"), writes=[MKT])
                    for j in range(2):
                        P.dma("pool", MVA[j].a[:, :, 0:256], cmv_d[l, tt, j * 128:(j + 1) * 128, :, :], writes=[MVA[j]])
                for h in range(4):
                    Om = acc[2 * nxt("acc", 2)]
                    st = tmp[nxt("tmp", 2)]
                    p4 = PT4[nxt("p4", 2)]
                    P.add("pe", seq([I("matmul", st.a[:, j * 128:j * 128 + NT], MKT.a[:, 2 * h + dt_, j * 128:(j + 1) * 128], QmT.a[:, 2 * h + dt_, qsl], start=(dt_ == 0), stop=(dt_ == 1)) for j in range(2) for dt_ in range(2)]), [MKT, QmT], [st])
                    P.add("act", I("activation", out=p4.a[:, 0:256].rearrange("p (b t) -> p b t", t=128)[:, :, :NT], in_=st.a[:, 0:256].rearrange("p (b t) -> p b t", t=128)[:, :, :NT], func=AF.Exp, scale=sc_m), [st], [p4])
                    P.add("pe", seq([I("matmul", Om.a[:NT, 0:257], p4.a[:, j * 128:j * 128 + NT], MVA[j].a[:, h, :], start=(j == 0), stop=(j == 1)) for j in range(2)]), [p4, MVA[0], MVA[1]], [Om])

                    def mk_mcomb(Om=Om, h=h, qsl=qsl):
                        def f():
                            P.add("act", I("copy", out=Osb[0].a[:NT, :], in_=Om.a[:NT, 0:257]), [Om], [Osb[0]])
                            P.add("dve", I("reciprocal", out=rcp.a[:NT, 0:1], in_=Osb[0].a[:NT, 256:257]), [Osb[0]], [rcp])
                            P.add("dve", I("tensor_scalar_mul", out=ydtm.a[:NT, :], in0=Osb[0].a[:NT, 0:256], scalar1=rcp.a[:NT, 0:1]), [Osb[0], rcp], [ydtm])
                            tb = trb[nxt("trb", 2)]
                            P.add("pe", seq([I("transpose", out=tb.a[:, c * 128:c * 128 + NT], in_=ydtm.a[:NT, c * 128:(c + 1) * 128], identity=identb.a[:NT, :NT]) for c in range(2)]), [ydtm, identb], [tb])
                            P.add("dve", I("tensor_tensor", out=yT.a[:, 24 + 2 * h:24 + 2 * h + 2, qsl], in0=yT.a[:, 24 + 2 * h:24 + 2 * h + 2, qsl], in1=tb.a[:, 0:256].rearrange("p (c t) -> p c t", t=128)[:, :, :NT], op=ALU.mult), [yT, tb], [yT])
                        return f
                    while pend_comb:
                        pend_comb.pop(0)()
                    pend_comb.append(mk_mcomb())
            while pend_comb:
                pend_comb.pop(0)()

            if CUT < 6:
                continue
            if s + 1 < G.nsuper:
                prenorm_A(G, (lambda tt, t1_=tok0 + TS: xsrc_d[t1_ + tt * NT: t1_ + (tt + 1) * NT, :]), (lambda tt, g1_=(s + 1) * NTT: xsrc_t(g1_ + tt)), NTT)
            for c in range(8):
                pss = tm_chunk(G, ws, wv_out, c * 512, 512, src=yT)
                for tt in range(NTT):
                    P.add("act", I("activation", out=junkb.a[:NT, 0:512], in_=pss[tt].a[:NT, :], func=AF.Square, accum_out=ssq.a[:NT, c:c + 1]), [pss[tt]], [junkb, ssq]) if tt == 0 else \
                        P.add("act", I("activation", out=junkb.a[:NT, 512:1024], in_=pss[tt].a[:NT, :], func=AF.Square, accum_out=nrm.a[:NT, c:c + 1]), [pss[tt]], [junkb, nrm])
                    P.add("dve", I("tensor_copy", outb[tt].a[:NT, c * 512:(c + 1) * 512], pss[tt].a[:NT, :]), [pss[tt]], [outb[tt]])
            for tt in range(NTT):
                sq_src = ssq if tt == 0 else nrm
                rc = rstd.a[:NT, tt:tt + 1]
                P.add("dve", I("reduce_sum", out=rc, in_=sq_src.a[:NT, 0:8], axis=AX.X), [sq_src], [rstd])
                P.add("dve", I("tensor_scalar", out=rc, in0=rc, scalar1=1.0 / D, scalar2=1e-6, op0=ALU.mult, op1=ALU.add), [rstd], [rstd])
                P.add("act", I("activation", out=rc, in_=rc, func=AF.Sqrt), [rstd], [rstd])
                P.add("dve", I("reciprocal", out=rc, in_=rc), [rstd], [rstd])
            for cb in range(4):
                csl = slice(cb * 1024, (cb + 1) * 1024)
                P.dma("sp", wpostb.a[:, :], wpost_d[l, :, csl], writes=[wpostb])
                for tt in range(NTT):
                    r0 = tok0 + tt * NT
                    xb = xblk[tt % 2]
                    rs = rstd.a[:NT, tt:tt + 1]
                    P.dma("sp", xb.a[:NT, :], xsrc_d[r0:r0 + NT, csl], reads=xsrc_t(gt[tt]), writes=[xb])
                    P.add("dve", I("scalar_tensor_tensor", out=t1.a[:NT, :], in0=outb[tt].a[:NT, cb * 1024:cb * 1024 + 512], scalar=rs, in1=wpostb.a[:NT, 0:512], op0=ALU.mult, op1=ALU.mult), [outb[tt], rstd, wpostb], [t1])
                    P.add("dve", I("scalar_tensor_tensor", out=t3.a[:NT, :], in0=outb[tt].a[:NT, cb * 1024 + 512:cb * 1024 + 1024], scalar=rs, in1=wpostb.a[:NT, 512:1024], op0=ALU.mult, op1=ALU.mult), [outb[tt], rstd, wpostb], [t3])
                    P.add("dve", I("tensor_tensor", out=xb.a[:NT, 0:512], in0=xb.a[:NT, 0:512], in1=t1.a[:NT, :], op=ALU.add), [xb, t1], [xb])
                    P.add("dve", I("tensor_tensor", out=xb.a[:NT, 512:1024], in0=xb.a[:NT, 512:1024], in1=t3.a[:NT, :], op=ALU.add), [xb, t3], [xb])
                    P.dma("sp", xdst_d[r0:r0 + NT, csl], xb.a[:NT, :], reads=[xb], writes=xdst_t(gt[tt]))

    Gp = Group("p", 128, 2, NSUP)
    Gs = Group("s", 16, 2, 1)
    Gm = Group("m", 128, 2, 1)
    def program():
        for l in LAYERS:
            P.dma("sp", prm.a[:], prm_d[l], writes=[prm])
            P.add("act", I("activation", out=abc.a[:], in_=pc("alog"), func=AF.Exp), [prm], [abc])
            P.add("dve", I("tensor_scalar_mul", out=abc.a[:], in0=abc.a[:], scalar1=-1.0), [abc], [abc])
            for i, (a, b) in enumerate((("lq1", "lk1"), ("lq2", "lk2"))):
                P.add("dve", I("tensor_tensor", out=lamtmp.a[:], in0=pc(a), in1=pc(b), op=ALU.mult), [prm], [lamtmp])
                P.add("dve", I("reduce_sum", out=lams.a[:, i:i + 1], in_=lamtmp.a[:], axis=AX.X), [lamtmp], [lams])
            P.add("act", I("activation", out=lams.a[:, 2:4], in_=lams.a[:, 0:2], func=AF.Exp), [lams], [lams])
            P.add("dve", I("tensor_tensor", out=neglam.a[:], in0=lams.a[:, 3:4], in1=lams.a[:, 2:3], op=ALU.subtract), [lams], [neglam])
            P.add("dve", I("tensor_scalar_add", out=neglam.a[:], in0=neglam.a[:], scalar1=-LAM_INIT[l]), [neglam], [neglam])
            prenorm_A(Gm, lambda tt: mem_d[tt * 128:(tt + 1) * 128, :], lambda tt: [], 2)
            prenorm_B(Gm, pc("memw"), 2)
            wv_kv = wkv_d[l].rearrange("(kt p) c -> p kt c", p=128)
            WVN[id(wv_kv)] = f"kv{l}"
            for c in range(4):
                pss = tm_chunk(Gm, WS, wv_kv, c * 512)
                for tt in range(2):
                    k32 = kf[nxt("kf", 2)]
                    P.add("act", I("copy", out=k32.a[:, :], in_=pss[tt].a[:, :]), [pss[tt]], [k32])
                    dst = pmk_d if c < 2 else pmv_d
                    P.dma("sp", dst[l, tt * 128:(tt + 1) * 128, (c % 2) * 512:(c % 2 + 1) * 512], k32.a[:, :], reads=[k32])
                    if c < 2:
                        P.add("dve", I("tensor_copy", kbf.a[:, :], pss[tt].a[:, :]), [pss[tt]], [kbf])
                        tb = trb[nxt("trb", 2)]
                        P.add("pe", seq([I("transpose", out=tb.a[:, j * 128:(j + 1) * 128], in_=kbf.a[:, j * 128:(j + 1) * 128], identity=identb.a[:]) for j in range(4)]), [kbf, identb], [tb])
                        P.add("act", I("copy", out=MKT.a[:, 4 * c:4 * c + 4, tt * 128:(tt + 1) * 128], in_=tb.a[:, 0:512].rearrange("p (j t) -> p j t", t=128)), [tb], [MKT])
                    else:
                        cc = c - 2
                        P.add("dve", I("tensor_copy", MVA[tt].a[:, 2 * cc:2 * cc + 2, 0:256], pss[tt].a[:, :].rearrange("p (h e) -> p h e", e=256)), [pss[tt]], [MVA[tt]])
            last = (l == LAYERS[-1])
            if DO_PROMPT:
                run_layer(l, Gp, xp_d if l == 0 else x1p_d, (lambda i: []) if l == 0 else (lambda i: [x1p_t[i]]),
                          yp_d if last else x1p_d, (lambda i: []) if last else (lambda i: [x1p_t[i]]), WS)
            if DO_SAMPLE:
                run_layer(l, Gs, xs_d if l == 0 else x1s_d, (lambda i: []) if l == 0 else (lambda i: [x1s_t[i]]),
                          ys_d if last else x1s_d, (lambda i: []) if last else (lambda i: [x1s_t[i]]), WS)

    P.maxops = cfg.get('maxops', 10 ** 9)
    WS = WStream()
    P.dry = True
    rr0 = dict(rr)
    program()
    P.dry = False
    rr.update(rr0)
    program()

    print('nops', len(P.ops), 'sbuf_free', nc.sbuf_bytes_remaining, flush=True)
    if cfg.get('dump'):
        for op in P.ops:
            print(op.idx, op.eng, 'dma' if op.dma else '', [d.idx for d in op.deps])
    P.emit(nc)
    return nc


def _fm(v, ntile):
    return np.ascontiguousarray(v.reshape(ntile, 128).T)


def kernel(x_prompt, x_sample, mem_prompt, cache_conv, state_ssm, cache_k, cache_v, cache_mem_k, cache_mem_v,
           norm_pre_w, norm_post_w, w_in, conv_w, conv_b, dt_bias, a_log, d_skip, ssm_norm_w,
           lambda_q1, lambda_k1, lambda_q2, lambda_k2, subln_w, mem_norm_w, w_mem_kv, w_out):
    f = np.float32
    A = lambda a: np.ascontiguousarray(np.asarray(a, dtype=f))
    x_prompt, x_sample, mem_prompt = A(x_prompt), A(x_sample), A(mem_prompt)
    cache_conv, state_ssm, cache_k, cache_v = A(cache_conv), A(state_ssm), A(cache_k), A(cache_v)
    cache_mem_k, cache_mem_v = A(cache_mem_k), A(cache_mem_v)
    w_in, w_out, w_mem_kv = A(w_in), A(w_out), A(w_mem_kv)
    prm = np.zeros((2, 128, NPRM), f)
    bc = lambda v: np.broadcast_to(np.asarray(v, f)[None, :], (128, len(v)))
    for l in range(2):
        def put(n, arr):
            a, b = PO[n]
            prm[l, :, a:b] = arr
        put("wpre", _fm(np.asarray(norm_pre_w[l], f), 32))
        put("memw", _fm(np.asarray(mem_norm_w[l], f), 32))
        cw = np.asarray(conv_w[l], f)
        put("convw", cw.reshape(4, 24, 128).transpose(2, 1, 0).reshape(128, 96))
        put("convb", _fm(np.asarray(conv_b[l], f), 24))
        put("dtb", bc(dt_bias[l]))
        put("alog", bc(a_log[l]))
        put("dsk", bc(d_skip[l]))
        put("ssmw", _fm(np.asarray(ssm_norm_w[l], f), 16))
        put("lq1", bc(lambda_q1[l])); put("lk1", bc(lambda_k1[l]))
        put("lq2", bc(lambda_q2[l])); put("lk2", bc(lambda_k2[l]))
        put("subln", bc(subln_w[l]))
    wpost = np.ascontiguousarray(np.broadcast_to(np.asarray(norm_post_w, f)[:, None, :], (2, 128, D)))
    cst = np.zeros((128, 4, 128), f)
    ii = np.arange(128)
    cst[:, 0, :] = np.eye(128)
    cst[:, 1, :] = (ii[:, None] <= ii[None, :])
    cst[:, 2, :] = (ii[:, None] > ii[None, :])
    cst[:, 3, :] = 1.0
    in_maps = []
    for c in range(NCORE):
        b = c % 4
        ss = slice(2 * c, 2 * c + 2)
        cc = cache_conv[:, ss]
        cconv = np.ascontiguousarray(cc.reshape(2, 2, 3, 24, 128).transpose(0, 4, 3, 1, 2))
        sssm = np.ascontiguousarray(state_ssm[:, ss].reshape(2, 2, 2048, 128).transpose(0, 1, 3, 2))
        ckT = np.ascontiguousarray(cache_k[:, ss].reshape(2, 2, 1024, 8, 128).transpose(0, 1, 3, 4, 2))
        cv = np.ascontiguousarray(cache_v[:, ss])
        cmkT = np.ascontiguousarray(cache_mem_k[:, ss].reshape(2, 2, 256, 8, 128).transpose(0, 1, 3, 4, 2))
        cmv = np.ascontiguousarray(cache_mem_v[:, ss])
        in_maps.append({
            "xp": x_prompt[b], "xs": np.ascontiguousarray(x_sample[ss].reshape(32, D)), "mem": mem_prompt[b],
            "cconv": cconv, "sssm": sssm, "ckT": ckT, "cv": cv, "cmkT": cmkT, "cmv": cmv,
            "w_in": w_in, "w_out": w_out, "w_kv": w_mem_kv, "prm": prm, "wpost": wpost, "cst": cst,
        })
    nc = build()
    res = run_bass_kernel_spmd(nc, in_maps, core_ids=list(range(NCORE)))
    R = res.results
    y_prompt = np.stack([R[b]["y_p"] for b in range(4)])
    y_sample = np.concatenate([R[c]["y_s"].reshape(2, 16, D) for c in range(NCORE)])
    p_conv = np.stack([R[b]["o_conv_p"].transpose(0, 3, 2, 1).reshape(2, 3, 3072) for b in range(4)], axis=1)
    p_ssm = np.stack([R[b]["o_ssm_p"].transpose(0, 2, 1).reshape(2, 32, 64, 128) for b in range(4)], axis=1)
    p_k = np.stack([R[b]["p_k"].reshape(2, SEQ, 4, 2, 128) for b in range(4)], axis=1)
    p_v = np.stack([R[b]["p_v"].reshape(2, SEQ, 4, 256) for b in range(4)], axis=1)
    p_mk = np.stack([R[b]["p_mk"].reshape(2, 256, 4, 256) for b in range(4)], axis=1)
    p_mv = np.stack([R[b]["p_mv"].reshape(2, 256, 4, 256) for b in range(4)], axis=1)
    s_conv = np.concatenate([R[c]["o_conv_s"].transpose(0, 3, 4, 2, 1).reshape(2, 2, 3, 3072) for c in range(NCORE)], axis=1)
    s_ssm = np.concatenate([R[c]["o_ssm_s"].transpose(0, 1, 3, 2).reshape(2, 2, 32, 64, 128) for c in range(NCORE)], axis=1)
    s_k = np.concatenate([R[c]["s_k"].reshape(2, 2, 16, 4, 2, 128) for c in range(NCORE)], axis=1)
    s_v = np.concatenate([R[c]["s_v"].reshape(2, 2, 16, 4, 256) for c in range(NCORE)], axis=1)
    outs = (y_prompt, y_sample, p_conv, p_ssm, p_k, p_v, p_mk, p_mv, s_conv, s_ssm, s_k, s_v)
    return tuple(np.ascontiguousarray(o, dtype=np.float32) for o in outs)
```
